# Optimizing a Trainium2 kernel written in Bass

```python
import math
import jax, jax.numpy as jnp
from jax import lax
import numpy as np

D_MODEL = 2048
BATCH = 1
SEQ = 16384
DEPTH = 1

DEEPNORM_ALPHA = (2.0 * DEPTH) ** 0.25
DEEPNORM_BETA = (8.0 * DEPTH) ** -0.25
LN_EPS = 1e-5
RMS_EPS = 1e-6
ROPE_THETA = 10000.0
NEG_INF = -1e30
Q_BLOCK = 128

D_FF = 5632
FFN_RES_WEIGHT = 0.5

MLA_HEADS = 8
MLA_Q_RANK = 512
MLA_KV_RANK = 256
MLA_NOPE_DIM = 128
MLA_ROPE_DIM = 64
MLA_V_DIM = 128

NSA_HEADS = 8
NSA_KV_HEADS = 2
NSA_HEAD_DIM = 128
NSA_GROUP = NSA_HEADS // NSA_KV_HEADS
CMP_BLOCK = 32
CMP_STRIDE = 16
CMP_HIDDEN = 256
SEL_BLOCK = 64
SEL_TOPK = 16
WINDOW = 512
N_BRANCH = 3
FORCE_BONUS = 1e4

MIX_WIDTH = MLA_HEADS * MLA_V_DIM + NSA_HEADS * NSA_HEAD_DIM
NSA_KV_WIDTH = NSA_KV_HEADS * NSA_HEAD_DIM
IN_SIZES = (MLA_Q_RANK, MLA_KV_RANK, MLA_ROPE_DIM, NSA_HEADS * NSA_HEAD_DIM,
            NSA_KV_WIDTH, NSA_KV_WIDTH, NSA_KV_WIDTH, NSA_KV_WIDTH, NSA_KV_WIDTH, NSA_KV_WIDTH,
            N_BRANCH * NSA_HEADS)
IN_COLS = sum(IN_SIZES)

kernel_name = "hybrid_mla_nsa_macaron_deepnorm"


def layer_norm(x, g, b):
    xf = x.astype(jnp.float32)
    mu = jnp.mean(xf, axis=-1, keepdims=True)
    var = jnp.mean(jnp.square(xf - mu), axis=-1, keepdims=True)
    return ((xf - mu) * lax.rsqrt(var + LN_EPS) * g + b).astype(x.dtype)


def rms_norm(x, g):
    xf = x.astype(jnp.float32)
    ms = jnp.mean(jnp.square(xf), axis=-1, keepdims=True)
    return (xf * lax.rsqrt(ms + RMS_EPS) * g).astype(x.dtype)


def rope(x, pos):
    d = x.shape[-1]
    half = d // 2
    inv = ROPE_THETA ** (-jnp.arange(half, dtype=jnp.float32) * (2.0 / d))
    ang = pos.astype(jnp.float32)[:, None] * inv[None, :]
    cos, sin = jnp.cos(ang), jnp.sin(ang)
    x1 = x[..., :half].astype(jnp.float32)
    x2 = x[..., half:].astype(jnp.float32)
    return jnp.concatenate([x1 * cos - x2 * sin, x2 * cos + x1 * sin], axis=-1).astype(x.dtype)


def swiglu(x, w_gate, w_up, w_down):
    return (jax.nn.silu(x @ w_gate) * (x @ w_up)) @ w_down


def split_cols(h, sizes):
    offsets = [int(v) for v in np.cumsum(sizes)[:-1]]
    return jnp.split(h, offsets, axis=-1)


def to_heads(t, n):
    b, s, _ = t.shape
    return t.reshape(b, s, n, -1).transpose(0, 2, 1, 3)


def masked_probs(s, mask):
    s = jnp.where(mask, s, NEG_INF)
    m = jnp.max(s, axis=-1, keepdims=True)
    p = jnp.where(mask, jnp.exp(s - m), 0.0)
    return p / jnp.maximum(jnp.sum(p, axis=-1, keepdims=True), 1e-30)


def mla_attention(c_q, c_kv, k_rope, q_norm_g, w_uq, kv_norm_g, w_ukv, pos):
    b, s, _ = c_q.shape
    q = (rms_norm(c_q, q_norm_g) @ w_uq).reshape(b, s, MLA_HEADS, MLA_NOPE_DIM + MLA_ROPE_DIM)
    q = q.transpose(0, 2, 1, 3)
    q_nope = q[..., :MLA_NOPE_DIM]
    q_rope = rope(q[..., MLA_NOPE_DIM:], pos)
    kv = (rms_norm(c_kv, kv_norm_g) @ w_ukv).reshape(b, s, MLA_HEADS, MLA_NOPE_DIM + MLA_V_DIM)
    kv = kv.transpose(0, 2, 1, 3)
    k_nope, v = kv[..., :MLA_NOPE_DIM], kv[..., MLA_NOPE_DIM:]
    k_r = rope(k_rope, pos)
    scale = (MLA_NOPE_DIM + MLA_ROPE_DIM) ** -0.5
    kpos = jnp.arange(s)

    def block(i):
        q0 = i * Q_BLOCK
        qn = lax.dynamic_slice_in_dim(q_nope, q0, Q_BLOCK, axis=2)
        qr = lax.dynamic_slice_in_dim(q_rope, q0, Q_BLOCK, axis=2)
        sc = (jnp.einsum("bhqd,bhkd->bhqk", qn, k_nope, preferred_element_type=jnp.float32)
              + jnp.einsum("bhqd,bkd->bhqk", qr, k_r, preferred_element_type=jnp.float32)) * scale
        tq = q0 + jnp.arange(Q_BLOCK)
        mask = tq[:, None] >= kpos[None, :]
        p = jax.nn.softmax(jnp.where(mask, sc, NEG_INF), axis=-1)
        return jnp.einsum("bhqk,bhkd->bhqd", p.astype(v.dtype), v)

    o = lax.map(block, jnp.arange(s // Q_BLOCK))
    return o.transpose(1, 0, 3, 2, 4).reshape(b, s, MLA_HEADS * MLA_V_DIM)


def nsa_attention(q, k_cmp, v_cmp, k_sel, v_sel, k_win, v_win, gate_logits, gate_b,
                  pe_k, w1_k, w2_k, pe_v, w1_v, w2_v, pos):
    b, s, _ = q.shape
    hk, g, d = NSA_KV_HEADS, NSA_GROUP, NSA_HEAD_DIM
    qg = rope(to_heads(q, NSA_HEADS), pos).reshape(b, hk, g, s, d)
    k_cmp = rope(to_heads(k_cmp, hk), pos)
    v_cmp = to_heads(v_cmp, hk)
    k_sel = rope(to_heads(k_sel, hk), pos)
    v_sel = to_heads(v_sel, hk)
    k_win = rope(to_heads(k_win, hk), pos)
    v_win = to_heads(v_win, hk)

    n_cmp = (s - CMP_BLOCK) // CMP_STRIDE + 1
    idx = jnp.arange(n_cmp)[:, None] * CMP_STRIDE + jnp.arange(CMP_BLOCK)[None, :]

    def compress(t, pe, w1, w2):
        blk = t[:, :, idx, :] + pe
        flat = blk.reshape(b, hk, n_cmp, CMP_BLOCK * d)
        return jax.nn.gelu(flat @ w1) @ w2

    kc = compress(k_cmp, pe_k, w1_k, w2_k)
    vc = compress(v_cmp, pe_v, w1_v, w2_v)
    cmp_end = jnp.arange(n_cmp) * CMP_STRIDE + CMP_BLOCK - 1

    n_sel = s // SEL_BLOCK
    top_n = min(SEL_TOPK, n_sel)
    ci = jnp.arange(n_cmp)[:, None] * CMP_STRIDE
    sj = jnp.arange(n_sel)[None, :] * SEL_BLOCK
    ov = jnp.clip(jnp.minimum(ci + CMP_BLOCK, sj + SEL_BLOCK) - jnp.maximum(ci, sj), 0, None)
    ov = ov.astype(jnp.float32) / CMP_STRIDE
    ks_blocks = k_sel.reshape(b, hk, n_sel, SEL_BLOCK, d)
    vs_blocks = v_sel.reshape(b, hk, n_sel, SEL_BLOCK, d)
    gather = jax.vmap(jax.vmap(lambda blocks, ix: blocks[ix]))
    bj = jnp.arange(n_sel)

    kw_pad = jnp.pad(k_win, ((0, 0), (0, 0), (WINDOW, 0), (0, 0)))
    vw_pad = jnp.pad(v_win, ((0, 0), (0, 0), (WINDOW, 0), (0, 0)))

    gates = jax.nn.sigmoid((gate_logits + gate_b).astype(jnp.float32))
    gates = gates.reshape(b, s, N_BRANCH, hk, g).transpose(0, 2, 3, 4, 1)
    scale = d ** -0.5

    def block(i):
        q0 = i * Q_BLOCK
        qb = lax.dynamic_slice_in_dim(qg, q0, Q_BLOCK, axis=3)
        tq = q0 + jnp.arange(Q_BLOCK)
        s_c = jnp.einsum("bhgqd,bhkd->bhgqk", qb, kc, preferred_element_type=jnp.float32) * scale
        p_c = masked_probs(s_c, cmp_end[None, :] <= tq[:, None])
        o_c = jnp.einsum("bhgqk,bhkd->bhgqd", p_c.astype(vc.dtype), vc)
        imp = jnp.einsum("bhgqc,cj->bhqj", p_c, ov)
        cur = tq // SEL_BLOCK
        valid = bj[None, :] * SEL_BLOCK <= tq[:, None]
        forced = (bj[None, :] == 0) | (bj[None, :] == cur[:, None]) | (bj[None, :] == cur[:, None] - 1)
        score = jnp.where(valid, imp + FORCE_BONUS * forced.astype(jnp.float32), NEG_INF)
        top_s, top_j = lax.top_k(score, top_n)
        sel_ok = top_s > 0.5 * NEG_INF
        kg = gather(ks_blocks, top_j).reshape(b, hk, Q_BLOCK, top_n * SEL_BLOCK, d)
        vg = gather(vs_blocks, top_j).reshape(b, hk, Q_BLOCK, top_n * SEL_BLOCK, d)
        key_pos = (top_j[..., None] * SEL_BLOCK + jnp.arange(SEL_BLOCK)).reshape(b, hk, Q_BLOCK, top_n * SEL_BLOCK)
        m_s = (key_pos <= tq[:, None]) & jnp.repeat(sel_ok, SEL_BLOCK, axis=-1)
        s_s = jnp.einsum("bhgqd,bhqkd->bhgqk", qb, kg, preferred_element_type=jnp.float32) * scale
        p_s = masked_probs(s_s, m_s[:, :, None])
        o_s = jnp.einsum("bhgqk,bhqkd->bhgqd", p_s.astype(vg.dtype), vg)
        kw = lax.dynamic_slice_in_dim(kw_pad, q0, WINDOW + Q_BLOCK, axis=2)
        vw = lax.dynamic_slice_in_dim(vw_pad, q0, WINDOW + Q_BLOCK, axis=2)
        wpos = q0 - WINDOW + jnp.arange(WINDOW + Q_BLOCK)
        m_w = (wpos[None, :] <= tq[:, None]) & (wpos[None, :] > tq[:, None] - WINDOW) & (wpos[None, :] >= 0)
        s_w = jnp.einsum("bhgqd,bhkd->bhgqk", qb, kw, preferred_element_type=jnp.float32) * scale
        p_w = masked_probs(s_w, m_w)
        o_w = jnp.einsum("bhgqk,bhkd->bhgqd", p_w.astype(vw.dtype), vw)
        gb = lax.dynamic_slice_in_dim(gates, q0, Q_BLOCK, axis=4).astype(o_c.dtype)
        return gb[:, 0, ..., None] * o_c + gb[:, 1, ..., None] * o_s + gb[:, 2, ..., None] * o_w

    o = lax.map(block, jnp.arange(s // Q_BLOCK))
    return o.transpose(1, 0, 4, 2, 3, 5).reshape(b, s, NSA_HEADS * d)


def setup_inputs(seed: int = 0) -> dict:
    key = jax.random.key(seed)
    ks = iter(jax.random.split(key, 40))

    def nrm(shape, scale):
        return jax.random.normal(next(ks), shape, jnp.float32) * scale

    def gain(shape):
        return 1.0 + nrm(shape, 0.02)

    L = DEPTH
    hd = NSA_HEAD_DIM
    return {
        "x": nrm((BATCH, SEQ, D_MODEL), 1.0),
        "ffn1_w_gate": nrm((L, D_MODEL, D_FF), D_MODEL ** -0.5),
        "ffn1_w_up": nrm((L, D_MODEL, D_FF), D_MODEL ** -0.5),
        "ffn1_w_down": nrm((L, D_FF, D_MODEL), D_FF ** -0.5 * DEEPNORM_BETA),
        "ln1_g": gain((L, D_MODEL)),
        "ln1_b": nrm((L, D_MODEL), 0.02),
        "w_in": nrm((L, D_MODEL, IN_COLS), D_MODEL ** -0.5),
        "mla_q_norm_g": gain((L, MLA_Q_RANK)),
        "mla_w_uq": nrm((L, MLA_Q_RANK, MLA_HEADS * (MLA_NOPE_DIM + MLA_ROPE_DIM)), MLA_Q_RANK ** -0.5),
        "mla_kv_norm_g": gain((L, MLA_KV_RANK)),
        "mla_w_ukv": nrm((L, MLA_KV_RANK, MLA_HEADS * (MLA_NOPE_DIM + MLA_V_DIM)), MLA_KV_RANK ** -0.5),
        "nsa_gate_b": nrm((L, N_BRANCH * NSA_HEADS), 0.02),
        "nsa_cmp_pe_k": nrm((L, CMP_BLOCK, hd), 0.1),
        "nsa_cmp_w1_k": nrm((L, CMP_BLOCK * hd, CMP_HIDDEN), (CMP_BLOCK * hd) ** -0.5),
        "nsa_cmp_w2_k": nrm((L, CMP_HIDDEN, hd), CMP_HIDDEN ** -0.5),
        "nsa_cmp_pe_v": nrm((L, CMP_BLOCK, hd), 0.1),
        "nsa_cmp_w1_v": nrm((L, CMP_BLOCK * hd, CMP_HIDDEN), (CMP_BLOCK * hd) ** -0.5),
        "nsa_cmp_w2_v": nrm((L, CMP_HIDDEN, hd), CMP_HIDDEN ** -0.5),
        "w_out": nrm((L, MIX_WIDTH, D_MODEL), MIX_WIDTH ** -0.5 * DEEPNORM_BETA),
        "ln2_g": gain((L, D_MODEL)),
        "ln2_b": nrm((L, D_MODEL), 0.02),
        "ffn2_w_gate": nrm((L, D_MODEL, D_FF), D_MODEL ** -0.5),
        "ffn2_w_up": nrm((L, D_MODEL, D_FF), D_MODEL ** -0.5),
        "ffn2_w_down": nrm((L, D_FF, D_MODEL), D_FF ** -0.5 * DEEPNORM_BETA),
        "ln3_g": gain((L, D_MODEL)),
        "ln3_b": nrm((L, D_MODEL), 0.02),
    }


def reference(x, ffn1_w_gate, ffn1_w_up, ffn1_w_down, ln1_g, ln1_b, w_in,
              mla_q_norm_g, mla_w_uq, mla_kv_norm_g, mla_w_ukv, nsa_gate_b,
              nsa_cmp_pe_k, nsa_cmp_w1_k, nsa_cmp_w2_k, nsa_cmp_pe_v, nsa_cmp_w1_v, nsa_cmp_w2_v,
              w_out, ln2_g, ln2_b, ffn2_w_gate, ffn2_w_up, ffn2_w_down, ln3_g, ln3_b):
    pos = jnp.arange(x.shape[1])
    for l in range(DEPTH):
        x = layer_norm(DEEPNORM_ALPHA * x + FFN_RES_WEIGHT * swiglu(x, ffn1_w_gate[l], ffn1_w_up[l], ffn1_w_down[l]),
                       ln1_g[l], ln1_b[l])
        (c_q, c_kv, k_rope, q_nsa, k_c, v_c, k_s, v_s, k_w, v_w, g_nsa) = split_cols(x @ w_in[l], IN_SIZES)
        o_mla = mla_attention(c_q, c_kv, k_rope, mla_q_norm_g[l], mla_w_uq[l], mla_kv_norm_g[l], mla_w_ukv[l], pos)
        o_nsa = nsa_attention(q_nsa, k_c, v_c, k_s, v_s, k_w, v_w, g_nsa, nsa_gate_b[l],
                              nsa_cmp_pe_k[l], nsa_cmp_w1_k[l], nsa_cmp_w2_k[l],
                              nsa_cmp_pe_v[l], nsa_cmp_w1_v[l], nsa_cmp_w2_v[l], pos)
        mix = jnp.concatenate([o_mla, o_nsa], axis=-1) @ w_out[l]
        x = layer_norm(DEEPNORM_ALPHA * x + mix, ln2_g[l], ln2_b[l])
        x = layer_norm(DEEPNORM_ALPHA * x + FFN_RES_WEIGHT * swiglu(x, ffn2_w_gate[l], ffn2_w_up[l], ffn2_w_down[l]),
                       ln3_g[l], ln3_b[l])
    return x
```

```python
import math
import os
STOP = int(os.environ.get('KSTOP', '99'))
NT1 = int(os.environ.get('KNT1', '32'))
KSLOTS = int(os.environ.get('KSLOTS', '4'))
KDBG = int(os.environ.get('KDBG', '0'))
DBG_LAYOUT = {}
LAST = {}
from contextlib import ExitStack
import numpy as np
import concourse.bass as bass
import concourse.mybir as mybir
from concourse.bass_utils import run_bass_kernel_spmd

F32 = mybir.dt.float32
BF16 = mybir.dt.bfloat16
AF = mybir.ActivationFunctionType
ALU = mybir.AluOpType
AX = mybir.AxisListType

S = 16384
D = 2048
DFF = 5632
NCORE = 8
TT = 512
NT_ALL = S // TT
NSLOT = 4
ALPHA = 2.0 ** 0.25
LN_EPS = 1e-5
RMS_EPS = 1e-6
NKV_F = 2176
NKV_T = 512
NQ = 512 + 2048 + 24
MLA_SCALE = 192.0 ** -0.5
NSA_SCALE = 128.0 ** -0.5


class Res:
    __slots__ = ("w", "r")

    def __init__(self):
        self.w = None
        self.r = {}


class Ctx:
    ENG = ("pe", "act", "dve", "pool", "sp")

    def __init__(self, nc, es):
        self.nc = nc
        self.ops = {e: [] for e in self.ENG}
        self.seq = {e: 0 for e in self.ENG}
        self.known = {e: {} for e in self.ENG}
        self.sems = {}
        for e in ("pe", "act", "dve", "pool"):
            self.sems[e] = es.enter_context(nc.semaphore("S_" + e))
        self.dmak = {"sp": 20, "pool": 6, "act": 6}
        self.dman = {q: 0 for q in self.dmak}
        for q, k in self.dmak.items():
            for i in range(k):
                self.sems[(q, i)] = es.enter_context(nc.semaphore("D_%s_%d" % (q, i)))
        self.last = {}

    def _need(self, eng, tok, waits, war=False):
        if tok is None:
            return
        key, val, teng = tok
        if teng == eng and not isinstance(key, tuple):
            if eng == "pe" or war:
                return
        if self.known[eng].get(key, 0) >= val:
            return
        self.known[eng][key] = val
        waits.append((key, val))

    def op(self, eng, fn, reads=(), writes=(), dma=False):
        waits = []
        for r in reads:
            self._need(eng, r.w, waits)
        for w in writes:
            self._need(eng, w.w, waits)
            for key, (val, teng) in w.r.items():
                self._need(eng, (key, val, teng), waits, war=True)
        if dma:
            k = self.dmak[eng]
            n = self.dman[eng]
            self.dman[eng] = n + 1
            key = (eng, n % k)
            val = 16 * (n // k + 1)
            if n >= k:
                self._need(eng, (key, val - 16, eng), waits)
            tok = (key, val, eng)
            inc = (key, 16)
        else:
            self.seq[eng] += 1
            tok = (eng, self.seq[eng], eng)
            inc = (eng, 1)
        self.ops[eng].append((waits, fn, inc))
        self.last[tok[0]] = tok
        for r in reads:
            r.r[tok[0]] = (tok[1], tok[2])
        for w in writes:
            w.w = tok
            w.r = {}
        return tok

    def barrier(self):
        toks = list(self.last.values())
        for e in self.ENG:
            waits = []
            for t in toks:
                self._need(e, (t[0], t[1], "x"), waits)
            if waits:
                self.ops[e].append((waits, None, None))

    def replay(self, eng, e):
        for waits, fn, inc in self.ops[eng]:
            for key, val in waits:
                e.wait_ge(self.sems[key], val)
            if fn is not None:
                ins = fn(e)
                ins.then_inc(self.sems[inc[0]], inc[1])


class Arena:
    def __init__(self, ap, nelem):
        self.ap = ap
        self.n = nelem
        self.off = 0

    def reset(self):
        self.off = 0

    def alloc(self, free_shape, dtype, parts=128):
        n = int(np.prod(free_shape))
        nb = n * (2 if dtype == F32 else 1)
        nb = (nb + 15) // 16 * 16
        assert self.off + nb <= self.n, ("arena overflow", self.off, nb, self.n)
        v = self.ap[0:parts, self.off:self.off + nb]
        self.off += nb
        if dtype == F32:
            v = v.bitcast(F32)
        v = v[:, 0:n]
        if len(free_shape) == 2:
            v = v.rearrange("p (a b) -> p a b", b=free_shape[1])
        elif len(free_shape) == 3:
            v = v.rearrange("p (a b c) -> p a b c", b=free_shape[1], c=free_shape[2])
        return v, Res()


def build_program():
    nc = bass.Bass("TRN2", target_bir_lowering=False)
    es = ExitStack()

    def din(name, shape, dt=F32):
        return nc.dram_tensor(name, list(shape), dt, kind="ExternalInput").ap()

    def dscr(name, shape, dt=BF16):
        return nc.dram_tensor(name, list(shape), dt).ap()

    x_all = din("x_all", [S, D])
    xT_all = din("xT_all", [D, S])
    x_my = din("x_my", [NSLOT * TT, D])
    xT_my = din("xT_my", [D, NSLOT * TT])
    wsrc = {}
    for nm in ("ffn1_w_gate", "ffn1_w_up", "ffn2_w_gate", "ffn2_w_up"):
        wsrc[nm] = din(nm, [D, DFF])
    for nm in ("ffn1_w_down", "ffn2_w_down"):
        wsrc[nm] = din(nm, [DFF, D])
    wsrc["w_kvf"] = din("w_kvf", [D, NKV_F])
    wsrc["w_kvt"] = din("w_kvt", [D, NKV_T])
    wsrc["w_q"] = din("w_q", [D, NQ])
    wsrc["w_uq"] = din("w_uq", [512, 2048])
    wsrc["w_ukv"] = din("w_ukv", [256, 2048])
    wsrc["w_out"] = din("w_out", [D, D])
    wsrc["w1_k"] = din("w1_k", [4096, 256])
    wsrc["w1_v"] = din("w1_v", [4096, 256])
    wsrc["w2_k"] = din("w2_k", [256, 128])
    wsrc["w2_v"] = din("w2_v", [256, 128])
    ln_in = {k: din(k, [1, D]) for k in ("ln1_g", "ln1_b", "ln2_g", "ln2_b", "ln3_g", "ln3_b")}
    qg_in = din("qg", [128, 4])
    kvg_in = din("kvg", [128, 2])
    gateb_in = din("gateb", [24, 1])
    peT_k_in = din("peT_k", [128, 32])
    peT_v_in = din("peT_v", [128, 32])
    cs64_all = din("cs64_all", [64, 2, S])
    cs128_all = din("cs128_all", [128, 2, S])
    cs64_my = din("cs64_my", [64, 2, NSLOT * TT])
    cs128_my = din("cs128_my", [128, 2, NSLOT * TT])
    sbias_in = din("sbias", [128, 16, 256])
    thr_in = din("thr", [128, 96])
    ident_in = din("ident", [128, 128])
    eexp_in = din("eexp", [128, 64 * 128])
    ov_in = din("ov", [128, 8 * 256])
    out = nc.dram_tensor("out", [NSLOT * TT, D], F32, kind="ExternalOutput").ap()

    wb = {nm: dscr("b_" + nm, ap.shape) for nm, ap in wsrc.items()}
    KN = dscr("KN", [8, 128, S])
    KR = dscr("KR", [64, S])
    VM = dscr("VM", [S, 1024])
    KS = dscr("KS", [2, 128, S])
    KW = dscr("KW", [2, 128, S])
    KC = dscr("KC", [2, 128, S])
    VC = dscr("VC", [2, 128, S])
    VSW = dscr("VSW", [S, 512])

    ARENA_N = 80 * 1024
    arena_t = es.enter_context(nc.sbuf_tensor("arena", [128, ARENA_N], BF16))
    A = Arena(arena_t, ARENA_N)

    def pers(name, shape, dt):
        t = es.enter_context(nc.sbuf_tensor("s_" + name, list(shape), dt))
        return t, Res()

    ident_f, r_identf = pers("ident_f", [128, 128], F32)
    ident_b, r_identb = pers("ident_b", [128, 128], BF16)
    ones_b, r_ones = pers("ones_b", [128, 128], BF16)
    eexp, r_eexp = pers("eexp", [128, 64 * 128], BF16)
    ovb, r_ov = pers("ovb", [128, 8 * 256], BF16)
    ones_f, r_onesf = pers("ones_f", [24, 128], F32)
    thr, r_thr = pers("thr", [128, 96], F32)
    iota_pf, r_iota = pers("iota_pf", [128, 512], F32)
    iota_16, r_iota16 = pers("iota_16", [128, 512], F32)
    qg, r_qg = pers("qg", [128, 4], F32)
    kvg, r_kvg = pers("kvg", [128, 2], F32)
    gateb, r_gateb = pers("gateb", [24, 1], F32)
    kcT, r_kcT = pers("kcT", [128, 2, 1024], BF16)
    vcc, r_vcc = pers("vcc", [128, 2, 8, 128], BF16)
    psum = []
    for i in range(8):
        t = es.enter_context(nc.psum_tensor("ps%d" % i, [128, 512], F32))
        psum.append((t, Res()))

    cx = Ctx(nc, es)
    rD = {}
    dbg_t = {}
    dbg_off = {"f": 0, "b": 0}
    if KDBG:
        dbg_t["f"] = nc.dram_tensor("dbg_f", [128, 40960], F32, kind="ExternalOutput").ap()
        dbg_t["b"] = nc.dram_tensor("dbg_b", [128, 81920], BF16, kind="ExternalOutput").ap()

    def dbg(name, ap, reads, parts=128):
        if not KDBG:
            return
        k = "f" if ap.dtype == F32 else "b"
        shp = list(ap.shape)
        n = int(np.prod(shp[1:]))
        off = dbg_off[k]
        dbg_off[k] = off + n
        DBG_LAYOUT[name] = (k, off, shp)
        dst = dbg_t[k][0:shp[0], off:off + n]
        if len(shp) == 3:
            dst = dst.rearrange("p (a b) -> p a b", b=shp[2])
        elif len(shp) == 4:
            dst = dst.rearrange("p (a b c) -> p a b c", b=shp[2], c=shp[3])
        cx.op("sp", lambda e: e.dma_start(out=dst, in_=ap), reads, [dres("dbg")], dma=True)

    def dres(name):
        if name not in rD:
            rD[name] = Res()
        return rD[name]

    def dma(q, out_ap, in_ap, reads, writes):
        return cx.op(q, lambda e: e.dma_start(out=out_ap, in_=in_ap), reads, writes, dma=True)

    def mm(ps, lhsT, rhs, start, stop, reads, writes):
        return cx.op("pe", lambda e: e.matmul(ps, lhsT, rhs, start=start, stop=stop), reads, writes)

    def tr(ps, in_, idt, reads, writes):
        return cx.op("pe", lambda e: e.transpose(ps, in_, idt), reads, writes)

    def act(out_ap, in_ap, func, reads, writes, scale=1.0, bias=0.0):
        return cx.op("act", lambda e: e.activation(out=out_ap, in_=in_ap, func=func, scale=scale, bias=bias),
                     reads, writes)

    def v_tt(eng, out_ap, a, b, op, reads, writes):
        return cx.op(eng, lambda e: e.tensor_tensor(out=out_ap, in0=a, in1=b, op=op), reads, writes)

    def v_ts(eng, out_ap, a, s1, s2, op0, op1, reads, writes):
        if s2 is None:
            return cx.op(eng, lambda e: e.tensor_scalar(out=out_ap, in0=a, scalar1=s1, scalar2=None, op0=op0),
                         reads, writes)
        return cx.op(eng, lambda e: e.tensor_scalar(out=out_ap, in0=a, scalar1=s1, scalar2=s2, op0=op0, op1=op1),
                     reads, writes)

    def v_stt(eng, out_ap, a, s, b, op0, op1, reads, writes):
        return cx.op(eng, lambda e: e.scalar_tensor_tensor(out=out_ap, in0=a, scalar=s, in1=b, op0=op0, op1=op1),
                     reads, writes)

    def v_copy(eng, out_ap, in_ap, reads, writes):
        return cx.op(eng, lambda e: e.tensor_copy(out=out_ap, in_=in_ap), reads, writes)

    cast_rr = [0]

    def cast(out_ap, in_ap, reads, writes):
        cast_rr[0] += 1
        if cast_rr[0] % 2:
            return act(out_ap, in_ap, AF.Copy, reads, writes)
        return v_copy("dve", out_ap, in_ap, reads, writes)

    A.reset()
    dma("sp", ident_f[:], ident_in, [], [r_identf])
    v_copy("dve", ident_b[:], ident_f[:], [r_identf], [r_identb])
    cx.op("pool", lambda e: e.memset(ones_b[:], 1.0), [], [r_ones])
    cx.op("pool", lambda e: e.memset(kcT[:], 0.0), [], [r_kcT])
    cx.op("pool", lambda e: e.memset(vcc[:], 0.0), [], [r_vcc])
    cx.op("pool", lambda e: e.memset(ones_f[:], 1.0), [], [r_onesf])
    dma("sp", thr[:], thr_in, [], [r_thr])
    dma("sp", qg[:], qg_in, [], [r_qg])
    dma("sp", kvg[:], kvg_in, [], [r_kvg])
    dma("sp", gateb[:], gateb_in, [], [r_gateb])
    cx.op("pool", lambda e: e.iota(iota_pf[:], [[-1, 512]], base=0, channel_multiplier=1,
                                   allow_small_or_imprecise_dtypes=True), [], [r_iota])
    cx.op("pool", lambda e: e.iota(iota_16[:], [[-1, 512]], base=0, channel_multiplier=16,
                                   allow_small_or_imprecise_dtypes=True), [], [r_iota16])
    st0, r_st0 = A.alloc([64 * 128], F32)
    dma("sp", st0, eexp_in, [], [r_st0])
    cast(eexp[:], st0, [r_st0], [r_eexp])
    st1, r_st1 = A.alloc([8 * 256], F32)
    dma("sp", st1, ov_in, [], [r_st1])
    cast(ovb[:], st1, [r_st1], [r_ov])

    CW = 4096
    stg = [A.alloc([CW], F32) for _ in range(3)]
    stb = [A.alloc([CW], BF16) for _ in range(3)]
    ci = 0
    for nm, src in wsrc.items():
        K_, N_ = src.shape
        flat_s = src.rearrange("k n -> (k n)")
        flat_d = wb[nm].rearrange("k n -> (k n)")
        tot = K_ * N_
        per = 128 * CW
        o = 0
        while o < tot:
            n = min(per, tot - o)
            w_ = n // 128
            assert w_ * 128 == n, (nm, n)
            (sf, rsf), (sb_, rsb) = stg[ci % 3], stb[ci % 3]
            ci += 1
            dma("sp", sf[:, 0:w_], flat_s[o:o + n].rearrange("(p w) -> p w", p=128), [], [rsf])
            cast(sb_[:, 0:w_], sf[:, 0:w_], [rsf], [rsb])
            dma("sp", flat_d[o:o + n].rearrange("(p w) -> p w", p=128), sb_[:, 0:w_], [rsb], [dres("b_" + nm)])
            o += n
    cx.barrier()
    if STOP == 0:
        return nc, es, cx

    def ffn_ln(A, xT_src, x_src, pre, lng, lnb, y_t, r_y, xT_res=None):
        Wg, Wu, Wd = wb[pre + "_w_gate"], wb[pre + "_w_up"], wb[pre + "_w_down"]
        rWg, rWu, rWd = dres("b_" + pre + "_w_gate"), dres("b_" + pre + "_w_up"), dres("b_" + pre + "_w_down")
        if xT_res is None:
            xTb, r_xTb = A.alloc([16, TT], BF16)
            sx = [A.alloc([2, TT], F32) for _ in range(2)]
            for c4 in range(8):
                sf, rsf = sx[c4 % 2]
                dma("sp", sf, xT_src[c4 * 256:(c4 + 1) * 256, :].rearrange("(kc p) t -> p kc t", p=128), [], [rsf])
                cast(xTb[:, c4 * 2:(c4 + 1) * 2, :], sf, [rsf], [r_xTb])
        else:
            xTb, r_xTb = xT_src, xT_res
        if x_src is not None:
            dma("sp", y_t, x_src.rearrange("(tb p) d -> p tb d", p=128), [], [r_y])
        ln_mark = A.off
        hT, r_hT = A.alloc([44, TT], BF16)
        GW = 256
        wg_t = [A.alloc([16, GW], BF16) for _ in range(2)]
        wu_t = [A.alloc([16, GW], BF16) for _ in range(2)]
        sg_t = [A.alloc([TT], F32) for _ in range(2)]
        for fg in range(DFF // GW):
            (wg, rwg), (wu, rwu) = wg_t[fg % 2], wu_t[fg % 2]
            dma("sp", wg, Wg[:, fg * GW:(fg + 1) * GW].rearrange("(kc p) n -> p kc n", p=128), [rWg], [rwg])
            dma("sp", wu, Wu[:, fg * GW:(fg + 1) * GW].rearrange("(kc p) n -> p kc n", p=128), [rWu], [rwu])
            for j in range(GW // 128):
                m = fg * (GW // 128) + j
                (pg, rpg), (pu, rpu) = psum[(2 * m) % 8], psum[(2 * m + 1) % 8]
                for kc in range(16):
                    mm(pg[:], wg[:, kc, j * 128:(j + 1) * 128], xTb[:, kc, :], kc == 0, kc == 15, [rwg, r_xTb], [rpg])
                for kc in range(16):
                    mm(pu[:], wu[:, kc, j * 128:(j + 1) * 128], xTb[:, kc, :], kc == 0, kc == 15, [rwu, r_xTb], [rpu])
                sg, rsg = sg_t[m % 2]
                act(sg, pg[:], AF.Silu, [rpg], [rsg])
                v_tt("dve", hT[:, m, :], sg, pu[:], ALU.mult, [rsg, rpu], [r_hT])
        for tb in range(4):
            act(y_t[:, tb, :], y_t[:, tb, :], AF.Copy, [r_y], [r_y], scale=ALPHA)
        KG = 4
        wd_t = [A.alloc([KG, 512], BF16) for _ in range(2)]
        wi = 0
        for ng in range(4):
            for kg in range(11):
                wd, rwd = wd_t[wi % 2]
                wi += 1
                dma("sp", wd, Wd[kg * KG * 128:(kg + 1) * KG * 128, ng * 512:(ng + 1) * 512]
                    .rearrange("(kc p) n -> p kc n", p=128), [rWd], [rwd])
                for tb in range(4):
                    ps, rps = psum[tb + 4 * (ng % 2)]
                    for kl in range(KG):
                        kc = kg * KG + kl
                        mm(ps[:], hT[:, kc, tb * 128:(tb + 1) * 128], wd[:, kl, :], kc == 0, kc == 43,
                           [r_hT, rwd], [rps])
            for tb in range(4):
                ps, rps = psum[tb + 4 * (ng % 2)]
                v_stt("dve", y_t[:, tb, ng * 512:(ng + 1) * 512], ps[:], 0.5, y_t[:, tb, ng * 512:(ng + 1) * 512],
                      ALU.mult, ALU.add, [rps, r_y], [r_y])
        cx.barrier()
        A.off = ln_mark
        layer_norm(A, y_t, r_y, lng, lnb)

    def layer_norm(A, y_t, r_y, lng, lnb):
        gb, r_gb = A.alloc([2, D], F32)
        dma("sp", gb[:, 0, :], lng.partition_broadcast(128), [], [r_gb])
        dma("sp", gb[:, 1, :], lnb.partition_broadcast(128), [], [r_gb])
        st, r_st = A.alloc([4, 4, 6], F32)
        mv, r_mv = A.alloc([4, 2], F32)
        rs, r_rs = A.alloc([4, 1], F32)
        for tb in range(4):
            for c in range(4):
                cx.op("dve", lambda e, tb=tb, c=c: e.bn_stats(out=st[:, tb, c, :], in_=y_t[:, tb, c * 512:(c + 1) * 512]),
                      [r_y], [r_st])
            cx.op("dve", lambda e, tb=tb: e.bn_aggr(out=mv[:, tb, :], in_=st[:, tb, :, :]), [r_st], [r_mv])
            act(rs[:, tb, :], mv[:, tb, 1:2], AF.Sqrt, [r_mv], [r_rs], scale=1.0, bias=LN_EPS)
            cx.op("dve", lambda e, tb=tb: e.reciprocal(out=rs[:, tb, :], in_=rs[:, tb, :]), [r_rs], [r_rs])
            v_ts("dve", y_t[:, tb, :], y_t[:, tb, :], mv[:, tb, 0:1], rs[:, tb, 0:1], ALU.subtract, ALU.mult,
                 [r_y, r_mv, r_rs], [r_y])
            v_tt("pool", y_t[:, tb, :], y_t[:, tb, :], gb[:, 0, :], ALU.mult, [r_y, r_gb], [r_y])
            v_tt("dve", y_t[:, tb, :], y_t[:, tb, :], gb[:, 1, :], ALU.add, [r_y, r_gb], [r_y])

    def transpose_to_bf16(y_t, r_y, xT, r_xT):
        for fc in range(16):
            ps, rps = psum[fc % 8]
            for tb in range(4):
                tr(ps[:, tb * 128:(tb + 1) * 128], y_t[:, tb, fc * 128:(fc + 1) * 128], ident_f[:],
                   [r_y, r_identf], [rps])
            cast(xT[:, fc, :], ps[:], [rps], [r_xT])

    def rms_scale(A, srcT, r_src, nchunk, width, gvec, r_g, dstT, r_dst):
        sq, r_sq = A.alloc([nchunk, TT], BF16)
        for c in range(nchunk):
            act(sq[:, c, :], srcT[:, c, :], AF.Square, [r_src], [r_sq])
        ps, rps = psum[7]
        for c in range(nchunk):
            mm(ps[:], ones_b[:], sq[:, c, :], c == 0, c == nchunk - 1, [r_ones, r_sq], [rps])
        rr, r_rr = A.alloc([TT], F32)
        act(rr, ps[:], AF.Sqrt, [rps], [r_rr], scale=1.0 / width, bias=RMS_EPS)
        cx.op("dve", lambda e: e.reciprocal(out=rr, in_=rr), [r_rr], [r_rr])
        for c in range(nchunk):
            v_stt("dve", dstT[:, c, :], srcT[:, c, :], gvec[:, c:c + 1], rr, ALU.mult, ALU.mult,
                  [r_src, r_g, r_rr], [r_dst])

    def rope_comb(eng, dst, xa, xb, cs, npart, reads, writes, tmp):
        v_tt("dve", tmp[0:npart, :], xb, cs[0:npart, 1, :], ALU.mult, reads, [writes[1]])
        v_tt("dve", dst, xa, cs[0:npart, 0, :], ALU.mult, reads, [writes[0]])
        v_tt(eng, dst, dst, tmp[0:npart, :], ALU.add, [writes[0], writes[1]], [writes[0]])

    for t in range(min(NT_ALL, NT1)):
        A.reset()
        t0 = t * TT
        y_t, r_y = A.alloc([4, D], F32)
        ffn_ln(A, xT_all[:, t0:t0 + TT], x_all[t0:t0 + TT, :], "ffn1", ln_in["ln1_g"], ln_in["ln1_b"], y_t, r_y)
        cx.barrier()
        A.off = 4 * D * 2
        x1T, r_x1T = A.alloc([16, TT], BF16)
        transpose_to_bf16(y_t, r_y, x1T, r_x1T)
        cs64, r_cs64 = A.alloc([2, TT], F32, parts=64)
        cs128, r_cs128 = A.alloc([2, TT], F32)
        dma("sp", cs64, cs64_all[:, :, t0:t0 + TT], [], [r_cs64])
        dma("sp", cs128, cs128_all[:, :, t0:t0 + TT], [], [r_cs128])
        wkv_t = [A.alloc([16, 256], BF16) for _ in range(2)]
        tmpr, r_tmpr = A.alloc([TT], F32)
        ob_t = [A.alloc([TT], BF16) for _ in range(2)]
        ckvT, r_ckvT = A.alloc([2, TT], F32)
        rW = dres("b_w_kvf")
        oi = 0
        def load_w(col0, ncol, idx):
            w, rw = wkv_t[idx % 2]
            dma("sp", w[:, :, 0:ncol], wb["w_kvf"][:, col0:col0 + ncol].rearrange("(kc p) n -> p kc n", p=128),
                [rW], [rw])
            return w, rw
        wi = 0
        w, rw = load_w(0, 256, wi); wi += 1
        for c in range(2):
            ps, rps = psum[c]
            for kc in range(16):
                mm(ps[:], w[:, kc, c * 128:(c + 1) * 128], x1T[:, kc, :], kc == 0, kc == 15, [rw, r_x1T], [rps])
            act(ckvT[:, c, :], ps[:], AF.Copy, [rps], [r_ckvT])
        w, rw = load_w(256, 128, wi); wi += 1
        psa, rpsa = psum[2]
        psb, rpsb = psum[3]
        for kc in range(16):
            mm(psa[0:64, :], w[:, kc, 0:64], x1T[:, kc, :], kc == 0, kc == 15, [rw, r_x1T], [rpsa])
        for kc in range(16):
            mm(psb[0:64, :], w[:, kc, 64:128], x1T[:, kc, :], kc == 0, kc == 15, [rw, r_x1T], [rpsb])
        ob, rob = ob_t[oi % 2]; oi += 1
        rope_comb("dve", ob[0:64, :], psa[0:64, :], psb[0:64, :], cs64, 64, [rpsa, rpsb, r_cs64], [rob, r_tmpr], tmpr)
        dma("sp", KR[:, t0:t0 + TT], ob[0:64, :], [rob], [dres("KR")])
        for gi, (dst, nm) in enumerate(((KC, "KC"), (KS, "KS"), (KW, "KW"))):
            for kh in range(2):
                w, rw = load_w(384 + 256 * (gi * 2 + kh), 256, wi); wi += 1
                psa, rpsa = psum[(2 * wi) % 8]
                psb, rpsb = psum[(2 * wi + 1) % 8]
                for kc in range(16):
                    mm(psa[:], w[:, kc, 0:128], x1T[:, kc, :], kc == 0, kc == 15, [rw, r_x1T], [rpsa])
                for kc in range(16):
                    mm(psb[:], w[:, kc, 128:256], x1T[:, kc, :], kc == 0, kc == 15, [rw, r_x1T], [rpsb])
                ob, rob = ob_t[oi % 2]; oi += 1
                rope_comb("dve", ob, psa[:], psb[:], cs128, 128, [rpsa, rpsb, r_cs128], [rob, r_tmpr], tmpr)
                dma("sp", dst[kh, :, t0:t0 + TT], ob, [rob], [dres(nm)])
        w, rw = load_w(1920, 256, wi); wi += 1
        for kh in range(2):
            ps, rps = psum[4 + kh]
            for kc in range(16):
                mm(ps[:], w[:, kc, kh * 128:(kh + 1) * 128], x1T[:, kc, :], kc == 0, kc == 15, [rw, r_x1T], [rps])
            ob, rob = ob_t[oi % 2]; oi += 1
            cast(ob, ps[:], [rps], [rob])
            dma("sp", VC[kh, :, t0:t0 + TT], ob, [rob], [dres("VC")])
        wt_t = [A.alloc([16, 256], BF16) for _ in range(2)]
        vo, r_vo = A.alloc([4, 512], BF16)
        for hf in range(2):
            w, rw = wt_t[hf]
            dma("sp", w, wb["w_kvt"][:, hf * 256:(hf + 1) * 256].rearrange("(kc p) n -> p kc n", p=128),
                [dres("b_w_kvt")], [rw])
            for tb in range(4):
                ps, rps = psum[tb + 4 * hf]
                for kc in range(16):
                    mm(ps[:, 0:256], x1T[:, kc, tb * 128:(tb + 1) * 128], w[:, kc, :], kc == 0, kc == 15,
                       [rw, r_x1T], [rps])
                cast(vo[:, tb, hf * 256:(hf + 1) * 256], ps[:, 0:256], [rps], [r_vo])
        dma("sp", VSW[t0:t0 + TT, :].rearrange("(tb p) n -> p tb n", p=128), vo, [r_vo], [dres("VSW")])
        ckvn, r_ckvn = A.alloc([2, TT], BF16)
        rms_scale(A, ckvT, r_ckvT, 2, 256.0, kvg, r_kvg, ckvn, r_ckvn)
        wukv, r_wukv = A.alloc([2, 2048], BF16)
        dma("sp", wukv, wb["w_ukv"].rearrange("(kc p) n -> p kc n", p=128), [dres("b_w_ukv")], [r_wukv])
        for h in range(8):
            ps, rps = psum[h % 4]
            for kc in range(2):
                mm(ps[:], wukv[:, kc, h * 256:h * 256 + 128], ckvn[:, kc, :], kc == 0, kc == 1, [r_wukv, r_ckvn], [rps])
            ob, rob = ob_t[oi % 2]; oi += 1
            cast(ob, ps[:], [rps], [rob])
            dma("sp", KN[h, :, t0:t0 + TT], ob, [rob], [dres("KN")])
        vm, r_vm = A.alloc([4, 1024], BF16)
        for tb in range(4):
            for hg in range(2):
                ps, rps = psum[4 + (tb * 2 + hg) % 4]
                for kc in range(2):
                    rhs = wukv[:, kc, hg * 1024:(hg + 1) * 1024].rearrange("p (h c) -> p h c", c=256)[:, :, 128:256]
                    mm(ps[:].rearrange("p (h c) -> p h c", c=128), ckvn[:, kc, tb * 128:(tb + 1) * 128], rhs,
                       kc == 0, kc == 1, [r_wukv, r_ckvn], [rps])
                cast(vm[:, tb, hg * 512:(hg + 1) * 512], ps[:], [rps], [r_vm])
        dma("sp", VM[t0:t0 + TT, :].rearrange("(tb p) n -> p tb n", p=128), vm, [r_vm], [dres("VM")])
        cx.barrier()
    cx.barrier()

    if STOP == 1:
        return nc, es, cx
    A.reset()
    w1, r_w1 = A.alloc([32, 256], BF16)
    w2, r_w2 = A.alloc([2, 128], BF16)
    pef, r_pef = A.alloc([32], F32)
    peb, r_peb = A.alloc([32], BF16)
    bias, r_bias = A.alloc([2], F32)
    tT, r_tT = A.alloc([S], BF16)
    hid, r_hid = A.alloc([2, 1024], BF16)
    u1, r_u1 = A.alloc([512], F32)
    u2, r_u2 = A.alloc([512], F32)
    for ti, (srcD, nm, w1n, w2n, peT_in) in enumerate(((KC, "KC", "w1_k", "w2_k", peT_k_in),
                                                         (VC, "VC", "w1_v", "w2_v", peT_v_in))):
        dma("sp", w1, wb[w1n].rearrange("(l p) n -> p l n", p=128), [dres("b_" + w1n)], [r_w1])
        dma("sp", w2, wb[w2n].rearrange("(c p) n -> p c n", p=128), [dres("b_" + w2n)], [r_w2])
        dma("sp", pef, peT_in, [], [r_pef])
        v_copy("dve", peb, pef, [r_pef], [r_peb])
        for hc in range(2):
            ps, rps = psum[hc]
            for l in range(32):
                mm(ps[:, 0:1], w1[:, l, hc * 128:(hc + 1) * 128], peb[:, l:l + 1], l == 0, l == 31, [r_w1, r_peb], [rps])
            v_copy("dve", bias[:, hc:hc + 1], ps[:, 0:1], [rps], [r_bias])
        for kh in range(2):
            dma("sp", tT, srcD[kh], [dres(nm)], [r_tT])
            cx.op("pool", lambda e: e.memset(hid, 0.0), [], [r_hid])
            for cg in range(2):
                ncs = 512 if cg == 0 else 511
                for hc in range(2):
                    ps, rps = psum[2 + (cg * 2 + hc) % 4]
                    for l in range(32):
                        st_ = l + 16 * 512 * cg
                        mm(ps[:, 0:ncs], w1[:, l, hc * 128:(hc + 1) * 128], tT[:, st_:st_ + 16 * (ncs - 1) + 1:16],
                           l == 0, l == 31, [r_w1, r_tT], [rps])
                    act(u1[:, 0:ncs], ps[:, 0:ncs], AF.Identity, [rps, r_bias], [r_u1], bias=bias[:, hc:hc + 1])
                    v_tt("dve", u2[:, 0:ncs], u1[:, 0:ncs], u1[:, 0:ncs], ALU.mult, [r_u1], [r_u2])
                    v_ts("dve", u2[:, 0:ncs], u2[:, 0:ncs], 0.044715, 1.0, ALU.mult, ALU.add, [r_u2], [r_u2])
                    v_tt("dve", u2[:, 0:ncs], u2[:, 0:ncs], u1[:, 0:ncs], ALU.mult, [r_u2, r_u1], [r_u2])
                    act(u2[:, 0:ncs], u2[:, 0:ncs], AF.Sigmoid, [r_u2], [r_u2], scale=1.5957691216057308)
                    v_tt("dve", hid[:, hc, cg * 512:cg * 512 + ncs], u2[:, 0:ncs], u1[:, 0:ncs], ALU.mult,
                         [r_u2, r_u1], [r_hid])
            if ti == 0:
                for cg in range(2):
                    ps, rps = psum[6 + cg]
                    for hc in range(2):
                        mm(ps[:], w2[:, hc, :], hid[:, hc, cg * 512:(cg + 1) * 512], hc == 0, hc == 1, [r_w2, r_hid], [rps])
                    cast(kcT[:, kh, cg * 512:(cg + 1) * 512], ps[:], [rps], [r_kcT])
            else:
                for jb in range(8):
                    ps, rps = psum[6 + jb % 2]
                    for hc in range(2):
                        mm(ps[:, 0:128], hid[:, hc, jb * 128:(jb + 1) * 128], w2[:, hc, :], hc == 0, hc == 1,
                           [r_w2, r_hid], [rps])
                    cast(vcc[:, kh, jb, :], ps[:, 0:128], [rps], [r_vcc])
    cx.barrier()

    if STOP == 3:
        return nc, es, cx
    def flash_branch(A, K_tile_fn, nkb, q_ap, r_q, extra_q, v_fn, mask_fn, scale, po, rpo, psm, rpsm):
        for kb in range(nkb):
            kT, r_kT, kT2, r_kT2 = K_tile_fn(kb)
            ps, rps = psum[4 + kb % 3]
            mm(ps[:], kT, q_ap, True, kT2 is None, [r_kT, r_q], [rps])
            if kT2 is not None:
                mm(ps[:], kT2, extra_q, False, True, [r_kT2, r_q], [rps])
            e_, r_e = e_tiles[kb % 3]
            act(e_, ps[:], AF.Exp, [rps], [r_e], scale=scale)
            mask_fn(kb, e_, r_e)
            vv, r_vv = v_fn(kb)
            mm(po[:], vv, e_, kb == 0, kb == nkb - 1, [r_vv, r_e], [rpo])
            mm(psm[:], ones_b[:], e_, kb == 0, kb == nkb - 1, [r_ones, r_e], [rpsm])

    e_tiles = None

    for si in range(min(NSLOT, KSLOTS)):
        A.reset()
        m0 = si * TT
        y_t, r_y = A.alloc([4, D], F32)
        ffn_ln(A, xT_my[:, m0:m0 + TT], x_my[m0:m0 + TT, :], "ffn1", ln_in["ln1_g"], ln_in["ln1_b"], y_t, r_y)
        cx.barrier()
        A.off = 4 * D * 2
        x1T, r_x1T = A.alloc([16, TT], BF16)
        transpose_to_bf16(y_t, r_y, x1T, r_x1T)
        if si == 0:
            dbg("x1", y_t, [r_y])
            dbg("x1T", x1T, [r_x1T])
        cs64, r_cs64 = A.alloc([2, TT], F32, parts=64)
        cs128, r_cs128 = A.alloc([2, TT], F32)
        dma("sp", cs64, cs64_my[:, :, m0:m0 + TT], [], [r_cs64])
        dma("sp", cs128, cs128_my[:, :, m0:m0 + TT], [], [r_cs128])
        tmpr, r_tmpr = A.alloc([TT], F32)
        qn, r_qn = A.alloc([8, TT], BF16)
        qr, r_qr = A.alloc([8, TT], BF16, parts=64)
        qs, r_qs = A.alloc([8, TT], BF16)
        gT, r_gT = A.alloc([TT], F32, parts=24)
        mixT, r_mixT = A.alloc([16, TT], BF16)
        mark_q = A.off
        wq_t = [A.alloc([16, 256], BF16) for _ in range(2)]
        cqT, r_cqT = A.alloc([4, TT], F32)
        rWq = dres("b_w_q")
        wi = 0
        for cg in range(2):
            w, rw = wq_t[wi % 2]; wi += 1
            dma("sp", w, wb["w_q"][:, cg * 256:(cg + 1) * 256].rearrange("(kc p) n -> p kc n", p=128), [rWq], [rw])
            for c in range(2):
                ps, rps = psum[(cg * 2 + c) % 4]
                for kc in range(16):
                    mm(ps[:], w[:, kc, c * 128:(c + 1) * 128], x1T[:, kc, :], kc == 0, kc == 15, [rw, r_x1T], [rps])
                act(cqT[:, cg * 2 + c, :], ps[:], AF.Copy, [rps], [r_cqT])
        for h in range(8):
            w, rw = wq_t[wi % 2]; wi += 1
            dma("sp", w, wb["w_q"][:, 512 + h * 256:512 + (h + 1) * 256].rearrange("(kc p) n -> p kc n", p=128),
                [rWq], [rw])
            psa, rpsa = psum[(2 * h) % 4]
            psb, rpsb = psum[(2 * h + 1) % 4]
            for kc in range(16):
                mm(psa[:], w[:, kc, 0:128], x1T[:, kc, :], kc == 0, kc == 15, [rw, r_x1T], [rpsa])
            for kc in range(16):
                mm(psb[:], w[:, kc, 128:256], x1T[:, kc, :], kc == 0, kc == 15, [rw, r_x1T], [rpsb])
            rope_comb("dve", qs[:, h, :], psa[:], psb[:], cs128, 128, [rpsa, rpsb, r_cs128], [r_qs, r_tmpr], tmpr)
        w, rw = wq_t[wi % 2]; wi += 1
        dma("sp", w[:, :, 0:24], wb["w_q"][:, 2560:2584].rearrange("(kc p) n -> p kc n", p=128), [rWq], [rw])
        ps, rps = psum[4]
        for kc in range(16):
            mm(ps[0:24, :], w[:, kc, 0:24], x1T[:, kc, :], kc == 0, kc == 15, [rw, r_x1T], [rps])
        act(gT, ps[0:24, :], AF.Sigmoid, [rps, r_gateb], [r_gT], bias=gateb[:, 0:1])
        cqn, r_cqn = A.alloc([4, TT], BF16)
        rms_scale(A, cqT, r_cqT, 4, 512.0, qg, r_qg, cqn, r_cqn)
        wuq, r_wuq = A.alloc([4, 2048], BF16)
        dma("sp", wuq, wb["w_uq"].rearrange("(kc p) n -> p kc n", p=128), [dres("b_w_uq")], [r_wuq])
        for h in range(8):
            ps, rps = psum[h % 2]
            for kc in range(4):
                mm(ps[:], wuq[:, kc, h * 256:h * 256 + 128], cqn[:, kc, :], kc == 0, kc == 3, [r_wuq, r_cqn], [rps])
            cast(qn[:, h, :], ps[:], [rps], [r_qn])
            psa, rpsa = psum[2 + (2 * h) % 4]
            psb, rpsb = psum[2 + (2 * h + 1) % 4]
            for kc in range(4):
                mm(psa[0:64, :], wuq[:, kc, h * 256 + 128:h * 256 + 192], cqn[:, kc, :], kc == 0, kc == 3,
                   [r_wuq, r_cqn], [rpsa])
            for kc in range(4):
                mm(psb[0:64, :], wuq[:, kc, h * 256 + 192:h * 256 + 256], cqn[:, kc, :], kc == 0, kc == 3,
                   [r_wuq, r_cqn], [rpsb])
            rope_comb("dve", qr[:, h, :], psa[0:64, :], psb[0:64, :], cs64, 64, [rpsa, rpsb, r_cs64],
                      [r_qr, r_tmpr], tmpr)
        if si == 0:
            dbg("qn", qn, [r_qn]); dbg("qr", qr, [r_qr]); dbg("qs", qs, [r_qs]); dbg("gT", gT, [r_gT])
            dbg("cqT", cqT, [r_cqT])
        cx.barrier()
        if STOP == 4:
            return nc, es, cx

        A.off = mark_q
        e_tiles = [A.alloc([TT], BF16) for _ in range(3)]
        kt_t = [A.alloc([TT], BF16) for _ in range(3)]
        kr_t = [A.alloc([TT], BF16, parts=64) for _ in range(3)]
        vt_t = [A.alloc([4, 128], BF16) for _ in range(3)]
        rsum, r_rsum = A.alloc([TT], F32)
        gbc, r_gbc = A.alloc([TT], F32)
        nkt_all = 8 * (si + 1)

        def causal_mask(kbz, e_, r_e):
            v_stt("dve", e_, iota_pf[:], thr[:, kbz + 4:kbz + 5], e_, ALU.is_le, ALU.mult, [r_iota, r_thr, r_e], [r_e])

        for h in range(8):
            po, rpo = psum[0 + 2 * (h % 2)]
            psm, rpsm = psum[1 + 2 * (h % 2)]
            nkb = nkt_all * 4
            state = {}

            def K_fn(kb, h=h, state=state):
                kt_i, kl = divmod(kb, 4)
                if kl == 0:
                    kt, rkt = kt_t[kt_i % 3]
                    kr_, rkr = kr_t[kt_i % 3]
                    vt, rvt = vt_t[kt_i % 3]
                    k0 = kt_i * TT
                    dma("sp", kt, KN[h, :, k0:k0 + TT], [dres("KN")], [rkt])
                    dma("sp", kr_, KR[:, k0:k0 + TT], [dres("KR")], [rkr])
                    dma("sp", vt, VM[k0:k0 + TT, h * 128:(h + 1) * 128].rearrange("(kb p) n -> p kb n", p=128),
                        [dres("VM")], [rvt])
                    state["cur"] = (kt, rkt, kr_, rkr, vt, rvt)
                kt, rkt, kr_, rkr, vt, rvt = state["cur"]
                return kt[:, kl * 128:(kl + 1) * 128], rkt, kr_[:, kl * 128:(kl + 1) * 128], rkr

            def V_fn(kb, state=state):
                kt, rkt, kr_, rkr, vt, rvt = state["cur"]
                return vt[:, kb % 4, :], rvt

            def M_fn(kb, e_, r_e, si=si):
                if kb >= 32 * si:
                    causal_mask(kb - 32 * si, e_, r_e)

            flash_branch(A, K_fn, nkb, qn[:, h, :], r_qn, qr[:, h, :], V_fn, M_fn, MLA_SCALE, po, rpo, psm, rpsm)
            v_ts("dve", rsum, psm[:], 1e-30, None, ALU.max, None, [rpsm], [r_rsum])
            cx.op("dve", lambda e: e.reciprocal(out=rsum, in_=rsum), [r_rsum], [r_rsum])
            v_tt("dve", mixT[:, h, :], po[:], rsum, ALU.mult, [rpo, r_rsum], [r_mixT])

        if si == 0:
            dbg("mix_mla", mixT, [r_mixT])
        if STOP == 5:
            cx.barrier()
            return nc, es, cx
        impT, r_impT = A.alloc([2, TT], F32)
        ocn, r_ocn = A.alloc([4, TT], F32)
        onsa, r_onsa = A.alloc([TT], F32)
        tmpf, r_tmpf = A.alloc([TT], F32)
        selT, r_selT = A.alloc([2, TT], BF16)
        msb_t = [A.alloc([TT], BF16) for _ in range(2)]
        sbias, r_sbias = A.alloc([4, 256], F32)
        dma("sp", sbias, sbias_in[:, si * 4:(si + 1) * 4, :], [], [r_sbias])
        score, r_score = A.alloc([256], F32)
        work, r_work = A.alloc([256], F32)
        m8, r_m8 = A.alloc([16], F32)
        selq, r_selq = A.alloc([256], BF16)
        selq2, r_selq2 = A.alloc([256], F32)

        gm, r_gm = A.alloc([TT], F32, parts=24)

        def gate_bcast(row):
            ps, rps = psum[7]
            v_ts("dve", gm, gT, ident_f[0:24, row:row + 1], None, ALU.mult, None, [r_gT, r_identf], [r_gm])
            mm(ps[:], ones_f[:], gm, True, True, [r_onesf, r_gm], [rps])
            return ps, rps

        def finish_branch(po, rpo, psm, rpsm, row, dst, r_dst, accumulate):
            v_ts("dve", rsum, psm[:], 1e-30, None, ALU.max, None, [rpsm], [r_rsum])
            cx.op("dve", lambda e: e.reciprocal(out=rsum, in_=rsum), [r_rsum], [r_rsum])
            psg, rpsg = gate_bcast(row)
            v_tt("dve", gbc, psg[:], rsum, ALU.mult, [rpsg, r_rsum], [r_gbc])
            if accumulate:
                v_tt("dve", tmpf, po[:], gbc, ALU.mult, [rpo, r_gbc], [r_tmpf])
                v_tt("pool", dst, dst, tmpf, ALU.add, [r_dst, r_tmpf], [r_dst])
            else:
                v_tt("dve", dst, po[:], gbc, ALU.mult, [rpo, r_gbc], [r_dst])

        for kh in range(2):
            ncb = 2 * si + 2
            for g in range(4):
                hq = kh * 4 + g
                po, rpo = psum[0]
                psm, rpsm = psum[1]
                pi0, rpi0 = psum[2]
                pi1, rpi1 = psum[3]
                for jb in range(ncb):
                    ps, rps = psum[4 + jb % 3]
                    mm(ps[:], kcT[:, kh, jb * 128:(jb + 1) * 128], qs[:, hq, :], True, True, [r_kcT, r_qs], [rps])
                    e_, r_e = e_tiles[jb % 3]
                    act(e_, ps[:], AF.Exp, [rps], [r_e], scale=NSA_SCALE)
                    if jb >= 2 * si - 1:
                        col = 36 + 3 * si + (jb - (2 * si - 1))
                        v_stt("dve", e_, iota_16[:], thr[:, col:col + 1], e_, ALU.is_le, ALU.mult,
                              [r_iota16, r_thr, r_e], [r_e])
                    first, last = jb == 0, jb == ncb - 1
                    mm(po[:], vcc[:, kh, jb, :], e_, first, last, [r_vcc, r_e], [rpo])
                    mm(psm[:], ones_b[:], e_, first, last, [r_ones, r_e], [rpsm])
                    mm(pi0[:], ovb[:, jb * 256:jb * 256 + 128], e_, first, last, [r_ov, r_e], [rpi0])
                    mm(pi1[:], ovb[:, jb * 256 + 128:jb * 256 + 256], e_, first, last, [r_ov, r_e], [rpi1])
                v_ts("dve", rsum, psm[:], 1e-30, None, ALU.max, None, [rpsm], [r_rsum])
                cx.op("dve", lambda e: e.reciprocal(out=rsum, in_=rsum), [r_rsum], [r_rsum])
                for jc, (pi, rpi) in enumerate(((pi0, rpi0), (pi1, rpi1))):
                    if g == 0:
                        v_tt("dve", impT[:, jc, :], pi[:], rsum, ALU.mult, [rpi, r_rsum], [r_impT])
                    else:
                        v_tt("dve", tmpf, pi[:], rsum, ALU.mult, [rpi, r_rsum], [r_tmpf])
                        v_tt("pool", impT[:, jc, :], impT[:, jc, :], tmpf, ALU.add, [r_impT, r_tmpf], [r_impT])
                psg, rpsg = gate_bcast(0 * 8 + hq)
                v_tt("dve", gbc, psg[:], rsum, ALU.mult, [rpsg, r_rsum], [r_gbc])
                v_tt("dve", ocn[:, g, :], po[:], gbc, ALU.mult, [rpo, r_gbc], [r_ocn])
            for qb in range(4):
                ps, rps = psum[4 + qb % 3]
                for jc in range(2):
                    tr(ps[:, jc * 128:(jc + 1) * 128], impT[:, jc, qb * 128:(qb + 1) * 128], ident_f[:],
                       [r_impT, r_identf], [rps])
                v_tt("dve", score, ps[:, 0:256], sbias[:, qb, :], ALU.add, [rps, r_sbias], [r_score])
                cx.op("dve", lambda e: e.max(out=m8[:, 0:8], in_=score), [r_score], [r_m8])
                cx.op("dve", lambda e: e.match_replace(out=work, in_to_replace=m8[:, 0:8], in_values=score,
                                                       imm_value=-3.0e38), [r_score, r_m8], [r_work])
                cx.op("dve", lambda e: e.max(out=m8[:, 8:16], in_=work), [r_work], [r_m8])
                v_ts("dve", selq2, score, m8[:, 15:16], None, ALU.is_ge, None, [r_score, r_m8], [r_selq2])
                v_stt("dve", selq, score, -1.0e29, selq2, ALU.is_gt, ALU.mult, [r_score, r_selq2], [r_selq])
                pst, rpst = psum[7]
                pstb = pst[:].bitcast(BF16)
                for jc in range(2):
                    tr(pstb[:, jc * 128:(jc + 1) * 128], selq[:, jc * 128:(jc + 1) * 128], ident_b[:],
                       [r_selq, r_identb], [rpst])
                for jc in range(2):
                    v_copy("dve", selT[:, jc, qb * 128:(qb + 1) * 128], pstb[:, jc * 128:(jc + 1) * 128],
                           [rpst], [r_selT])
            for g in range(4):
                hq = kh * 4 + g
                po, rpo = psum[0]
                psm, rpsm = psum[1]
                nkb = nkt_all * 4
                state = {}

                def K_fn(kb, kh=kh, state=state):
                    kt_i, kl = divmod(kb, 4)
                    if kl == 0:
                        kt, rkt = kt_t[kt_i % 3]
                        vt, rvt = vt_t[kt_i % 3]
                        k0 = kt_i * TT
                        dma("sp", kt, KS[kh, :, k0:k0 + TT], [dres("KS")], [rkt])
                        dma("sp", vt, VSW[k0:k0 + TT, kh * 128:(kh + 1) * 128].rearrange("(kb p) n -> p kb n", p=128),
                            [dres("VSW")], [rvt])
                        state["cur"] = (kt, rkt, vt, rvt)
                    kt, rkt, vt, rvt = state["cur"]
                    return kt[:, kl * 128:(kl + 1) * 128], rkt, None, None

                def V_fn(kb, state=state):
                    kt, rkt, vt, rvt = state["cur"]
                    return vt[:, kb % 4, :], rvt

                def M_fn(kb, e_, r_e, si=si):
                    pm, rpm = psum[7]
                    mm(pm[:], eexp[:, (kb % 64) * 128:(kb % 64 + 1) * 128], selT[:, kb // 64, :], True, True,
                       [r_eexp, r_selT], [rpm])
                    msb, rmsb = msb_t[kb % 2]
                    if kb >= 32 * si:
                        kbz = kb - 32 * si
                        v_stt("dve", msb, iota_pf[:], thr[:, kbz + 4:kbz + 5], pm[:], ALU.is_le, ALU.mult,
                              [r_iota, r_thr, rpm], [rmsb])
                    else:
                        v_copy("dve", msb, pm[:], [rpm], [rmsb])
                    v_tt("pool", e_, e_, msb, ALU.mult, [r_e, rmsb], [r_e])

                flash_branch(A, K_fn, nkb, qs[:, hq, :], r_qs, None, V_fn, M_fn, NSA_SCALE, po, rpo, psm, rpsm)
                v_copy("pool", onsa, ocn[:, g, :], [r_ocn], [r_onsa])
                finish_branch(po, rpo, psm, rpsm, 1 * 8 + hq, onsa, r_onsa, True)
                po, rpo = psum[2]
                psm, rpsm = psum[3]
                kb_lo = 32 * si - 4 if si > 0 else 0
                nkb = 32 * si + 32 - kb_lo
                state = {}

                def K_fn(kb, kh=kh, state=state, kb_lo=kb_lo):
                    kt_i, kl = divmod(kb, 4)
                    if kl == 0:
                        kt, rkt = kt_t[kt_i % 3]
                        vt, rvt = vt_t[kt_i % 3]
                        k0 = kb_lo * 128 + kt_i * TT
                        dma("sp", kt, KW[kh, :, k0:k0 + TT], [dres("KW")], [rkt])
                        dma("sp", vt, VSW[k0:k0 + TT, 256 + kh * 128:256 + (kh + 1) * 128]
                            .rearrange("(kb p) n -> p kb n", p=128), [dres("VSW")], [rvt])
                        state["cur"] = (kt, rkt, vt, rvt)
                    kt, rkt, vt, rvt = state["cur"]
                    return kt[:, kl * 128:(kl + 1) * 128], rkt, None, None

                def V_fn(kb, state=state):
                    kt, rkt, vt, rvt = state["cur"]
                    return vt[:, kb % 4, :], rvt

                def M_fn(kb, e_, r_e, si=si, kb_lo=kb_lo):
                    kbz = kb + kb_lo - 32 * si
                    if kbz >= 0:
                        causal_mask(kbz, e_, r_e)
                    wcol = 48 + kbz + 4
                    v_stt("dve", e_, iota_pf[:], thr[:, wcol:wcol + 1], e_, ALU.is_gt, ALU.mult,
                          [r_iota, r_thr, r_e], [r_e])

                flash_branch(A, K_fn, nkb, qs[:, hq, :], r_qs, None, V_fn, M_fn, NSA_SCALE, po, rpo, psm, rpsm)
                finish_branch(po, rpo, psm, rpsm, 2 * 8 + hq, onsa, r_onsa, True)
                v_copy("dve", mixT[:, 8 + hq, :], onsa, [r_onsa], [r_mixT])
        cx.barrier()

        if si == 0:
            dbg("mix_all", mixT, [r_mixT])
            dbg("kcT", kcT[:], [r_kcT]); dbg("vcc", vcc[:], [r_vcc])
            dbg("KN0", KN[:, :, 0:512].rearrange("h p t -> p h t"), [dres("KN")])
            dbg("KR0", KR[:, 0:512], [dres("KR")])
            dbg("KS0", KS[:, :, 0:512].rearrange("h p t -> p h t"), [dres("KS")])
            dbg("KW0", KW[:, :, 0:512].rearrange("h p t -> p h t"), [dres("KW")])
            dbg("KC0", KC[:, :, 0:512].rearrange("h p t -> p h t"), [dres("KC")])
            dbg("VC0", VC[:, :, 0:512].rearrange("h p t -> p h t"), [dres("VC")])
            dbg("VM0", VM[0:512, :].rearrange("(tb p) n -> p tb n", p=128), [dres("VM")])
            dbg("VSW0", VSW[0:512, :].rearrange("(tb p) n -> p tb n", p=128), [dres("VSW")])
        if STOP == 6:
            return nc, es, cx
        A.off = mark_q
        for tb in range(4):
            act(y_t[:, tb, :], y_t[:, tb, :], AF.Copy, [r_y], [r_y], scale=ALPHA)
        wo_t = [A.alloc([16, 512], BF16) for _ in range(2)]
        for ng in range(4):
            wo, rwo = wo_t[ng % 2]
            dma("sp", wo, wb["w_out"][:, ng * 512:(ng + 1) * 512].rearrange("(kc p) n -> p kc n", p=128),
                [dres("b_w_out")], [rwo])
            for tb in range(4):
                ps, rps = psum[tb + 4 * (ng % 2)]
                for kc in range(16):
                    mm(ps[:], mixT[:, kc, tb * 128:(tb + 1) * 128], wo[:, kc, :], kc == 0, kc == 15, [r_mixT, rwo], [rps])
                v_tt("dve", y_t[:, tb, ng * 512:(ng + 1) * 512], ps[:], y_t[:, tb, ng * 512:(ng + 1) * 512], ALU.add,
                     [rps, r_y], [r_y])
        layer_norm(A, y_t, r_y, ln_in["ln2_g"], ln_in["ln2_b"])
        if si == 0:
            dbg("x2", y_t, [r_y])
        cx.barrier()
        A.off = 4 * D * 2
        x2T, r_x2T = A.alloc([16, TT], BF16)
        transpose_to_bf16(y_t, r_y, x2T, r_x2T)
        ffn_ln(A, x2T, None, "ffn2", ln_in["ln3_g"], ln_in["ln3_b"], y_t, r_y, xT_res=r_x2T)
        dma("sp", out[m0:m0 + TT, :].rearrange("(tb p) d -> p tb d", p=128), y_t, [r_y], [dres("out")])
        cx.barrier()

    cx.barrier()
    return nc, es, cx


def emit_program():
    nc, es, cx = build_program()
    with nc.Block() as block:
        @block.tensor
        def _(e):
            cx.replay("pe", e)

        @block.scalar
        def _(e):
            cx.replay("act", e)

        @block.vector
        def _(e):
            cx.replay("dve", e)

        @block.gpsimd
        def _(e):
            cx.replay("pool", e)

        @block.sync
        def _(e):
            cx.replay("sp", e)
    return nc


def _rope_tables(pos, d):
    half = d // 2
    inv = (np.float32(10000.0) ** (-np.arange(half, dtype=np.float32) * np.float32(2.0 / d))).astype(np.float32)
    ang = pos.astype(np.float32)[None, :] * inv[:, None]
    cos, sin = np.cos(ang).astype(np.float32), np.sin(ang).astype(np.float32)
    t = np.empty((d, 2, pos.shape[0]), np.float32)
    t[:half, 0], t[half:, 0] = cos, cos
    t[:half, 1], t[half:, 1] = -sin, sin
    return t


def _swap_halves(w):
    h = w.shape[1] // 2
    return np.concatenate([w[:, h:], w[:, :h]], axis=1)


def kernel(**inp):
    f = lambda k: np.ascontiguousarray(np.asarray(inp[k], dtype=np.float32)[0])
    x = f("x")
    w_in = f("w_in")
    c_q, c_kv, k_rope = w_in[:, 0:512], w_in[:, 512:768], w_in[:, 768:832]
    q_nsa = w_in[:, 832:1856]
    k_cmp, v_cmp = w_in[:, 1856:2112], w_in[:, 2112:2368]
    k_sel, v_sel = w_in[:, 2368:2624], w_in[:, 2624:2880]
    k_win, v_win = w_in[:, 2880:3136], w_in[:, 3136:3392]
    gl = w_in[:, 3392:3416]
    cols = [c_kv, k_rope, _swap_halves(k_rope)]
    for kk in (k_cmp, k_sel, k_win):
        for kh in range(2):
            blk = kk[:, kh * 128:(kh + 1) * 128]
            cols += [blk, _swap_halves(blk)]
    cols.append(v_cmp)
    w_kvf = np.ascontiguousarray(np.concatenate(cols, axis=1))
    assert w_kvf.shape[1] == NKV_F
    w_kvt = np.ascontiguousarray(np.concatenate([v_sel, v_win], axis=1))
    qcols = [c_q]
    for h in range(8):
        blk = q_nsa[:, h * 128:(h + 1) * 128]
        qcols += [blk, _swap_halves(blk)]
    qcols.append(gl)
    w_q = np.ascontiguousarray(np.concatenate(qcols, axis=1))
    wuq = f("mla_w_uq")
    ucols = []
    for h in range(8):
        blk = wuq[:, h * 192:(h + 1) * 192]
        ucols += [blk[:, :128], blk[:, 128:], _swap_halves(blk[:, 128:])]
    w_uq = np.ascontiguousarray(np.concatenate(ucols, axis=1))
    shared = {
        "x_all": x, "xT_all": np.ascontiguousarray(x.T),
        "ffn1_w_gate": f("ffn1_w_gate"), "ffn1_w_up": f("ffn1_w_up"), "ffn1_w_down": f("ffn1_w_down"),
        "ffn2_w_gate": f("ffn2_w_gate"), "ffn2_w_up": f("ffn2_w_up"), "ffn2_w_down": f("ffn2_w_down"),
        "w_kvf": w_kvf, "w_kvt": w_kvt, "w_q": w_q, "w_uq": w_uq, "w_ukv": f("mla_w_ukv"), "w_out": f("w_out"),
        "w1_k": f("nsa_cmp_w1_k"), "w1_v": f("nsa_cmp_w1_v"), "w2_k": f("nsa_cmp_w2_k"), "w2_v": f("nsa_cmp_w2_v"),
        "qg": np.ascontiguousarray(f("mla_q_norm_g").reshape(4, 128).T),
        "kvg": np.ascontiguousarray(f("mla_kv_norm_g").reshape(2, 128).T),
        "gateb": np.ascontiguousarray(f("nsa_gate_b").reshape(24, 1)),
        "peT_k": np.ascontiguousarray(f("nsa_cmp_pe_k").T), "peT_v": np.ascontiguousarray(f("nsa_cmp_pe_v").T),
        "ident": np.eye(128, dtype=np.float32),
    }
    for k in ("ln1_g", "ln1_b", "ln2_g", "ln2_b", "ln3_g", "ln3_b"):
        shared[k] = np.ascontiguousarray(np.asarray(inp[k], np.float32).reshape(1, D))
    pos_all = np.arange(S)
    shared["cs64_all"] = _rope_tables(pos_all, 64)
    shared["cs128_all"] = _rope_tables(pos_all, 128)
    ee = np.zeros((128, 64, 128), np.float32)
    for kbl in range(64):
        ee[2 * kbl, kbl, :64] = 1.0
        ee[2 * kbl + 1, kbl, 64:] = 1.0
    shared["eexp"] = ee.reshape(128, 64 * 128)
    ci = np.arange(1024)[:, None] * 16
    sj = np.arange(256)[None, :] * 64
    ovm = np.clip(np.minimum(ci + 32, sj + 64) - np.maximum(ci, sj), 0, None).astype(np.float32) / 16.0
    ovm[1023, :] = 0.0
    shared["ov"] = np.ascontiguousarray(ovm.reshape(8, 128, 256).transpose(1, 0, 2).reshape(128, 8 * 256))

    in_maps = []
    tok_idx = []
    for c in range(NCORE):
        idx = np.concatenate([np.arange((8 * i + c) * TT, (8 * i + c + 1) * TT) for i in range(NSLOT)])
        tok_idx.append(idx)
        m = dict(shared)
        m["x_my"] = np.ascontiguousarray(x[idx])
        m["xT_my"] = np.ascontiguousarray(x[idx].T)
        m["cs64_my"] = _rope_tables(idx, 64)
        m["cs128_my"] = _rope_tables(idx, 128)
        tq = idx[:, None]
        bj = np.arange(256)[None, :]
        cur = tq // 64
        valid = bj * 64 <= tq
        forced = (bj == 0) | (bj == cur) | (bj == cur - 1)
        sb = np.where(valid, np.where(forced, np.float32(1e4), np.float32(0.0)), np.float32(-1e30)).astype(np.float32)
        m["sbias"] = np.ascontiguousarray(sb.reshape(16, 128, 256).transpose(1, 0, 2))
        th = np.zeros((96,), np.float32)
        for kbz in range(-4, 32):
            th[kbz + 4] = 512 * c - 128 * kbz
            th[48 + kbz + 4] = 512 * c - 128 * kbz - 512
        for i in range(NSLOT):
            for r in range(3):
                jb = 2 * i - 1 + r
                th[36 + 3 * i + r] = 4096 * i + 512 * c - 31 - 2048 * jb
        m["thr"] = np.ascontiguousarray(np.broadcast_to(th[None, :], (128, 96)))
        in_maps.append(m)

    nc = emit_program()
    if os.environ.get('KTRACE'):
        res = run_bass_kernel_spmd(nc, in_maps, core_ids=list(range(NCORE)), trace=True)
    else:
        res = run_bass_kernel_spmd(nc, in_maps, core_ids=list(range(NCORE)))
    LAST["res"] = res
    out = np.empty((1, S, D), np.float32)
    for c in range(NCORE):
        out[0, tok_idx[c]] = res.results[c]["out"]
    return out
```

```python
import math
import os
STOP = int(os.environ.get('KSTOP', '99'))
NT1 = int(os.environ.get('KNT1', '32'))
KSLOTS = int(os.environ.get('KSLOTS', '4'))
KDBG = int(os.environ.get('KDBG', '0'))
DBG_LAYOUT = {}
LAST = {}
from contextlib import ExitStack
import numpy as np
import concourse.bass as bass
import concourse.mybir as mybir
from concourse.bass_utils import run_bass_kernel_spmd

F32 = mybir.dt.float32
BF16 = mybir.dt.bfloat16
AF = mybir.ActivationFunctionType
ALU = mybir.AluOpType
AX = mybir.AxisListType

S = 16384
D = 2048
DFF = 5632
NCORE = 8
TT = 512
NT_ALL = S // TT
NSLOT = 4
ALPHA = 2.0 ** 0.25
LN_EPS = 1e-5
RMS_EPS = 1e-6
NKV_F = 2176
NKV_T = 512
NQ = 512 + 2048 + 24
MLA_SCALE = 192.0 ** -0.5
NSA_SCALE = 128.0 ** -0.5


class Res:
    __slots__ = ("w", "r")

    def __init__(self):
        self.w = None
        self.r = {}


class Ctx:
    ENG = ("pe", "act", "dve", "pool", "sp")

    def __init__(self, nc, es):
        self.nc = nc
        self.ops = {e: [] for e in self.ENG}
        self.seq = {e: 0 for e in self.ENG}
        self.known = {e: {} for e in self.ENG}
        self.sems = {}
        for e in ("pe", "act", "dve", "pool"):
            self.sems[e] = es.enter_context(nc.semaphore("S_" + e))
        self.dmak = {"sp": 20, "pool": 6, "act": 6}
        self.dman = {q: 0 for q in self.dmak}
        for q, k in self.dmak.items():
            for i in range(k):
                self.sems[(q, i)] = es.enter_context(nc.semaphore("D_%s_%d" % (q, i)))
        self.last = {}

    def _need(self, eng, tok, waits, war=False):
        if tok is None:
            return
        key, val, teng = tok
        if teng == eng and not isinstance(key, tuple):
            if eng == "pe" or war:
                return
        if self.known[eng].get(key, 0) >= val:
            return
        self.known[eng][key] = val
        waits.append((key, val))

    def op(self, eng, fn, reads=(), writes=(), dma=False):
        waits = []
        for r in reads:
            self._need(eng, r.w, waits)
        for w in writes:
            self._need(eng, w.w, waits)
            for key, (val, teng) in w.r.items():
                self._need(eng, (key, val, teng), waits, war=True)
        if dma:
            k = self.dmak[eng]
            n = self.dman[eng]
            self.dman[eng] = n + 1
            key = (eng, n % k)
            val = 16 * (n // k + 1)
            if n >= k:
                self._need(eng, (key, val - 16, eng), waits)
            tok = (key, val, eng)
            inc = (key, 16)
        else:
            self.seq[eng] += 1
            tok = (eng, self.seq[eng], eng)
            inc = (eng, 1)
        self.ops[eng].append((waits, fn, inc))
        self.last[tok[0]] = tok
        for r in reads:
            r.r[tok[0]] = (tok[1], tok[2])
        for w in writes:
            w.w = tok
            w.r = {}
        return tok

    def barrier(self):
        toks = list(self.last.values())
        for e in self.ENG:
            waits = []
            for t in toks:
                self._need(e, (t[0], t[1], "x"), waits)
            if waits:
                self.ops[e].append((waits, None, None))

    def replay(self, eng, e):
        for waits, fn, inc in self.ops[eng]:
            for key, val in waits:
                e.wait_ge(self.sems[key], val)
            if fn is not None:
                ins = fn(e)
                ins.then_inc(self.sems[inc[0]], inc[1])


class Arena:
    def __init__(self, ap, nelem):
        self.ap = ap
        self.n = nelem
        self.off = 0

    def reset(self):
        self.off = 0

    def alloc(self, free_shape, dtype, parts=128):
        n = int(np.prod(free_shape))
        nb = n * (2 if dtype == F32 else 1)
        nb = (nb + 15) // 16 * 16
        assert self.off + nb <= self.n, ("arena overflow", self.off, nb, self.n)
        v = self.ap[0:parts, self.off:self.off + nb]
        self.off += nb
        if dtype == F32:
            v = v.bitcast(F32)
        v = v[:, 0:n]
        if len(free_shape) == 2:
            v = v.rearrange("p (a b) -> p a b", b=free_shape[1])
        elif len(free_shape) == 3:
            v = v.rearrange("p (a b c) -> p a b c", b=free_shape[1], c=free_shape[2])
        return v, Res()


def build_program():
    nc = bass.Bass("TRN2", target_bir_lowering=False)
    es = ExitStack()

    def din(name, shape, dt=F32):
        return nc.dram_tensor(name, list(shape), dt, kind="ExternalInput").ap()

    def dscr(name, shape, dt=BF16):
        return nc.dram_tensor(name, list(shape), dt).ap()

    x_all = din("x_all", [S, D])
    xT_all = din("xT_all", [D, S])
    x_my = din("x_my", [NSLOT * TT, D])
    xT_my = din("xT_my", [D, NSLOT * TT])
    wsrc = {}
    for nm in ("ffn1_w_gate", "ffn1_w_up", "ffn2_w_gate", "ffn2_w_up"):
        wsrc[nm] = din(nm, [D, DFF])
    for nm in ("ffn1_w_down", "ffn2_w_down"):
        wsrc[nm] = din(nm, [DFF, D])
    wsrc["w_kvf"] = din("w_kvf", [D, NKV_F])
    wsrc["w_kvt"] = din("w_kvt", [D, NKV_T])
    wsrc["w_q"] = din("w_q", [D, NQ])
    wsrc["w_uq"] = din("w_uq", [512, 2048])
    wsrc["w_ukv"] = din("w_ukv", [256, 2048])
    wsrc["w_out"] = din("w_out", [D, D])
    wsrc["w1_k"] = din("w1_k", [4096, 256])
    wsrc["w1_v"] = din("w1_v", [4096, 256])
    wsrc["w2_k"] = din("w2_k", [256, 128])
    wsrc["w2_v"] = din("w2_v", [256, 128])
    ln_in = {k: din(k, [1, D]) for k in ("ln1_g", "ln1_b", "ln2_g", "ln2_b", "ln3_g", "ln3_b")}
    qg_in = din("qg", [128, 4])
    kvg_in = din("kvg", [128, 2])
    gateb_in = din("gateb", [24, 1])
    peT_k_in = din("peT_k", [128, 32])
    peT_v_in = din("peT_v", [128, 32])
    cs64_all = din("cs64_all", [64, 2, S])
    cs128_all = din("cs128_all", [128, 2, S])
    cs64_my = din("cs64_my", [64, 2, NSLOT * TT])
    cs128_my = din("cs128_my", [128, 2, NSLOT * TT])
    sbias_in = din("sbias", [128, 16, 256])
    thr_in = din("thr", [128, 96])
    ident_in = din("ident", [128, 128])
    eexp_in = din("eexp", [128, 64 * 128])
    ov_in = din("ov", [128, 8 * 256])
    out = nc.dram_tensor("out", [NSLOT * TT, D], F32, kind="ExternalOutput").ap()

    wb = {nm: dscr("b_" + nm, ap.shape) for nm, ap in wsrc.items()}
    KN = dscr("KN", [8, 128, S])
    KR = dscr("KR", [64, S])
    VM = dscr("VM", [S, 1024])
    KS = dscr("KS", [2, 128, S])
    KW = dscr("KW", [2, 128, S])
    KC = dscr("KC", [2, 128, S])
    VC = dscr("VC", [2, 128, S])
    VSW = dscr("VSW", [S, 512])

    ARENA_N = 80 * 1024
    arena_t = es.enter_context(nc.sbuf_tensor("arena", [128, ARENA_N], BF16))
    A = Arena(arena_t, ARENA_N)

    def pers(name, shape, dt):
        t = es.enter_context(nc.sbuf_tensor("s_" + name, list(shape), dt))
        return t, Res()

    ident_f, r_identf = pers("ident_f", [128, 128], F32)
    ident_b, r_identb = pers("ident_b", [128, 128], BF16)
    ones_b, r_ones = pers("ones_b", [128, 128], BF16)
    eexp, r_eexp = pers("eexp", [128, 64 * 128], BF16)
    ovb, r_ov = pers("ovb", [128, 8 * 256], BF16)
    ones_f, r_onesf = pers("ones_f", [24, 128], F32)
    thr, r_thr = pers("thr", [128, 96], F32)
    iota_pf, r_iota = pers("iota_pf", [128, 512], F32)
    iota_16, r_iota16 = pers("iota_16", [128, 512], F32)
    qg, r_qg = pers("qg", [128, 4], F32)
    kvg, r_kvg = pers("kvg", [128, 2], F32)
    gateb, r_gateb = pers("gateb", [24, 1], F32)
    kcT, r_kcT = pers("kcT", [128, 2, 1024], BF16)
    vcc, r_vcc = pers("vcc", [128, 2, 8, 128], BF16)
    psum = []
    for i in range(8):
        t = es.enter_context(nc.psum_tensor("ps%d" % i, [128, 512], F32))
        psum.append((t, Res()))

    cx = Ctx(nc, es)
    rD = {}
    dbg_t = {}
    dbg_off = {"f": 0, "b": 0}
    if KDBG:
        dbg_t["f"] = nc.dram_tensor("dbg_f", [128, 40960], F32, kind="ExternalOutput").ap()
        dbg_t["b"] = nc.dram_tensor("dbg_b", [128, 81920], BF16, kind="ExternalOutput").ap()

    def dbg(name, ap, reads, parts=128):
        if not KDBG:
            return
        k = "f" if ap.dtype == F32 else "b"
        shp = list(ap.shape)
        n = int(np.prod(shp[1:]))
        off = dbg_off[k]
        dbg_off[k] = off + n
        DBG_LAYOUT[name] = (k, off, shp)
        dst = dbg_t[k][0:shp[0], off:off + n]
        if len(shp) == 3:
            dst = dst.rearrange("p (a b) -> p a b", b=shp[2])
        elif len(shp) == 4:
            dst = dst.rearrange("p (a b c) -> p a b c", b=shp[2], c=shp[3])
        cx.op("sp", lambda e: e.dma_start(out=dst, in_=ap), reads, [dres("dbg")], dma=True)

    def dres(name):
        if name not in rD:
            rD[name] = Res()
        return rD[name]

    def dma(q, out_ap, in_ap, reads, writes):
        return cx.op(q, lambda e: e.dma_start(out=out_ap, in_=in_ap), reads, writes, dma=True)

    def mm(ps, lhsT, rhs, start, stop, reads, writes):
        return cx.op("pe", lambda e: e.matmul(ps, lhsT, rhs, start=start, stop=stop), reads, writes)

    def tr(ps, in_, idt, reads, writes):
        return cx.op("pe", lambda e: e.transpose(ps, in_, idt), reads, writes)

    def act(out_ap, in_ap, func, reads, writes, scale=1.0, bias=0.0):
        return cx.op("act", lambda e: e.activation(out=out_ap, in_=in_ap, func=func, scale=scale, bias=bias),
                     reads, writes)

    def v_tt(eng, out_ap, a, b, op, reads, writes):
        return cx.op(eng, lambda e: e.tensor_tensor(out=out_ap, in0=a, in1=b, op=op), reads, writes)

    def v_ts(eng, out_ap, a, s1, s2, op0, op1, reads, writes):
        if s2 is None:
            return cx.op(eng, lambda e: e.tensor_scalar(out=out_ap, in0=a, scalar1=s1, scalar2=None, op0=op0),
                         reads, writes)
        return cx.op(eng, lambda e: e.tensor_scalar(out=out_ap, in0=a, scalar1=s1, scalar2=s2, op0=op0, op1=op1),
                     reads, writes)

    def v_stt(eng, out_ap, a, s, b, op0, op1, reads, writes):
        return cx.op(eng, lambda e: e.scalar_tensor_tensor(out=out_ap, in0=a, scalar=s, in1=b, op0=op0, op1=op1),
                     reads, writes)

    def v_copy(eng, out_ap, in_ap, reads, writes):
        return cx.op(eng, lambda e: e.tensor_copy(out=out_ap, in_=in_ap), reads, writes)

    cast_rr = [0]

    def cast(out_ap, in_ap, reads, writes):
        cast_rr[0] += 1
        if cast_rr[0] % 2:
            return act(out_ap, in_ap, AF.Copy, reads, writes)
        return v_copy("dve", out_ap, in_ap, reads, writes)

    A.reset()
    dma("sp", ident_f[:], ident_in, [], [r_identf])
    v_copy("dve", ident_b[:], ident_f[:], [r_identf], [r_identb])
    cx.op("pool", lambda e: e.memset(ones_b[:], 1.0), [], [r_ones])
    cx.op("pool", lambda e: e.memset(kcT[:], 0.0), [], [r_kcT])
    cx.op("pool", lambda e: e.memset(vcc[:], 0.0), [], [r_vcc])
    cx.op("pool", lambda e: e.memset(ones_f[:], 1.0), [], [r_onesf])
    dma("sp", thr[:], thr_in, [], [r_thr])
    dma("sp", qg[:], qg_in, [], [r_qg])
    dma("sp", kvg[:], kvg_in, [], [r_kvg])
    dma("sp", gateb[:], gateb_in, [], [r_gateb])
    cx.op("pool", lambda e: e.iota(iota_pf[:], [[-1, 512]], base=0, channel_multiplier=1,
                                   allow_small_or_imprecise_dtypes=True), [], [r_iota])
    cx.op("pool", lambda e: e.iota(iota_16[:], [[-1, 512]], base=0, channel_multiplier=16,
                                   allow_small_or_imprecise_dtypes=True), [], [r_iota16])
    st0, r_st0 = A.alloc([64 * 128], F32)
    dma("sp", st0, eexp_in, [], [r_st0])
    cast(eexp[:], st0, [r_st0], [r_eexp])
    st1, r_st1 = A.alloc([8 * 256], F32)
    dma("sp", st1, ov_in, [], [r_st1])
    cast(ovb[:], st1, [r_st1], [r_ov])

    CW = 4096
    stg = [A.alloc([CW], F32) for _ in range(3)]
    stb = [A.alloc([CW], BF16) for _ in range(3)]
    ci = 0
    for nm, src in wsrc.items():
        K_, N_ = src.shape
        flat_s = src.rearrange("k n -> (k n)")
        flat_d = wb[nm].rearrange("k n -> (k n)")
        tot = K_ * N_
        per = 128 * CW
        o = 0
        while o < tot:
            n = min(per, tot - o)
            w_ = n // 128
            assert w_ * 128 == n, (nm, n)
            (sf, rsf), (sb_, rsb) = stg[ci % 3], stb[ci % 3]
            ci += 1
            dma("sp", sf[:, 0:w_], flat_s[o:o + n].rearrange("(p w) -> p w", p=128), [], [rsf])
            cast(sb_[:, 0:w_], sf[:, 0:w_], [rsf], [rsb])
            dma("sp", flat_d[o:o + n].rearrange("(p w) -> p w", p=128), sb_[:, 0:w_], [rsb], [dres("b_" + nm)])
            o += n
    cx.barrier()
    if STOP == 0:
        return nc, es, cx

    def ffn_ln(A, xT_src, x_src, pre, lng, lnb, y_t, r_y, xT_res=None):
        Wg, Wu, Wd = wb[pre + "_w_gate"], wb[pre + "_w_up"], wb[pre + "_w_down"]
        rWg, rWu, rWd = dres("b_" + pre + "_w_gate"), dres("b_" + pre + "_w_up"), dres("b_" + pre + "_w_down")
        if xT_res is None:
            xTb, r_xTb = A.alloc([16, TT], BF16)
            sx = [A.alloc([2, TT], F32) for _ in range(2)]
            for c4 in range(8):
                sf, rsf = sx[c4 % 2]
                dma("sp", sf, xT_src[c4 * 256:(c4 + 1) * 256, :].rearrange("(kc p) t -> p kc t", p=128), [], [rsf])
                cast(xTb[:, c4 * 2:(c4 + 1) * 2, :], sf, [rsf], [r_xTb])
        else:
            xTb, r_xTb = xT_src, xT_res
        if x_src is not None:
            dma("sp", y_t, x_src.rearrange("(tb p) d -> p tb d", p=128), [], [r_y])
        ln_mark = A.off
        hT, r_hT = A.alloc([44, TT], BF16)
        GW = 256
        wg_t = [A.alloc([16, GW], BF16) for _ in range(2)]
        wu_t = [A.alloc([16, GW], BF16) for _ in range(2)]
        sg_t = [A.alloc([TT], F32) for _ in range(2)]
        for fg in range(DFF // GW):
            (wg, rwg), (wu, rwu) = wg_t[fg % 2], wu_t[fg % 2]
            dma("sp", wg, Wg[:, fg * GW:(fg + 1) * GW].rearrange("(kc p) n -> p kc n", p=128), [rWg], [rwg])
            dma("sp", wu, Wu[:, fg * GW:(fg + 1) * GW].rearrange("(kc p) n -> p kc n", p=128), [rWu], [rwu])
            for j in range(GW // 128):
                m = fg * (GW // 128) + j
                (pg, rpg), (pu, rpu) = psum[(2 * m) % 8], psum[(2 * m + 1) % 8]
                for kc in range(16):
                    mm(pg[:], wg[:, kc, j * 128:(j + 1) * 128], xTb[:, kc, :], kc == 0, kc == 15, [rwg, r_xTb], [rpg])
                for kc in range(16):
                    mm(pu[:], wu[:, kc, j * 128:(j + 1) * 128], xTb[:, kc, :], kc == 0, kc == 15, [rwu, r_xTb], [rpu])
                sg, rsg = sg_t[m % 2]
                act(sg, pg[:], AF.Silu, [rpg], [rsg])
                v_tt("dve", hT[:, m, :], sg, pu[:], ALU.mult, [rsg, rpu], [r_hT])
        for tb in range(4):
            act(y_t[:, tb, :], y_t[:, tb, :], AF.Copy, [r_y], [r_y], scale=ALPHA)
        KG = 4
        wd_t = [A.alloc([KG, 512], BF16) for _ in range(2)]
        wi = 0
        for ng in range(4):
            for kg in range(11):
                wd, rwd = wd_t[wi % 2]
                wi += 1
                dma("sp", wd, Wd[kg * KG * 128:(kg + 1) * KG * 128, ng * 512:(ng + 1) * 512]
                    .rearrange("(kc p) n -> p kc n", p=128), [rWd], [rwd])
                for tb in range(4):
                    ps, rps = psum[tb + 4 * (ng % 2)]
                    for kl in range(KG):
                        kc = kg * KG + kl
                        mm(ps[:], hT[:, kc, tb * 128:(tb + 1) * 128], wd[:, kl, :], kc == 0, kc == 43,
                           [r_hT, rwd], [rps])
            for tb in range(4):
                ps, rps = psum[tb + 4 * (ng % 2)]
                v_stt("dve", y_t[:, tb, ng * 512:(ng + 1) * 512], ps[:], 0.5, y_t[:, tb, ng * 512:(ng + 1) * 512],
                      ALU.mult, ALU.add, [rps, r_y], [r_y])
        cx.barrier()
        A.off = ln_mark
        layer_norm(A, y_t, r_y, lng, lnb)

    def layer_norm(A, y_t, r_y, lng, lnb):
        gb, r_gb = A.alloc([2, D], F32)
        dma("sp", gb[:, 0, :], lng.partition_broadcast(128), [], [r_gb])
        dma("sp", gb[:, 1, :], lnb.partition_broadcast(128), [], [r_gb])
        st, r_st = A.alloc([4, 4, 6], F32)
        mv, r_mv = A.alloc([4, 2], F32)
        rs, r_rs = A.alloc([4, 1], F32)
        for tb in range(4):
            for c in range(4):
                cx.op("dve", lambda e, tb=tb, c=c: e.bn_stats(out=st[:, tb, c, :], in_=y_t[:, tb, c * 512:(c + 1) * 512]),
                      [r_y], [r_st])
            cx.op("dve", lambda e, tb=tb: e.bn_aggr(out=mv[:, tb, :], in_=st[:, tb, :, :]), [r_st], [r_mv])
            act(rs[:, tb, :], mv[:, tb, 1:2], AF.Sqrt, [r_mv], [r_rs], scale=1.0, bias=LN_EPS)
            cx.op("dve", lambda e, tb=tb: e.reciprocal(out=rs[:, tb, :], in_=rs[:, tb, :]), [r_rs], [r_rs])
            v_ts("dve", y_t[:, tb, :], y_t[:, tb, :], mv[:, tb, 0:1], rs[:, tb, 0:1], ALU.subtract, ALU.mult,
                 [r_y, r_mv, r_rs], [r_y])
            v_tt("pool", y_t[:, tb, :], y_t[:, tb, :], gb[:, 0, :], ALU.mult, [r_y, r_gb], [r_y])
            v_tt("dve", y_t[:, tb, :], y_t[:, tb, :], gb[:, 1, :], ALU.add, [r_y, r_gb], [r_y])

    def transpose_to_bf16(y_t, r_y, xT, r_xT):
        for fc in range(16):
            ps, rps = psum[fc % 8]
            for tb in range(4):
                tr(ps[:, tb * 128:(tb + 1) * 128], y_t[:, tb, fc * 128:(fc + 1) * 128], ident_f[:],
                   [r_y, r_identf], [rps])
            cast(xT[:, fc, :], ps[:], [rps], [r_xT])

    def rms_scale(A, srcT, r_src, nchunk, width, gvec, r_g, dstT, r_dst):
        sq, r_sq = A.alloc([nchunk, TT], BF16)
        for c in range(nchunk):
            act(sq[:, c, :], srcT[:, c, :], AF.Square, [r_src], [r_sq])
        ps, rps = psum[7]
        for c in range(nchunk):
            mm(ps[:], ones_b[:], sq[:, c, :], c == 0, c == nchunk - 1, [r_ones, r_sq], [rps])
        rr, r_rr = A.alloc([TT], F32)
        act(rr, ps[:], AF.Sqrt, [rps], [r_rr], scale=1.0 / width, bias=RMS_EPS)
        cx.op("dve", lambda e: e.reciprocal(out=rr, in_=rr), [r_rr], [r_rr])
        for c in range(nchunk):
            v_stt("dve", dstT[:, c, :], srcT[:, c, :], gvec[:, c:c + 1], rr, ALU.mult, ALU.mult,
                  [r_src, r_g, r_rr], [r_dst])

    def rope_comb(eng, dst, xa, xb, cs, npart, reads, writes, tmp):
        v_tt("dve", tmp[0:npart, :], xb, cs[0:npart, 1, :], ALU.mult, reads, [writes[1]])
        v_tt("dve", dst, xa, cs[0:npart, 0, :], ALU.mult, reads, [writes[0]])
        v_tt(eng, dst, dst, tmp[0:npart, :], ALU.add, [writes[0], writes[1]], [writes[0]])

    for t in range(min(NT_ALL, NT1)):
        A.reset()
        t0 = t * TT
        y_t, r_y = A.alloc([4, D], F32)
        ffn_ln(A, xT_all[:, t0:t0 + TT], x_all[t0:t0 + TT, :], "ffn1", ln_in["ln1_g"], ln_in["ln1_b"], y_t, r_y)
        cx.barrier()
        A.off = 4 * D * 2
        x1T, r_x1T = A.alloc([16, TT], BF16)
        transpose_to_bf16(y_t, r_y, x1T, r_x1T)
        cs64, r_cs64 = A.alloc([2, TT], F32, parts=64)
        cs128, r_cs128 = A.alloc([2, TT], F32)
        dma("sp", cs64, cs64_all[:, :, t0:t0 + TT], [], [r_cs64])
        dma("sp", cs128, cs128_all[:, :, t0:t0 + TT], [], [r_cs128])
        wkv_t = [A.alloc([16, 256], BF16) for _ in range(2)]
        tmpr, r_tmpr = A.alloc([TT], F32)
        ob_t = [A.alloc([TT], BF16) for _ in range(2)]
        ckvT, r_ckvT = A.alloc([2, TT], F32)
        rW = dres("b_w_kvf")
        oi = 0
        def load_w(col0, ncol, idx):
            w, rw = wkv_t[idx % 2]
            dma("sp", w[:, :, 0:ncol], wb["w_kvf"][:, col0:col0 + ncol].rearrange("(kc p) n -> p kc n", p=128),
                [rW], [rw])
            return w, rw
        wi = 0
        w, rw = load_w(0, 256, wi); wi += 1
        for c in range(2):
            ps, rps = psum[c]
            for kc in range(16):
                mm(ps[:], w[:, kc, c * 128:(c + 1) * 128], x1T[:, kc, :], kc == 0, kc == 15, [rw, r_x1T], [rps])
            act(ckvT[:, c, :], ps[:], AF.Copy, [rps], [r_ckvT])
        w, rw = load_w(256, 128, wi); wi += 1
        psa, rpsa = psum[2]
        psb, rpsb = psum[3]
        for kc in range(16):
            mm(psa[0:64, :], w[:, kc, 0:64], x1T[:, kc, :], kc == 0, kc == 15, [rw, r_x1T], [rpsa])
        for kc in range(16):
            mm(psb[0:64, :], w[:, kc, 64:128], x1T[:, kc, :], kc == 0, kc == 15, [rw, r_x1T], [rpsb])
        ob, rob = ob_t[oi % 2]; oi += 1
        rope_comb("dve", ob[0:64, :], psa[0:64, :], psb[0:64, :], cs64, 64, [rpsa, rpsb, r_cs64], [rob, r_tmpr], tmpr)
        dma("sp", KR[:, t0:t0 + TT], ob[0:64, :], [rob], [dres("KR")])
        for gi, (dst, nm) in enumerate(((KC, "KC"), (KS, "KS"), (KW, "KW"))):
            for kh in range(2):
                w, rw = load_w(384 + 256 * (gi * 2 + kh), 256, wi); wi += 1
                psa, rpsa = psum[(2 * wi) % 8]
                psb, rpsb = psum[(2 * wi + 1) % 8]
                for kc in range(16):
                    mm(psa[:], w[:, kc, 0:128], x1T[:, kc, :], kc == 0, kc == 15, [rw, r_x1T], [rpsa])
                for kc in range(16):
                    mm(psb[:], w[:, kc, 128:256], x1T[:, kc, :], kc == 0, kc == 15, [rw, r_x1T], [rpsb])
                ob, rob = ob_t[oi % 2]; oi += 1
                rope_comb("dve", ob, psa[:], psb[:], cs128, 128, [rpsa, rpsb, r_cs128], [rob, r_tmpr], tmpr)
                dma("sp", dst[kh, :, t0:t0 + TT], ob, [rob], [dres(nm)])
        w, rw = load_w(1920, 256, wi); wi += 1
        for kh in range(2):
            ps, rps = psum[4 + kh]
            for kc in range(16):
                mm(ps[:], w[:, kc, kh * 128:(kh + 1) * 128], x1T[:, kc, :], kc == 0, kc == 15, [rw, r_x1T], [rps])
            ob, rob = ob_t[oi % 2]; oi += 1
            cast(ob, ps[:], [rps], [rob])
            dma("sp", VC[kh, :, t0:t0 + TT], ob, [rob], [dres("VC")])
        wt_t = [A.alloc([16, 256], BF16) for _ in range(2)]
        vo, r_vo = A.alloc([4, 512], BF16)
        for hf in range(2):
            w, rw = wt_t[hf]
            dma("sp", w, wb["w_kvt"][:, hf * 256:(hf + 1) * 256].rearrange("(kc p) n -> p kc n", p=128),
                [dres("b_w_kvt")], [rw])
            for tb in range(4):
                ps, rps = psum[tb + 4 * hf]
                for kc in range(16):
                    mm(ps[:, 0:256], x1T[:, kc, tb * 128:(tb + 1) * 128], w[:, kc, :], kc == 0, kc == 15,
                       [rw, r_x1T], [rps])
                cast(vo[:, tb, hf * 256:(hf + 1) * 256], ps[:, 0:256], [rps], [r_vo])
        dma("sp", VSW[t0:t0 + TT, :].rearrange("(tb p) n -> p tb n", p=128), vo, [r_vo], [dres("VSW")])
        ckvn, r_ckvn = A.alloc([2, TT], BF16)
        rms_scale(A, ckvT, r_ckvT, 2, 256.0, kvg, r_kvg, ckvn, r_ckvn)
        wukv, r_wukv = A.alloc([2, 2048], BF16)
        dma("sp", wukv, wb["w_ukv"].rearrange("(kc p) n -> p kc n", p=128), [dres("b_w_ukv")], [r_wukv])
        for h in range(8):
            ps, rps = psum[h % 4]
            for kc in range(2):
                mm(ps[:], wukv[:, kc, h * 256:h * 256 + 128], ckvn[:, kc, :], kc == 0, kc == 1, [r_wukv, r_ckvn], [rps])
            ob, rob = ob_t[oi % 2]; oi += 1
            cast(ob, ps[:], [rps], [rob])
            dma("sp", KN[h, :, t0:t0 + TT], ob, [rob], [dres("KN")])
        vm, r_vm = A.alloc([4, 1024], BF16)
        for tb in range(4):
            for hg in range(2):
                ps, rps = psum[4 + (tb * 2 + hg) % 4]
                for kc in range(2):
                    rhs = wukv[:, kc, hg * 1024:(hg + 1) * 1024].rearrange("p (h c) -> p h c", c=256)[:, :, 128:256]
                    mm(ps[:].rearrange("p (h c) -> p h c", c=128), ckvn[:, kc, tb * 128:(tb + 1) * 128], rhs,
                       kc == 0, kc == 1, [r_wukv, r_ckvn], [rps])
                cast(vm[:, tb, hg * 512:(hg + 1) * 512], ps[:], [rps], [r_vm])
        dma("sp", VM[t0:t0 + TT, :].rearrange("(tb p) n -> p tb n", p=128), vm, [r_vm], [dres("VM")])
        cx.barrier()
    cx.barrier()

    if STOP == 1:
        return nc, es, cx
    A.reset()
    w1, r_w1 = A.alloc([32, 256], BF16)
    w2, r_w2 = A.alloc([2, 128], BF16)
    pef, r_pef = A.alloc([32], F32)
    peb, r_peb = A.alloc([32], BF16)
    bias, r_bias = A.alloc([2], F32)
    tT, r_tT = A.alloc([S], BF16)
    hid, r_hid = A.alloc([2, 1024], BF16)
    u1, r_u1 = A.alloc([512], F32)
    u2, r_u2 = A.alloc([512], F32)
    for ti, (srcD, nm, w1n, w2n, peT_in) in enumerate(((KC, "KC", "w1_k", "w2_k", peT_k_in),
                                                         (VC, "VC", "w1_v", "w2_v", peT_v_in))):
        dma("sp", w1, wb[w1n].rearrange("(l p) n -> p l n", p=128), [dres("b_" + w1n)], [r_w1])
        dma("sp", w2, wb[w2n].rearrange("(c p) n -> p c n", p=128), [dres("b_" + w2n)], [r_w2])
        dma("sp", pef, peT_in, [], [r_pef])
        v_copy("dve", peb, pef, [r_pef], [r_peb])
        for hc in range(2):
            ps, rps = psum[hc]
            for l in range(32):
                mm(ps[:, 0:1], w1[:, l, hc * 128:(hc + 1) * 128], peb[:, l:l + 1], l == 0, l == 31, [r_w1, r_peb], [rps])
            v_copy("dve", bias[:, hc:hc + 1], ps[:, 0:1], [rps], [r_bias])
        for kh in range(2):
            dma("sp", tT, srcD[kh], [dres(nm)], [r_tT])
            cx.op("pool", lambda e: e.memset(hid, 0.0), [], [r_hid])
            for cg in range(2):
                ncs = 512 if cg == 0 else 511
                for hc in range(2):
                    ps, rps = psum[2 + (cg * 2 + hc) % 4]
                    for l in range(32):
                        st_ = l + 16 * 512 * cg
                        mm(ps[:, 0:ncs], w1[:, l, hc * 128:(hc + 1) * 128], tT[:, st_:st_ + 16 * (ncs - 1) + 1:16],
                           l == 0, l == 31, [r_w1, r_tT], [rps])
                    act(u1[:, 0:ncs], ps[:, 0:ncs], AF.Identity, [rps, r_bias], [r_u1], bias=bias[:, hc:hc + 1])
                    v_tt("dve", u2[:, 0:ncs], u1[:, 0:ncs], u1[:, 0:ncs], ALU.mult, [r_u1], [r_u2])
                    v_ts("dve", u2[:, 0:ncs], u2[:, 0:ncs], 0.044715, 1.0, ALU.mult, ALU.add, [r_u2], [r_u2])
                    v_tt("dve", u2[:, 0:ncs], u2[:, 0:ncs], u1[:, 0:ncs], ALU.mult, [r_u2, r_u1], [r_u2])
                    act(u2[:, 0:ncs], u2[:, 0:ncs], AF.Sigmoid, [r_u2], [r_u2], scale=1.5957691216057308)
                    v_tt("dve", hid[:, hc, cg * 512:cg * 512 + ncs], u2[:, 0:ncs], u1[:, 0:ncs], ALU.mult,
                         [r_u2, r_u1], [r_hid])
            if ti == 0:
                for cg in range(2):
                    ps, rps = psum[6 + cg]
                    for hc in range(2):
                        mm(ps[:], w2[:, hc, :], hid[:, hc, cg * 512:(cg + 1) * 512], hc == 0, hc == 1, [r_w2, r_hid], [rps])
                    cast(kcT[:, kh, cg * 512:(cg + 1) * 512], ps[:], [rps], [r_kcT])
            else:
                for jb in range(8):
                    ps, rps = psum[6 + jb % 2]
                    for hc in range(2):
                        mm(ps[:, 0:128], hid[:, hc, jb * 128:(jb + 1) * 128], w2[:, hc, :], hc == 0, hc == 1,
                           [r_w2, r_hid], [rps])
                    cast(vcc[:, kh, jb, :], ps[:, 0:128], [rps], [r_vcc])
    cx.barrier()

    if STOP == 3:
        return nc, es, cx
    def flash_branch(A, K_tile_fn, nkb, q_ap, r_q, extra_q, v_fn, mask_fn, scale, po, rpo, psm, rpsm,
                     sbanks=(4, 5, 6)):
        for kb in range(nkb):
            kT, r_kT, kT2, r_kT2 = K_tile_fn(kb)
            ps, rps = psum[sbanks[kb % len(sbanks)]]
            mm(ps[:], kT, q_ap, True, kT2 is None, [r_kT, r_q], [rps])
            if kT2 is not None:
                mm(ps[:], kT2, extra_q, False, True, [r_kT2, r_q], [rps])
            e_, r_e = e_tiles[kb % len(e_tiles)]
            act(e_, ps[:], AF.Exp, [rps], [r_e], scale=scale)
            mask_fn(kb, e_, r_e)
            vv, r_vv = v_fn(kb)
            mm(po[:], vv, e_, kb == 0, kb == nkb - 1, [r_vv, r_e], [rpo])
            mm(psm[:], ones_b[:], e_, kb == 0, kb == nkb - 1, [r_ones, r_e], [rpsm])

    e_tiles = None

    for si in range(min(NSLOT, KSLOTS)):
        A.reset()
        m0 = si * TT
        y_t, r_y = A.alloc([4, D], F32)
        ffn_ln(A, xT_my[:, m0:m0 + TT], x_my[m0:m0 + TT, :], "ffn1", ln_in["ln1_g"], ln_in["ln1_b"], y_t, r_y)
        cx.barrier()
        A.off = 4 * D * 2
        x1T, r_x1T = A.alloc([16, TT], BF16)
        transpose_to_bf16(y_t, r_y, x1T, r_x1T)
        if si == 0:
            dbg("x1", y_t, [r_y])
            dbg("x1T", x1T, [r_x1T])
        cs64, r_cs64 = A.alloc([2, TT], F32, parts=64)
        cs128, r_cs128 = A.alloc([2, TT], F32)
        dma("sp", cs64, cs64_my[:, :, m0:m0 + TT], [], [r_cs64])
        dma("sp", cs128, cs128_my[:, :, m0:m0 + TT], [], [r_cs128])
        tmpr, r_tmpr = A.alloc([TT], F32)
        qn, r_qn = A.alloc([8, TT], BF16)
        qr, r_qr = A.alloc([8, TT], BF16, parts=64)
        qs, r_qs = A.alloc([8, TT], BF16)
        gT, r_gT = A.alloc([TT], F32, parts=24)
        mixT, r_mixT = A.alloc([16, TT], BF16)
        mark_q = A.off
        wq_t = [A.alloc([16, 256], BF16) for _ in range(2)]
        cqT, r_cqT = A.alloc([4, TT], F32)
        rWq = dres("b_w_q")
        wi = 0
        for cg in range(2):
            w, rw = wq_t[wi % 2]; wi += 1
            dma("sp", w, wb["w_q"][:, cg * 256:(cg + 1) * 256].rearrange("(kc p) n -> p kc n", p=128), [rWq], [rw])
            for c in range(2):
                ps, rps = psum[(cg * 2 + c) % 4]
                for kc in range(16):
                    mm(ps[:], w[:, kc, c * 128:(c + 1) * 128], x1T[:, kc, :], kc == 0, kc == 15, [rw, r_x1T], [rps])
                act(cqT[:, cg * 2 + c, :], ps[:], AF.Copy, [rps], [r_cqT])
        for h in range(8):
            w, rw = wq_t[wi % 2]; wi += 1
            dma("sp", w, wb["w_q"][:, 512 + h * 256:512 + (h + 1) * 256].rearrange("(kc p) n -> p kc n", p=128),
                [rWq], [rw])
            psa, rpsa = psum[(2 * h) % 4]
            psb, rpsb = psum[(2 * h + 1) % 4]
            for kc in range(16):
                mm(psa[:], w[:, kc, 0:128], x1T[:, kc, :], kc == 0, kc == 15, [rw, r_x1T], [rpsa])
            for kc in range(16):
                mm(psb[:], w[:, kc, 128:256], x1T[:, kc, :], kc == 0, kc == 15, [rw, r_x1T], [rpsb])
            rope_comb("dve", qs[:, h, :], psa[:], psb[:], cs128, 128, [rpsa, rpsb, r_cs128], [r_qs, r_tmpr], tmpr)
        w, rw = wq_t[wi % 2]; wi += 1
        dma("sp", w[:, :, 0:24], wb["w_q"][:, 2560:2584].rearrange("(kc p) n -> p kc n", p=128), [rWq], [rw])
        ps, rps = psum[4]
        for kc in range(16):
            mm(ps[0:24, :], w[:, kc, 0:24], x1T[:, kc, :], kc == 0, kc == 15, [rw, r_x1T], [rps])
        act(gT, ps[0:24, :], AF.Sigmoid, [rps, r_gateb], [r_gT], bias=gateb[:, 0:1])
        cqn, r_cqn = A.alloc([4, TT], BF16)
        rms_scale(A, cqT, r_cqT, 4, 512.0, qg, r_qg, cqn, r_cqn)
        wuq, r_wuq = A.alloc([4, 2048], BF16)
        dma("sp", wuq, wb["w_uq"].rearrange("(kc p) n -> p kc n", p=128), [dres("b_w_uq")], [r_wuq])
        for h in range(8):
            ps, rps = psum[h % 2]
            for kc in range(4):
                mm(ps[:], wuq[:, kc, h * 256:h * 256 + 128], cqn[:, kc, :], kc == 0, kc == 3, [r_wuq, r_cqn], [rps])
            cast(qn[:, h, :], ps[:], [rps], [r_qn])
            psa, rpsa = psum[2 + (2 * h) % 4]
            psb, rpsb = psum[2 + (2 * h + 1) % 4]
            for kc in range(4):
                mm(psa[0:64, :], wuq[:, kc, h * 256 + 128:h * 256 + 192], cqn[:, kc, :], kc == 0, kc == 3,
                   [r_wuq, r_cqn], [rpsa])
            for kc in range(4):
                mm(psb[0:64, :], wuq[:, kc, h * 256 + 192:h * 256 + 256], cqn[:, kc, :], kc == 0, kc == 3,
                   [r_wuq, r_cqn], [rpsb])
            rope_comb("dve", qr[:, h, :], psa[0:64, :], psb[0:64, :], cs64, 64, [rpsa, rpsb, r_cs64],
                      [r_qr, r_tmpr], tmpr)
        if si == 0:
            dbg("qn", qn, [r_qn]); dbg("qr", qr, [r_qr]); dbg("qs", qs, [r_qs]); dbg("gT", gT, [r_gT])
            dbg("cqT", cqT, [r_cqT])
        cx.barrier()
        if STOP == 4:
            return nc, es, cx

        A.off = mark_q
        e_tiles = [A.alloc([TT], BF16) for _ in range(6)]
        kt_t = [A.alloc([TT], BF16) for _ in range(3)]
        kr_t = [A.alloc([TT], BF16, parts=64) for _ in range(3)]
        vt_t = [A.alloc([4, 128], BF16) for _ in range(3)]
        rsum, r_rsum = A.alloc([TT], F32)
        gbc, r_gbc = A.alloc([TT], F32)
        nkt_all = 8 * (si + 1)

        def causal_mask(kbz, e_, r_e):
            v_stt("dve", e_, iota_pf[:], thr[:, kbz + 4:kbz + 5], e_, ALU.is_le, ALU.mult, [r_iota, r_thr, r_e], [r_e])

        for h in range(8):
            po, rpo = psum[0 + 2 * (h % 2)]
            psm, rpsm = psum[1 + 2 * (h % 2)]
            nkb = nkt_all * 4
            state = {}

            def K_fn(kb, h=h, state=state):
                kt_i, kl = divmod(kb, 4)
                if kl == 0:
                    kt, rkt = kt_t[kt_i % 3]
                    kr_, rkr = kr_t[kt_i % 3]
                    vt, rvt = vt_t[kt_i % 3]
                    k0 = kt_i * TT
                    dma("sp", kt, KN[h, :, k0:k0 + TT], [dres("KN")], [rkt])
                    dma("sp", kr_, KR[:, k0:k0 + TT], [dres("KR")], [rkr])
                    dma("sp", vt, VM[k0:k0 + TT, h * 128:(h + 1) * 128].rearrange("(kb p) n -> p kb n", p=128),
                        [dres("VM")], [rvt])
                    state["cur"] = (kt, rkt, kr_, rkr, vt, rvt)
                kt, rkt, kr_, rkr, vt, rvt = state["cur"]
                return kt[:, kl * 128:(kl + 1) * 128], rkt, kr_[:, kl * 128:(kl + 1) * 128], rkr

            def V_fn(kb, state=state):
                kt, rkt, kr_, rkr, vt, rvt = state["cur"]
                return vt[:, kb % 4, :], rvt

            def M_fn(kb, e_, r_e, si=si):
                if kb >= 32 * si:
                    causal_mask(kb - 32 * si, e_, r_e)

            flash_branch(A, K_fn, nkb, qn[:, h, :], r_qn, qr[:, h, :], V_fn, M_fn, MLA_SCALE, po, rpo, psm, rpsm,
                         sbanks=(4, 5, 6, 7))
            v_ts("dve", rsum, psm[:], 1e-30, None, ALU.max, None, [rpsm], [r_rsum])
            cx.op("dve", lambda e: e.reciprocal(out=rsum, in_=rsum), [r_rsum], [r_rsum])
            v_tt("dve", mixT[:, h, :], po[:], rsum, ALU.mult, [rpo, r_rsum], [r_mixT])

        if si == 0:
            dbg("mix_mla", mixT, [r_mixT])
        if STOP == 5:
            cx.barrier()
            return nc, es, cx
        impT, r_impT = A.alloc([2, TT], F32)
        ocn, r_ocn = A.alloc([4, TT], F32)
        onsa, r_onsa = A.alloc([TT], F32)
        tmpf, r_tmpf = A.alloc([TT], F32)
        selT, r_selT = A.alloc([2, TT], BF16)
        msb_t = [A.alloc([TT], BF16) for _ in range(2)]
        sbias, r_sbias = A.alloc([4, 256], F32)
        dma("sp", sbias, sbias_in[:, si * 4:(si + 1) * 4, :], [], [r_sbias])
        score, r_score = A.alloc([256], F32)
        work, r_work = A.alloc([256], F32)
        m8, r_m8 = A.alloc([16], F32)
        selq, r_selq = A.alloc([256], BF16)
        selq2, r_selq2 = A.alloc([256], F32)

        gm, r_gm = A.alloc([TT], F32, parts=24)

        def gate_bcast(row):
            ps, rps = psum[7]
            v_ts("dve", gm, gT, ident_f[0:24, row:row + 1], None, ALU.mult, None, [r_gT, r_identf], [r_gm])
            mm(ps[:], ones_f[:], gm, True, True, [r_onesf, r_gm], [rps])
            return ps, rps

        def finish_branch(po, rpo, psm, rpsm, row, dst, r_dst, accumulate):
            v_ts("dve", rsum, psm[:], 1e-30, None, ALU.max, None, [rpsm], [r_rsum])
            cx.op("dve", lambda e: e.reciprocal(out=rsum, in_=rsum), [r_rsum], [r_rsum])
            psg, rpsg = gate_bcast(row)
            v_tt("dve", gbc, psg[:], rsum, ALU.mult, [rpsg, r_rsum], [r_gbc])
            if accumulate:
                v_tt("dve", tmpf, po[:], gbc, ALU.mult, [rpo, r_gbc], [r_tmpf])
                v_tt("pool", dst, dst, tmpf, ALU.add, [r_dst, r_tmpf], [r_dst])
            else:
                v_tt("dve", dst, po[:], gbc, ALU.mult, [rpo, r_gbc], [r_dst])

        for kh in range(2):
            ncb = 2 * si + 2
            for g in range(4):
                hq = kh * 4 + g
                po, rpo = psum[0]
                psm, rpsm = psum[1]
                pi0, rpi0 = psum[2]
                pi1, rpi1 = psum[3]
                for jb in range(ncb):
                    ps, rps = psum[4 + jb % 3]
                    mm(ps[:], kcT[:, kh, jb * 128:(jb + 1) * 128], qs[:, hq, :], True, True, [r_kcT, r_qs], [rps])
                    e_, r_e = e_tiles[jb % 3]
                    act(e_, ps[:], AF.Exp, [rps], [r_e], scale=NSA_SCALE)
                    if jb >= 2 * si - 1:
                        col = 36 + 3 * si + (jb - (2 * si - 1))
                        v_stt("dve", e_, iota_16[:], thr[:, col:col + 1], e_, ALU.is_le, ALU.mult,
                              [r_iota16, r_thr, r_e], [r_e])
                    first, last = jb == 0, jb == ncb - 1
                    mm(po[:], vcc[:, kh, jb, :], e_, first, last, [r_vcc, r_e], [rpo])
                    mm(psm[:], ones_b[:], e_, first, last, [r_ones, r_e], [rpsm])
                    mm(pi0[:], ovb[:, jb * 256:jb * 256 + 128], e_, first, last, [r_ov, r_e], [rpi0])
                    mm(pi1[:], ovb[:, jb * 256 + 128:jb * 256 + 256], e_, first, last, [r_ov, r_e], [rpi1])
                v_ts("dve", rsum, psm[:], 1e-30, None, ALU.max, None, [rpsm], [r_rsum])
                cx.op("dve", lambda e: e.reciprocal(out=rsum, in_=rsum), [r_rsum], [r_rsum])
                for jc, (pi, rpi) in enumerate(((pi0, rpi0), (pi1, rpi1))):
                    if g == 0:
                        v_tt("dve", impT[:, jc, :], pi[:], rsum, ALU.mult, [rpi, r_rsum], [r_impT])
                    else:
                        v_tt("dve", tmpf, pi[:], rsum, ALU.mult, [rpi, r_rsum], [r_tmpf])
                        v_tt("pool", impT[:, jc, :], impT[:, jc, :], tmpf, ALU.add, [r_impT, r_tmpf], [r_impT])
                psg, rpsg = gate_bcast(0 * 8 + hq)
                v_tt("dve", gbc, psg[:], rsum, ALU.mult, [rpsg, r_rsum], [r_gbc])
                v_tt("dve", ocn[:, g, :], po[:], gbc, ALU.mult, [rpo, r_gbc], [r_ocn])
            for qb in range(4):
                ps, rps = psum[4 + qb % 3]
                for jc in range(2):
                    tr(ps[:, jc * 128:(jc + 1) * 128], impT[:, jc, qb * 128:(qb + 1) * 128], ident_f[:],
                       [r_impT, r_identf], [rps])
                v_tt("dve", score, ps[:, 0:256], sbias[:, qb, :], ALU.add, [rps, r_sbias], [r_score])
                cx.op("dve", lambda e: e.max(out=m8[:, 0:8], in_=score), [r_score], [r_m8])
                cx.op("dve", lambda e: e.match_replace(out=work, in_to_replace=m8[:, 0:8], in_values=score,
                                                       imm_value=-3.0e38), [r_score, r_m8], [r_work])
                cx.op("dve", lambda e: e.max(out=m8[:, 8:16], in_=work), [r_work], [r_m8])
                v_ts("dve", selq2, score, m8[:, 15:16], None, ALU.is_ge, None, [r_score, r_m8], [r_selq2])
                v_stt("dve", selq, score, -1.0e29, selq2, ALU.is_gt, ALU.mult, [r_score, r_selq2], [r_selq])
                pst, rpst = psum[7]
                pstb = pst[:].bitcast(BF16)
                for jc in range(2):
                    tr(pstb[:, jc * 128:(jc + 1) * 128], selq[:, jc * 128:(jc + 1) * 128], ident_b[:],
                       [r_selq, r_identb], [rpst])
                for jc in range(2):
                    v_copy("dve", selT[:, jc, qb * 128:(qb + 1) * 128], pstb[:, jc * 128:(jc + 1) * 128],
                           [rpst], [r_selT])
            for g in range(4):
                hq = kh * 4 + g
                po, rpo = psum[0]
                psm, rpsm = psum[1]
                nkb = nkt_all * 4
                state = {}

                def K_fn(kb, kh=kh, state=state):
                    kt_i, kl = divmod(kb, 4)
                    if kl == 0:
                        kt, rkt = kt_t[kt_i % 3]
                        vt, rvt = vt_t[kt_i % 3]
                        k0 = kt_i * TT
                        dma("sp", kt, KS[kh, :, k0:k0 + TT], [dres("KS")], [rkt])
                        dma("sp", vt, VSW[k0:k0 + TT, kh * 128:(kh + 1) * 128].rearrange("(kb p) n -> p kb n", p=128),
                            [dres("VSW")], [rvt])
                        state["cur"] = (kt, rkt, vt, rvt)
                    kt, rkt, vt, rvt = state["cur"]
                    return kt[:, kl * 128:(kl + 1) * 128], rkt, None, None

                def V_fn(kb, state=state):
                    kt, rkt, vt, rvt = state["cur"]
                    return vt[:, kb % 4, :], rvt

                def M_fn(kb, e_, r_e, si=si):
                    pm, rpm = psum[6 + kb % 2]
                    mm(pm[:], eexp[:, (kb % 64) * 128:(kb % 64 + 1) * 128], selT[:, kb // 64, :], True, True,
                       [r_eexp, r_selT], [rpm])
                    if kb >= 32 * si:
                        causal_mask(kb - 32 * si, e_, r_e)
                    v_tt("dve", e_, e_, pm[:], ALU.mult, [r_e, rpm], [r_e])

                flash_branch(A, K_fn, nkb, qs[:, hq, :], r_qs, None, V_fn, M_fn, NSA_SCALE, po, rpo, psm, rpsm,
                             sbanks=(2, 3, 4, 5))
                v_copy("pool", onsa, ocn[:, g, :], [r_ocn], [r_onsa])
                finish_branch(po, rpo, psm, rpsm, 1 * 8 + hq, onsa, r_onsa, True)
                po, rpo = psum[0]
                psm, rpsm = psum[1]
                kb_lo = 32 * si - 4 if si > 0 else 0
                nkb = 32 * si + 32 - kb_lo
                state = {}

                def K_fn(kb, kh=kh, state=state, kb_lo=kb_lo):
                    kt_i, kl = divmod(kb, 4)
                    if kl == 0:
                        kt, rkt = kt_t[kt_i % 3]
                        vt, rvt = vt_t[kt_i % 3]
                        k0 = kb_lo * 128 + kt_i * TT
                        dma("sp", kt, KW[kh, :, k0:k0 + TT], [dres("KW")], [rkt])
                        dma("sp", vt, VSW[k0:k0 + TT, 256 + kh * 128:256 + (kh + 1) * 128]
                            .rearrange("(kb p) n -> p kb n", p=128), [dres("VSW")], [rvt])
                        state["cur"] = (kt, rkt, vt, rvt)
                    kt, rkt, vt, rvt = state["cur"]
                    return kt[:, kl * 128:(kl + 1) * 128], rkt, None, None

                def V_fn(kb, state=state):
                    kt, rkt, vt, rvt = state["cur"]
                    return vt[:, kb % 4, :], rvt

                def M_fn(kb, e_, r_e, si=si, kb_lo=kb_lo):
                    kbz = kb + kb_lo - 32 * si
                    if kbz >= 0:
                        causal_mask(kbz, e_, r_e)
                    wcol = 48 + kbz + 4
                    v_stt("dve", e_, iota_pf[:], thr[:, wcol:wcol + 1], e_, ALU.is_gt, ALU.mult,
                          [r_iota, r_thr, r_e], [r_e])

                flash_branch(A, K_fn, nkb, qs[:, hq, :], r_qs, None, V_fn, M_fn, NSA_SCALE, po, rpo, psm, rpsm,
                             sbanks=(2, 3, 4, 5, 6))
                finish_branch(po, rpo, psm, rpsm, 2 * 8 + hq, onsa, r_onsa, True)
                v_copy("dve", mixT[:, 8 + hq, :], onsa, [r_onsa], [r_mixT])
        cx.barrier()

        if si == 0:
            dbg("mix_all", mixT, [r_mixT])
            dbg("kcT", kcT[:], [r_kcT]); dbg("vcc", vcc[:], [r_vcc])
            dbg("KN0", KN[:, :, 0:512].rearrange("h p t -> p h t"), [dres("KN")])
            dbg("KR0", KR[:, 0:512], [dres("KR")])
            dbg("KS0", KS[:, :, 0:512].rearrange("h p t -> p h t"), [dres("KS")])
            dbg("KW0", KW[:, :, 0:512].rearrange("h p t -> p h t"), [dres("KW")])
            dbg("KC0", KC[:, :, 0:512].rearrange("h p t -> p h t"), [dres("KC")])
            dbg("VC0", VC[:, :, 0:512].rearrange("h p t -> p h t"), [dres("VC")])
            dbg("VM0", VM[0:512, :].rearrange("(tb p) n -> p tb n", p=128), [dres("VM")])
            dbg("VSW0", VSW[0:512, :].rearrange("(tb p) n -> p tb n", p=128), [dres("VSW")])
        if STOP == 6:
            return nc, es, cx
        A.off = mark_q
        for tb in range(4):
            act(y_t[:, tb, :], y_t[:, tb, :], AF.Copy, [r_y], [r_y], scale=ALPHA)
        wo_t = [A.alloc([16, 512], BF16) for _ in range(2)]
        for ng in range(4):
            wo, rwo = wo_t[ng % 2]
            dma("sp", wo, wb["w_out"][:, ng * 512:(ng + 1) * 512].rearrange("(kc p) n -> p kc n", p=128),
                [dres("b_w_out")], [rwo])
            for tb in range(4):
                ps, rps = psum[tb + 4 * (ng % 2)]
                for kc in range(16):
                    mm(ps[:], mixT[:, kc, tb * 128:(tb + 1) * 128], wo[:, kc, :], kc == 0, kc == 15, [r_mixT, rwo], [rps])
                v_tt("dve", y_t[:, tb, ng * 512:(ng + 1) * 512], ps[:], y_t[:, tb, ng * 512:(ng + 1) * 512], ALU.add,
                     [rps, r_y], [r_y])
        layer_norm(A, y_t, r_y, ln_in["ln2_g"], ln_in["ln2_b"])
        if si == 0:
            dbg("x2", y_t, [r_y])
        cx.barrier()
        A.off = 4 * D * 2
        x2T, r_x2T = A.alloc([16, TT], BF16)
        transpose_to_bf16(y_t, r_y, x2T, r_x2T)
        ffn_ln(A, x2T, None, "ffn2", ln_in["ln3_g"], ln_in["ln3_b"], y_t, r_y, xT_res=r_x2T)
        dma("sp", out[m0:m0 + TT, :].rearrange("(tb p) d -> p tb d", p=128), y_t, [r_y], [dres("out")])
        cx.barrier()

    cx.barrier()
    return nc, es, cx


def emit_program():
    nc, es, cx = build_program()
    with nc.Block() as block:
        @block.tensor
        def _(e):
            cx.replay("pe", e)

        @block.scalar
        def _(e):
            cx.replay("act", e)

        @block.vector
        def _(e):
            cx.replay("dve", e)

        @block.gpsimd
        def _(e):
            cx.replay("pool", e)

        @block.sync
        def _(e):
            cx.replay("sp", e)
    return nc


def _rope_tables(pos, d):
    half = d // 2
    inv = (np.float32(10000.0) ** (-np.arange(half, dtype=np.float32) * np.float32(2.0 / d))).astype(np.float32)
    ang = pos.astype(np.float32)[None, :] * inv[:, None]
    cos, sin = np.cos(ang).astype(np.float32), np.sin(ang).astype(np.float32)
    t = np.empty((d, 2, pos.shape[0]), np.float32)
    t[:half, 0], t[half:, 0] = cos, cos
    t[:half, 1], t[half:, 1] = -sin, sin
    return t


def _swap_halves(w):
    h = w.shape[1] // 2
    return np.concatenate([w[:, h:], w[:, :h]], axis=1)


def kernel(**inp):
    f = lambda k: np.ascontiguousarray(np.asarray(inp[k], dtype=np.float32)[0])
    x = f("x")
    w_in = f("w_in")
    c_q, c_kv, k_rope = w_in[:, 0:512], w_in[:, 512:768], w_in[:, 768:832]
    q_nsa = w_in[:, 832:1856]
    k_cmp, v_cmp = w_in[:, 1856:2112], w_in[:, 2112:2368]
    k_sel, v_sel = w_in[:, 2368:2624], w_in[:, 2624:2880]
    k_win, v_win = w_in[:, 2880:3136], w_in[:, 3136:3392]
    gl = w_in[:, 3392:3416]
    cols = [c_kv, k_rope, _swap_halves(k_rope)]
    for kk in (k_cmp, k_sel, k_win):
        for kh in range(2):
            blk = kk[:, kh * 128:(kh + 1) * 128]
            cols += [blk, _swap_halves(blk)]
    cols.append(v_cmp)
    w_kvf = np.ascontiguousarray(np.concatenate(cols, axis=1))
    assert w_kvf.shape[1] == NKV_F
    w_kvt = np.ascontiguousarray(np.concatenate([v_sel, v_win], axis=1))
    qcols = [c_q]
    for h in range(8):
        blk = q_nsa[:, h * 128:(h + 1) * 128]
        qcols += [blk, _swap_halves(blk)]
    qcols.append(gl)
    w_q = np.ascontiguousarray(np.concatenate(qcols, axis=1))
    wuq = f("mla_w_uq")
    ucols = []
    for h in range(8):
        blk = wuq[:, h * 192:(h + 1) * 192]
        ucols += [blk[:, :128], blk[:, 128:], _swap_halves(blk[:, 128:])]
    w_uq = np.ascontiguousarray(np.concatenate(ucols, axis=1))
    shared = {
        "x_all": x, "xT_all": np.ascontiguousarray(x.T),
        "ffn1_w_gate": f("ffn1_w_gate"), "ffn1_w_up": f("ffn1_w_up"), "ffn1_w_down": f("ffn1_w_down"),
        "ffn2_w_gate": f("ffn2_w_gate"), "ffn2_w_up": f("ffn2_w_up"), "ffn2_w_down": f("ffn2_w_down"),
        "w_kvf": w_kvf, "w_kvt": w_kvt, "w_q": w_q, "w_uq": w_uq, "w_ukv": f("mla_w_ukv"), "w_out": f("w_out"),
        "w1_k": f("nsa_cmp_w1_k"), "w1_v": f("nsa_cmp_w1_v"), "w2_k": f("nsa_cmp_w2_k"), "w2_v": f("nsa_cmp_w2_v"),
        "qg": np.ascontiguousarray(f("mla_q_norm_g").reshape(4, 128).T),
        "kvg": np.ascontiguousarray(f("mla_kv_norm_g").reshape(2, 128).T),
        "gateb": np.ascontiguousarray(f("nsa_gate_b").reshape(24, 1)),
        "peT_k": np.ascontiguousarray(f("nsa_cmp_pe_k").T), "peT_v": np.ascontiguousarray(f("nsa_cmp_pe_v").T),
        "ident": np.eye(128, dtype=np.float32),
    }
    for k in ("ln1_g", "ln1_b", "ln2_g", "ln2_b", "ln3_g", "ln3_b"):
        shared[k] = np.ascontiguousarray(np.asarray(inp[k], np.float32).reshape(1, D))
    pos_all = np.arange(S)
    shared["cs64_all"] = _rope_tables(pos_all, 64)
    shared["cs128_all"] = _rope_tables(pos_all, 128)
    ee = np.zeros((128, 64, 128), np.float32)
    for kbl in range(64):
        ee[2 * kbl, kbl, :64] = 1.0
        ee[2 * kbl + 1, kbl, 64:] = 1.0
    shared["eexp"] = ee.reshape(128, 64 * 128)
    ci = np.arange(1024)[:, None] * 16
    sj = np.arange(256)[None, :] * 64
    ovm = np.clip(np.minimum(ci + 32, sj + 64) - np.maximum(ci, sj), 0, None).astype(np.float32) / 16.0
    ovm[1023, :] = 0.0
    shared["ov"] = np.ascontiguousarray(ovm.reshape(8, 128, 256).transpose(1, 0, 2).reshape(128, 8 * 256))

    in_maps = []
    tok_idx = []
    for c in range(NCORE):
        idx = np.concatenate([np.arange((8 * i + c) * TT, (8 * i + c + 1) * TT) for i in range(NSLOT)])
        tok_idx.append(idx)
        m = dict(shared)
        m["x_my"] = np.ascontiguousarray(x[idx])
        m["xT_my"] = np.ascontiguousarray(x[idx].T)
        m["cs64_my"] = _rope_tables(idx, 64)
        m["cs128_my"] = _rope_tables(idx, 128)
        tq = idx[:, None]
        bj = np.arange(256)[None, :]
        cur = tq // 64
        valid = bj * 64 <= tq
        forced = (bj == 0) | (bj == cur) | (bj == cur - 1)
        sb = np.where(valid, np.where(forced, np.float32(1e4), np.float32(0.0)), np.float32(-1e30)).astype(np.float32)
        m["sbias"] = np.ascontiguousarray(sb.reshape(16, 128, 256).transpose(1, 0, 2))
        th = np.zeros((96,), np.float32)
        for kbz in range(-4, 32):
            th[kbz + 4] = 512 * c - 128 * kbz
            th[48 + kbz + 4] = 512 * c - 128 * kbz - 512
        for i in range(NSLOT):
            for r in range(3):
                jb = 2 * i - 1 + r
                th[36 + 3 * i + r] = 4096 * i + 512 * c - 31 - 2048 * jb
        m["thr"] = np.ascontiguousarray(np.broadcast_to(th[None, :], (128, 96)))
        in_maps.append(m)

    nc = emit_program()
    if os.environ.get('KTRACE'):
        res = run_bass_kernel_spmd(nc, in_maps, core_ids=list(range(NCORE)), trace=True)
    else:
        res = run_bass_kernel_spmd(nc, in_maps, core_ids=list(range(NCORE)))
    LAST["res"] = res
    out = np.empty((1, S, D), np.float32)
    for c in range(NCORE):
        out[0, tok_idx[c]] = res.results[c]["out"]
    return out
```

```python
import math
import os
STOP = int(os.environ.get('KSTOP', '99'))
NT1 = int(os.environ.get('KNT1', '32'))
KSLOTS = int(os.environ.get('KSLOTS', '4'))
KDBG = int(os.environ.get('KDBG', '0'))
DBG_LAYOUT = {}
LAST = {}
from contextlib import ExitStack
import numpy as np
import concourse.bass as bass
import concourse.mybir as mybir
from concourse.bass_utils import run_bass_kernel_spmd

F32 = mybir.dt.float32
BF16 = mybir.dt.bfloat16
AF = mybir.ActivationFunctionType
ALU = mybir.AluOpType
AX = mybir.AxisListType

S = 16384
D = 2048
DFF = 5632
NCORE = 8
TT = 512
NT_ALL = S // TT
NSLOT = 4
ALPHA = 2.0 ** 0.25
LN_EPS = 1e-5
RMS_EPS = 1e-6
NKV_F = 2176
NKV_T = 512
NQ = 512 + 2048 + 24
MLA_SCALE = 192.0 ** -0.5
NSA_SCALE = 128.0 ** -0.5


class Res:
    __slots__ = ("w", "r")

    def __init__(self):
        self.w = None
        self.r = {}


class Ctx:
    ENG = ("pe", "act", "dve", "pool", "sp")

    def __init__(self, nc, es):
        self.nc = nc
        self.ops = {e: [] for e in self.ENG}
        self.seq = {e: 0 for e in self.ENG}
        self.known = {e: {} for e in self.ENG}
        self.sems = {}
        for e in ("pe", "act", "dve", "pool"):
            self.sems[e] = es.enter_context(nc.semaphore("S_" + e))
        self.dmak = {"sp": 20, "pool": 6, "act": 6}
        self.dman = {q: 0 for q in self.dmak}
        for q, k in self.dmak.items():
            for i in range(k):
                self.sems[(q, i)] = es.enter_context(nc.semaphore("D_%s_%d" % (q, i)))
        self.last = {}

    def _need(self, eng, tok, waits, war=False):
        if tok is None:
            return
        key, val, teng = tok
        if teng == eng and not isinstance(key, tuple):
            if eng == "pe" or war:
                return
        if self.known[eng].get(key, 0) >= val:
            return
        self.known[eng][key] = val
        waits.append((key, val))

    def op(self, eng, fn, reads=(), writes=(), dma=False):
        waits = []
        for r in reads:
            self._need(eng, r.w, waits)
        for w in writes:
            self._need(eng, w.w, waits)
            for key, (val, teng) in w.r.items():
                self._need(eng, (key, val, teng), waits, war=True)
        if dma:
            k = self.dmak[eng]
            n = self.dman[eng]
            self.dman[eng] = n + 1
            key = (eng, n % k)
            val = 16 * (n // k + 1)
            if n >= k:
                self._need(eng, (key, val - 16, eng), waits)
            tok = (key, val, eng)
            inc = (key, 16)
        else:
            self.seq[eng] += 1
            tok = (eng, self.seq[eng], eng)
            inc = (eng, 1)
        self.ops[eng].append((waits, fn, inc))
        self.last[tok[0]] = tok
        for r in reads:
            r.r[tok[0]] = (tok[1], tok[2])
        for w in writes:
            w.w = tok
            w.r = {}
        return tok

    def barrier(self):
        toks = list(self.last.values())
        for e in self.ENG:
            waits = []
            for t in toks:
                self._need(e, (t[0], t[1], "x"), waits)
            if waits:
                self.ops[e].append((waits, None, None))

    def replay(self, eng, e):
        for waits, fn, inc in self.ops[eng]:
            for key, val in waits:
                e.wait_ge(self.sems[key], val)
            if fn is not None:
                ins = fn(e)
                ins.then_inc(self.sems[inc[0]], inc[1])


class Arena:
    def __init__(self, ap, nelem):
        self.ap = ap
        self.n = nelem
        self.off = 0

    def reset(self):
        self.off = 0

    def alloc(self, free_shape, dtype, parts=128):
        n = int(np.prod(free_shape))
        nb = n * (2 if dtype == F32 else 1)
        nb = (nb + 15) // 16 * 16
        assert self.off + nb <= self.n, ("arena overflow", self.off, nb, self.n)
        v = self.ap[0:parts, self.off:self.off + nb]
        self.off += nb
        if dtype == F32:
            v = v.bitcast(F32)
        v = v[:, 0:n]
        if len(free_shape) == 2:
            v = v.rearrange("p (a b) -> p a b", b=free_shape[1])
        elif len(free_shape) == 3:
            v = v.rearrange("p (a b c) -> p a b c", b=free_shape[1], c=free_shape[2])
        return v, Res()


def build_program():
    nc = bass.Bass("TRN2", target_bir_lowering=False)
    es = ExitStack()

    def din(name, shape, dt=F32):
        return nc.dram_tensor(name, list(shape), dt, kind="ExternalInput").ap()

    def dscr(name, shape, dt=BF16):
        return nc.dram_tensor(name, list(shape), dt).ap()

    x_all = din("x_all", [S, D])
    xT_all = din("xT_all", [D, S])
    x_my = din("x_my", [NSLOT * TT, D])
    xT_my = din("xT_my", [D, NSLOT * TT])
    wsrc = {}
    for nm in ("ffn1_w_gate", "ffn1_w_up", "ffn2_w_gate", "ffn2_w_up"):
        wsrc[nm] = din(nm, [D, DFF])
    for nm in ("ffn1_w_down", "ffn2_w_down"):
        wsrc[nm] = din(nm, [DFF, D])
    wsrc["w_kvf"] = din("w_kvf", [D, NKV_F])
    wsrc["w_kvt"] = din("w_kvt", [D, NKV_T])
    wsrc["w_q"] = din("w_q", [D, NQ])
    wsrc["w_uq"] = din("w_uq", [512, 2048])
    wsrc["w_ukv"] = din("w_ukv", [256, 2048])
    wsrc["w_out"] = din("w_out", [D, D])
    wsrc["w1_k"] = din("w1_k", [4096, 256])
    wsrc["w1_v"] = din("w1_v", [4096, 256])
    wsrc["w2_k"] = din("w2_k", [256, 128])
    wsrc["w2_v"] = din("w2_v", [256, 128])
    ln_in = {k: din(k, [1, D]) for k in ("ln1_g", "ln1_b", "ln2_g", "ln2_b", "ln3_g", "ln3_b")}
    qg_in = din("qg", [128, 4])
    kvg_in = din("kvg", [128, 2])
    gateb_in = din("gateb", [24, 1])
    peT_k_in = din("peT_k", [128, 32])
    peT_v_in = din("peT_v", [128, 32])
    cs64_all = din("cs64_all", [64, 2, S])
    cs128_all = din("cs128_all", [128, 2, S])
    cs64_my = din("cs64_my", [64, 2, NSLOT * TT])
    cs128_my = din("cs128_my", [128, 2, NSLOT * TT])
    sbias_in = din("sbias", [128, 16, 256])
    thr_in = din("thr", [128, 96])
    ident_in = din("ident", [128, 128])
    eexp_in = din("eexp", [128, 64 * 128])
    ov_in = din("ov", [128, 8 * 256])
    out = nc.dram_tensor("out", [NSLOT * TT, D], F32, kind="ExternalOutput").ap()

    wb = {nm: dscr("b_" + nm, ap.shape) for nm, ap in wsrc.items()}
    KN = dscr("KN", [8, 128, S])
    KR = dscr("KR", [64, S])
    VM = dscr("VM", [S, 1024])
    KS = dscr("KS", [2, 128, S])
    KW = dscr("KW", [2, 128, S])
    KC = dscr("KC", [2, 128, S])
    VC = dscr("VC", [2, 128, S])
    VSW = dscr("VSW", [S, 512])

    ARENA_N = 80 * 1024
    arena_t = es.enter_context(nc.sbuf_tensor("arena", [128, ARENA_N], BF16))
    A = Arena(arena_t, ARENA_N)

    def pers(name, shape, dt):
        t = es.enter_context(nc.sbuf_tensor("s_" + name, list(shape), dt))
        return t, Res()

    ident_f, r_identf = pers("ident_f", [128, 128], F32)
    ident_b, r_identb = pers("ident_b", [128, 128], BF16)
    ones_b, r_ones = pers("ones_b", [128, 128], BF16)
    eexp, r_eexp = pers("eexp", [128, 64 * 128], BF16)
    ovb, r_ov = pers("ovb", [128, 8 * 256], BF16)
    ones_f, r_onesf = pers("ones_f", [24, 128], F32)
    thr, r_thr = pers("thr", [128, 96], F32)
    iota_pf, r_iota = pers("iota_pf", [128, 512], F32)
    iota_16, r_iota16 = pers("iota_16", [128, 512], F32)
    qg, r_qg = pers("qg", [128, 4], F32)
    kvg, r_kvg = pers("kvg", [128, 2], F32)
    gateb, r_gateb = pers("gateb", [24, 1], F32)
    kcT, r_kcT = pers("kcT", [128, 2, 1024], BF16)
    vcc, r_vcc = pers("vcc", [128, 2, 8, 128], BF16)
    psum = []
    for i in range(8):
        t = es.enter_context(nc.psum_tensor("ps%d" % i, [128, 512], F32))
        psum.append((t, Res()))

    cx = Ctx(nc, es)
    rD = {}
    dbg_t = {}
    dbg_off = {"f": 0, "b": 0}
    if KDBG:
        dbg_t["f"] = nc.dram_tensor("dbg_f", [128, 40960], F32, kind="ExternalOutput").ap()
        dbg_t["b"] = nc.dram_tensor("dbg_b", [128, 81920], BF16, kind="ExternalOutput").ap()

    def dbg(name, ap, reads, parts=128):
        if not KDBG:
            return
        k = "f" if ap.dtype == F32 else "b"
        shp = list(ap.shape)
        n = int(np.prod(shp[1:]))
        off = dbg_off[k]
        dbg_off[k] = off + n
        DBG_LAYOUT[name] = (k, off, shp)
        dst = dbg_t[k][0:shp[0], off:off + n]
        if len(shp) == 3:
            dst = dst.rearrange("p (a b) -> p a b", b=shp[2])
        elif len(shp) == 4:
            dst = dst.rearrange("p (a b c) -> p a b c", b=shp[2], c=shp[3])
        cx.op("sp", lambda e: e.dma_start(out=dst, in_=ap), reads, [dres("dbg")], dma=True)

    def dres(name):
        if name not in rD:
            rD[name] = Res()
        return rD[name]

    def dma(q, out_ap, in_ap, reads, writes):
        return cx.op(q, lambda e: e.dma_start(out=out_ap, in_=in_ap), reads, writes, dma=True)

    def mm(ps, lhsT, rhs, start, stop, reads, writes):
        return cx.op("pe", lambda e: e.matmul(ps, lhsT, rhs, start=start, stop=stop), reads, writes)

    def tr(ps, in_, idt, reads, writes):
        return cx.op("pe", lambda e: e.transpose(ps, in_, idt), reads, writes)

    def act(out_ap, in_ap, func, reads, writes, scale=1.0, bias=0.0):
        return cx.op("act", lambda e: e.activation(out=out_ap, in_=in_ap, func=func, scale=scale, bias=bias),
                     reads, writes)

    def v_tt(eng, out_ap, a, b, op, reads, writes):
        return cx.op(eng, lambda e: e.tensor_tensor(out=out_ap, in0=a, in1=b, op=op), reads, writes)

    def v_ts(eng, out_ap, a, s1, s2, op0, op1, reads, writes):
        if s2 is None:
            return cx.op(eng, lambda e: e.tensor_scalar(out=out_ap, in0=a, scalar1=s1, scalar2=None, op0=op0),
                         reads, writes)
        return cx.op(eng, lambda e: e.tensor_scalar(out=out_ap, in0=a, scalar1=s1, scalar2=s2, op0=op0, op1=op1),
                     reads, writes)

    def v_stt(eng, out_ap, a, s, b, op0, op1, reads, writes):
        return cx.op(eng, lambda e: e.scalar_tensor_tensor(out=out_ap, in0=a, scalar=s, in1=b, op0=op0, op1=op1),
                     reads, writes)

    def v_copy(eng, out_ap, in_ap, reads, writes):
        return cx.op(eng, lambda e: e.tensor_copy(out=out_ap, in_=in_ap), reads, writes)

    cast_rr = [0]

    def cast(out_ap, in_ap, reads, writes):
        cast_rr[0] += 1
        if cast_rr[0] % 2:
            return act(out_ap, in_ap, AF.Copy, reads, writes)
        return v_copy("dve", out_ap, in_ap, reads, writes)

    A.reset()
    dma("sp", ident_f[:], ident_in, [], [r_identf])
    v_copy("dve", ident_b[:], ident_f[:], [r_identf], [r_identb])
    cx.op("pool", lambda e: e.memset(ones_b[:], 1.0), [], [r_ones])
    cx.op("pool", lambda e: e.memset(kcT[:], 0.0), [], [r_kcT])
    cx.op("pool", lambda e: e.memset(vcc[:], 0.0), [], [r_vcc])
    cx.op("pool", lambda e: e.memset(ones_f[:], 1.0), [], [r_onesf])
    dma("sp", thr[:], thr_in, [], [r_thr])
    dma("sp", qg[:], qg_in, [], [r_qg])
    dma("sp", kvg[:], kvg_in, [], [r_kvg])
    dma("sp", gateb[:], gateb_in, [], [r_gateb])
    cx.op("pool", lambda e: e.iota(iota_pf[:], [[-1, 512]], base=0, channel_multiplier=1,
                                   allow_small_or_imprecise_dtypes=True), [], [r_iota])
    cx.op("pool", lambda e: e.iota(iota_16[:], [[-1, 512]], base=0, channel_multiplier=16,
                                   allow_small_or_imprecise_dtypes=True), [], [r_iota16])
    st0, r_st0 = A.alloc([64 * 128], F32)
    dma("sp", st0, eexp_in, [], [r_st0])
    cast(eexp[:], st0, [r_st0], [r_eexp])
    st1, r_st1 = A.alloc([8 * 256], F32)
    dma("sp", st1, ov_in, [], [r_st1])
    cast(ovb[:], st1, [r_st1], [r_ov])

    CW = 4096
    stg = [A.alloc([CW], F32) for _ in range(3)]
    stb = [A.alloc([CW], BF16) for _ in range(3)]
    ci = 0
    for nm, src in wsrc.items():
        K_, N_ = src.shape
        flat_s = src.rearrange("k n -> (k n)")
        flat_d = wb[nm].rearrange("k n -> (k n)")
        tot = K_ * N_
        per = 128 * CW
        o = 0
        while o < tot:
            n = min(per, tot - o)
            w_ = n // 128
            assert w_ * 128 == n, (nm, n)
            (sf, rsf), (sb_, rsb) = stg[ci % 3], stb[ci % 3]
            ci += 1
            dma("sp", sf[:, 0:w_], flat_s[o:o + n].rearrange("(p w) -> p w", p=128), [], [rsf])
            cast(sb_[:, 0:w_], sf[:, 0:w_], [rsf], [rsb])
            dma("sp", flat_d[o:o + n].rearrange("(p w) -> p w", p=128), sb_[:, 0:w_], [rsb], [dres("b_" + nm)])
            o += n
    cx.barrier()
    if STOP == 0:
        return nc, es, cx

    def ffn_ln(A, xT_src, x_src, pre, lng, lnb, y_t, r_y, xT_res=None):
        Wg, Wu, Wd = wb[pre + "_w_gate"], wb[pre + "_w_up"], wb[pre + "_w_down"]
        rWg, rWu, rWd = dres("b_" + pre + "_w_gate"), dres("b_" + pre + "_w_up"), dres("b_" + pre + "_w_down")
        if xT_res is None:
            xTb, r_xTb = A.alloc([16, TT], BF16)
            sx = [A.alloc([2, TT], F32) for _ in range(2)]
            for c4 in range(8):
                sf, rsf = sx[c4 % 2]
                dma("sp", sf, xT_src[c4 * 256:(c4 + 1) * 256, :].rearrange("(kc p) t -> p kc t", p=128), [], [rsf])
                cast(xTb[:, c4 * 2:(c4 + 1) * 2, :], sf, [rsf], [r_xTb])
        else:
            xTb, r_xTb = xT_src, xT_res
        if x_src is not None:
            dma("sp", y_t, x_src.rearrange("(tb p) d -> p tb d", p=128), [], [r_y])
        ln_mark = A.off
        hT, r_hT = A.alloc([44, TT], BF16)
        GW = 256
        wg_t = [A.alloc([16, GW], BF16) for _ in range(2)]
        wu_t = [A.alloc([16, GW], BF16) for _ in range(2)]
        sg_t = [A.alloc([TT], F32) for _ in range(2)]
        for fg in range(DFF // GW):
            (wg, rwg), (wu, rwu) = wg_t[fg % 2], wu_t[fg % 2]
            dma("sp", wg, Wg[:, fg * GW:(fg + 1) * GW].rearrange("(kc p) n -> p kc n", p=128), [rWg], [rwg])
            dma("sp", wu, Wu[:, fg * GW:(fg + 1) * GW].rearrange("(kc p) n -> p kc n", p=128), [rWu], [rwu])
            for j in range(GW // 128):
                m = fg * (GW // 128) + j
                (pg, rpg), (pu, rpu) = psum[(2 * m) % 8], psum[(2 * m + 1) % 8]
                for kc in range(16):
                    mm(pg[:], wg[:, kc, j * 128:(j + 1) * 128], xTb[:, kc, :], kc == 0, kc == 15, [rwg, r_xTb], [rpg])
                for kc in range(16):
                    mm(pu[:], wu[:, kc, j * 128:(j + 1) * 128], xTb[:, kc, :], kc == 0, kc == 15, [rwu, r_xTb], [rpu])
                sg, rsg = sg_t[m % 2]
                act(sg, pg[:], AF.Silu, [rpg], [rsg])
                v_tt("dve", hT[:, m, :], sg, pu[:], ALU.mult, [rsg, rpu], [r_hT])
        for tb in range(4):
            act(y_t[:, tb, :], y_t[:, tb, :], AF.Copy, [r_y], [r_y], scale=ALPHA)
        KG = 4
        wd_t = [A.alloc([KG, 512], BF16) for _ in range(2)]
        wi = 0
        for ng in range(4):
            for kg in range(11):
                wd, rwd = wd_t[wi % 2]
                wi += 1
                dma("sp", wd, Wd[kg * KG * 128:(kg + 1) * KG * 128, ng * 512:(ng + 1) * 512]
                    .rearrange("(kc p) n -> p kc n", p=128), [rWd], [rwd])
                for tb in range(4):
                    ps, rps = psum[tb + 4 * (ng % 2)]
                    for kl in range(KG):
                        kc = kg * KG + kl
                        mm(ps[:], hT[:, kc, tb * 128:(tb + 1) * 128], wd[:, kl, :], kc == 0, kc == 43,
                           [r_hT, rwd], [rps])
            for tb in range(4):
                ps, rps = psum[tb + 4 * (ng % 2)]
                v_stt("dve", y_t[:, tb, ng * 512:(ng + 1) * 512], ps[:], 0.5, y_t[:, tb, ng * 512:(ng + 1) * 512],
                      ALU.mult, ALU.add, [rps, r_y], [r_y])
        cx.barrier()
        A.off = ln_mark
        layer_norm(A, y_t, r_y, lng, lnb)

    def layer_norm(A, y_t, r_y, lng, lnb):
        gb, r_gb = A.alloc([2, D], F32)
        dma("sp", gb[:, 0, :], lng.partition_broadcast(128), [], [r_gb])
        dma("sp", gb[:, 1, :], lnb.partition_broadcast(128), [], [r_gb])
        st, r_st = A.alloc([4, 4, 6], F32)
        mv, r_mv = A.alloc([4, 2], F32)
        rs, r_rs = A.alloc([4, 1], F32)
        for tb in range(4):
            for c in range(4):
                cx.op("dve", lambda e, tb=tb, c=c: e.bn_stats(out=st[:, tb, c, :], in_=y_t[:, tb, c * 512:(c + 1) * 512]),
                      [r_y], [r_st])
            cx.op("dve", lambda e, tb=tb: e.bn_aggr(out=mv[:, tb, :], in_=st[:, tb, :, :]), [r_st], [r_mv])
            act(rs[:, tb, :], mv[:, tb, 1:2], AF.Sqrt, [r_mv], [r_rs], scale=1.0, bias=LN_EPS)
            cx.op("dve", lambda e, tb=tb: e.reciprocal(out=rs[:, tb, :], in_=rs[:, tb, :]), [r_rs], [r_rs])
            v_ts("dve", y_t[:, tb, :], y_t[:, tb, :], mv[:, tb, 0:1], rs[:, tb, 0:1], ALU.subtract, ALU.mult,
                 [r_y, r_mv, r_rs], [r_y])
            v_tt("pool", y_t[:, tb, :], y_t[:, tb, :], gb[:, 0, :], ALU.mult, [r_y, r_gb], [r_y])
            v_tt("dve", y_t[:, tb, :], y_t[:, tb, :], gb[:, 1, :], ALU.add, [r_y, r_gb], [r_y])

    def transpose_to_bf16(y_t, r_y, xT, r_xT):
        for fc in range(16):
            ps, rps = psum[fc % 8]
            for tb in range(4):
                tr(ps[:, tb * 128:(tb + 1) * 128], y_t[:, tb, fc * 128:(fc + 1) * 128], ident_f[:],
                   [r_y, r_identf], [rps])
            cast(xT[:, fc, :], ps[:], [rps], [r_xT])

    def rms_scale(A, srcT, r_src, nchunk, width, gvec, r_g, dstT, r_dst):
        sq, r_sq = A.alloc([nchunk, TT], BF16)
        for c in range(nchunk):
            act(sq[:, c, :], srcT[:, c, :], AF.Square, [r_src], [r_sq])
        ps, rps = psum[7]
        for c in range(nchunk):
            mm(ps[:], ones_b[:], sq[:, c, :], c == 0, c == nchunk - 1, [r_ones, r_sq], [rps])
        rr, r_rr = A.alloc([TT], F32)
        act(rr, ps[:], AF.Sqrt, [rps], [r_rr], scale=1.0 / width, bias=RMS_EPS)
        cx.op("dve", lambda e: e.reciprocal(out=rr, in_=rr), [r_rr], [r_rr])
        for c in range(nchunk):
            v_stt("dve", dstT[:, c, :], srcT[:, c, :], gvec[:, c:c + 1], rr, ALU.mult, ALU.mult,
                  [r_src, r_g, r_rr], [r_dst])

    def rope_comb(eng, dst, xa, xb, cs, npart, reads, writes, tmp):
        v_tt("dve", tmp[0:npart, :], xb, cs[0:npart, 1, :], ALU.mult, reads, [writes[1]])
        v_tt("dve", dst, xa, cs[0:npart, 0, :], ALU.mult, reads, [writes[0]])
        v_tt(eng, dst, dst, tmp[0:npart, :], ALU.add, [writes[0], writes[1]], [writes[0]])

    for t in range(min(NT_ALL, NT1)):
        A.reset()
        t0 = t * TT
        y_t, r_y = A.alloc([4, D], F32)
        ffn_ln(A, xT_all[:, t0:t0 + TT], x_all[t0:t0 + TT, :], "ffn1", ln_in["ln1_g"], ln_in["ln1_b"], y_t, r_y)
        cx.barrier()
        A.off = 4 * D * 2
        x1T, r_x1T = A.alloc([16, TT], BF16)
        transpose_to_bf16(y_t, r_y, x1T, r_x1T)
        cs64, r_cs64 = A.alloc([2, TT], F32, parts=64)
        cs128, r_cs128 = A.alloc([2, TT], F32)
        dma("sp", cs64, cs64_all[:, :, t0:t0 + TT], [], [r_cs64])
        dma("sp", cs128, cs128_all[:, :, t0:t0 + TT], [], [r_cs128])
        wkv_t = [A.alloc([16, 256], BF16) for _ in range(2)]
        tmpr, r_tmpr = A.alloc([TT], F32)
        ob_t = [A.alloc([TT], BF16) for _ in range(2)]
        ckvT, r_ckvT = A.alloc([2, TT], F32)
        rW = dres("b_w_kvf")
        oi = 0
        def load_w(col0, ncol, idx):
            w, rw = wkv_t[idx % 2]
            dma("sp", w[:, :, 0:ncol], wb["w_kvf"][:, col0:col0 + ncol].rearrange("(kc p) n -> p kc n", p=128),
                [rW], [rw])
            return w, rw
        wi = 0
        w, rw = load_w(0, 256, wi); wi += 1
        for c in range(2):
            ps, rps = psum[c]
            for kc in range(16):
                mm(ps[:], w[:, kc, c * 128:(c + 1) * 128], x1T[:, kc, :], kc == 0, kc == 15, [rw, r_x1T], [rps])
            act(ckvT[:, c, :], ps[:], AF.Copy, [rps], [r_ckvT])
        w, rw = load_w(256, 128, wi); wi += 1
        psa, rpsa = psum[2]
        psb, rpsb = psum[3]
        for kc in range(16):
            mm(psa[0:64, :], w[:, kc, 0:64], x1T[:, kc, :], kc == 0, kc == 15, [rw, r_x1T], [rpsa])
        for kc in range(16):
            mm(psb[0:64, :], w[:, kc, 64:128], x1T[:, kc, :], kc == 0, kc == 15, [rw, r_x1T], [rpsb])
        ob, rob = ob_t[oi % 2]; oi += 1
        rope_comb("dve", ob[0:64, :], psa[0:64, :], psb[0:64, :], cs64, 64, [rpsa, rpsb, r_cs64], [rob, r_tmpr], tmpr)
        dma("sp", KR[:, t0:t0 + TT], ob[0:64, :], [rob], [dres("KR")])
        for gi, (dst, nm) in enumerate(((KC, "KC"), (KS, "KS"), (KW, "KW"))):
            for kh in range(2):
                w, rw = load_w(384 + 256 * (gi * 2 + kh), 256, wi); wi += 1
                psa, rpsa = psum[(2 * wi) % 8]
                psb, rpsb = psum[(2 * wi + 1) % 8]
                for kc in range(16):
                    mm(psa[:], w[:, kc, 0:128], x1T[:, kc, :], kc == 0, kc == 15, [rw, r_x1T], [rpsa])
                for kc in range(16):
                    mm(psb[:], w[:, kc, 128:256], x1T[:, kc, :], kc == 0, kc == 15, [rw, r_x1T], [rpsb])
                ob, rob = ob_t[oi % 2]; oi += 1
                rope_comb("dve", ob, psa[:], psb[:], cs128, 128, [rpsa, rpsb, r_cs128], [rob, r_tmpr], tmpr)
                dma("sp", dst[kh, :, t0:t0 + TT], ob, [rob], [dres(nm)])
        w, rw = load_w(1920, 256, wi); wi += 1
        for kh in range(2):
            ps, rps = psum[4 + kh]
            for kc in range(16):
                mm(ps[:], w[:, kc, kh * 128:(kh + 1) * 128], x1T[:, kc, :], kc == 0, kc == 15, [rw, r_x1T], [rps])
            ob, rob = ob_t[oi % 2]; oi += 1
            cast(ob, ps[:], [rps], [rob])
            dma("sp", VC[kh, :, t0:t0 + TT], ob, [rob], [dres("VC")])
        wt_t = [A.alloc([16, 256], BF16) for _ in range(2)]
        vo, r_vo = A.alloc([4, 512], BF16)
        for hf in range(2):
            w, rw = wt_t[hf]
            dma("sp", w, wb["w_kvt"][:, hf * 256:(hf + 1) * 256].rearrange("(kc p) n -> p kc n", p=128),
                [dres("b_w_kvt")], [rw])
            for tb in range(4):
                ps, rps = psum[tb + 4 * hf]
                for kc in range(16):
                    mm(ps[:, 0:256], x1T[:, kc, tb * 128:(tb + 1) * 128], w[:, kc, :], kc == 0, kc == 15,
                       [rw, r_x1T], [rps])
                cast(vo[:, tb, hf * 256:(hf + 1) * 256], ps[:, 0:256], [rps], [r_vo])
        dma("sp", VSW[t0:t0 + TT, :].rearrange("(tb p) n -> p tb n", p=128), vo, [r_vo], [dres("VSW")])
        ckvn, r_ckvn = A.alloc([2, TT], BF16)
        rms_scale(A, ckvT, r_ckvT, 2, 256.0, kvg, r_kvg, ckvn, r_ckvn)
        wukv, r_wukv = A.alloc([2, 2048], BF16)
        dma("sp", wukv, wb["w_ukv"].rearrange("(kc p) n -> p kc n", p=128), [dres("b_w_ukv")], [r_wukv])
        for h in range(8):
            ps, rps = psum[h % 4]
            for kc in range(2):
                mm(ps[:], wukv[:, kc, h * 256:h * 256 + 128], ckvn[:, kc, :], kc == 0, kc == 1, [r_wukv, r_ckvn], [rps])
            ob, rob = ob_t[oi % 2]; oi += 1
            cast(ob, ps[:], [rps], [rob])
            dma("sp", KN[h, :, t0:t0 + TT], ob, [rob], [dres("KN")])
        vm, r_vm = A.alloc([4, 1024], BF16)
        for tb in range(4):
            for hg in range(2):
                ps, rps = psum[4 + (tb * 2 + hg) % 4]
                for kc in range(2):
                    rhs = wukv[:, kc, hg * 1024:(hg + 1) * 1024].rearrange("p (h c) -> p h c", c=256)[:, :, 128:256]
                    mm(ps[:].rearrange("p (h c) -> p h c", c=128), ckvn[:, kc, tb * 128:(tb + 1) * 128], rhs,
                       kc == 0, kc == 1, [r_wukv, r_ckvn], [rps])
                cast(vm[:, tb, hg * 512:(hg + 1) * 512], ps[:], [rps], [r_vm])
        dma("sp", VM[t0:t0 + TT, :].rearrange("(tb p) n -> p tb n", p=128), vm, [r_vm], [dres("VM")])
        cx.barrier()
    cx.barrier()

    if STOP == 1:
        return nc, es, cx
    A.reset()
    w1, r_w1 = A.alloc([32, 256], BF16)
    w2, r_w2 = A.alloc([2, 128], BF16)
    pef, r_pef = A.alloc([32], F32)
    peb, r_peb = A.alloc([32], BF16)
    bias, r_bias = A.alloc([2], F32)
    tT, r_tT = A.alloc([S], BF16)
    hid, r_hid = A.alloc([2, 1024], BF16)
    u1, r_u1 = A.alloc([512], F32)
    u2, r_u2 = A.alloc([512], F32)
    for ti, (srcD, nm, w1n, w2n, peT_in) in enumerate(((KC, "KC", "w1_k", "w2_k", peT_k_in),
                                                         (VC, "VC", "w1_v", "w2_v", peT_v_in))):
        dma("sp", w1, wb[w1n].rearrange("(l p) n -> p l n", p=128), [dres("b_" + w1n)], [r_w1])
        dma("sp", w2, wb[w2n].rearrange("(c p) n -> p c n", p=128), [dres("b_" + w2n)], [r_w2])
        dma("sp", pef, peT_in, [], [r_pef])
        v_copy("dve", peb, pef, [r_pef], [r_peb])
        for hc in range(2):
            ps, rps = psum[hc]
            for l in range(32):
                mm(ps[:, 0:1], w1[:, l, hc * 128:(hc + 1) * 128], peb[:, l:l + 1], l == 0, l == 31, [r_w1, r_peb], [rps])
            v_copy("dve", bias[:, hc:hc + 1], ps[:, 0:1], [rps], [r_bias])
        for kh in range(2):
            dma("sp", tT, srcD[kh], [dres(nm)], [r_tT])
            cx.op("pool", lambda e: e.memset(hid, 0.0), [], [r_hid])
            for cg in range(2):
                ncs = 512 if cg == 0 else 511
                for hc in range(2):
                    ps, rps = psum[2 + (cg * 2 + hc) % 4]
                    for l in range(32):
                        st_ = l + 16 * 512 * cg
                        mm(ps[:, 0:ncs], w1[:, l, hc * 128:(hc + 1) * 128], tT[:, st_:st_ + 16 * (ncs - 1) + 1:16],
                           l == 0, l == 31, [r_w1, r_tT], [rps])
                    act(u1[:, 0:ncs], ps[:, 0:ncs], AF.Identity, [rps, r_bias], [r_u1], bias=bias[:, hc:hc + 1])
                    v_tt("dve", u2[:, 0:ncs], u1[:, 0:ncs], u1[:, 0:ncs], ALU.mult, [r_u1], [r_u2])
                    v_ts("dve", u2[:, 0:ncs], u2[:, 0:ncs], 0.044715, 1.0, ALU.mult, ALU.add, [r_u2], [r_u2])
                    v_tt("dve", u2[:, 0:ncs], u2[:, 0:ncs], u1[:, 0:ncs], ALU.mult, [r_u2, r_u1], [r_u2])
                    act(u2[:, 0:ncs], u2[:, 0:ncs], AF.Sigmoid, [r_u2], [r_u2], scale=1.5957691216057308)
                    v_tt("dve", hid[:, hc, cg * 512:cg * 512 + ncs], u2[:, 0:ncs], u1[:, 0:ncs], ALU.mult,
                         [r_u2, r_u1], [r_hid])
            if ti == 0:
                for cg in range(2):
                    ps, rps = psum[6 + cg]
                    for hc in range(2):
                        mm(ps[:], w2[:, hc, :], hid[:, hc, cg * 512:(cg + 1) * 512], hc == 0, hc == 1, [r_w2, r_hid], [rps])
                    cast(kcT[:, kh, cg * 512:(cg + 1) * 512], ps[:], [rps], [r_kcT])
            else:
                for jb in range(8):
                    ps, rps = psum[6 + jb % 2]
                    for hc in range(2):
                        mm(ps[:, 0:128], hid[:, hc, jb * 128:(jb + 1) * 128], w2[:, hc, :], hc == 0, hc == 1,
                           [r_w2, r_hid], [rps])
                    cast(vcc[:, kh, jb, :], ps[:, 0:128], [rps], [r_vcc])
    cx.barrier()

    if STOP == 3:
        return nc, es, cx
    def flash_branch(A, K_tile_fn, nkb, q_ap, r_q, extra_q, v_fn, mask_fn, scale, po, rpo, psm, rpsm,
                     sbanks=(4, 5, 6), depth=2):
        pend = {}
        for step in range(nkb + depth):
            kb = step
            if kb < nkb:
                kT, r_kT, kT2, r_kT2 = K_tile_fn(kb)
                vv, r_vv = v_fn(kb)
                ps, rps = psum[sbanks[kb % len(sbanks)]]
                mm(ps[:], kT, q_ap, True, kT2 is None, [r_kT, r_q], [rps])
                if kT2 is not None:
                    mm(ps[:], kT2, extra_q, False, True, [r_kT2, r_q], [rps])
                e_, r_e = e_tiles[kb % len(e_tiles)]
                act(e_, ps[:], AF.Exp, [rps], [r_e], scale=scale)
                mask_fn(kb, e_, r_e)
                pend[kb] = (e_, r_e, vv, r_vv)
            kb = step - depth
            if kb >= 0:
                e_, r_e, vv, r_vv = pend.pop(kb)
                mm(po[:], vv, e_, kb == 0, kb == nkb - 1, [r_vv, r_e], [rpo])
                mm(psm[:], ones_b[:], e_, kb == 0, kb == nkb - 1, [r_ones, r_e], [rpsm])

    e_tiles = None

    for si in range(min(NSLOT, KSLOTS)):
        A.reset()
        m0 = si * TT
        y_t, r_y = A.alloc([4, D], F32)
        ffn_ln(A, xT_my[:, m0:m0 + TT], x_my[m0:m0 + TT, :], "ffn1", ln_in["ln1_g"], ln_in["ln1_b"], y_t, r_y)
        cx.barrier()
        A.off = 4 * D * 2
        x1T, r_x1T = A.alloc([16, TT], BF16)
        transpose_to_bf16(y_t, r_y, x1T, r_x1T)
        if si == 0:
            dbg("x1", y_t, [r_y])
            dbg("x1T", x1T, [r_x1T])
        cs64, r_cs64 = A.alloc([2, TT], F32, parts=64)
        cs128, r_cs128 = A.alloc([2, TT], F32)
        dma("sp", cs64, cs64_my[:, :, m0:m0 + TT], [], [r_cs64])
        dma("sp", cs128, cs128_my[:, :, m0:m0 + TT], [], [r_cs128])
        tmpr, r_tmpr = A.alloc([TT], F32)
        qn, r_qn = A.alloc([8, TT], BF16)
        qr, r_qr = A.alloc([8, TT], BF16, parts=64)
        qs, r_qs = A.alloc([8, TT], BF16)
        gT, r_gT = A.alloc([TT], F32, parts=24)
        mixT, r_mixT = A.alloc([16, TT], BF16)
        mark_q = A.off
        wq_t = [A.alloc([16, 256], BF16) for _ in range(2)]
        cqT, r_cqT = A.alloc([4, TT], F32)
        rWq = dres("b_w_q")
        wi = 0
        for cg in range(2):
            w, rw = wq_t[wi % 2]; wi += 1
            dma("sp", w, wb["w_q"][:, cg * 256:(cg + 1) * 256].rearrange("(kc p) n -> p kc n", p=128), [rWq], [rw])
            for c in range(2):
                ps, rps = psum[(cg * 2 + c) % 4]
                for kc in range(16):
                    mm(ps[:], w[:, kc, c * 128:(c + 1) * 128], x1T[:, kc, :], kc == 0, kc == 15, [rw, r_x1T], [rps])
                act(cqT[:, cg * 2 + c, :], ps[:], AF.Copy, [rps], [r_cqT])
        for h in range(8):
            w, rw = wq_t[wi % 2]; wi += 1
            dma("sp", w, wb["w_q"][:, 512 + h * 256:512 + (h + 1) * 256].rearrange("(kc p) n -> p kc n", p=128),
                [rWq], [rw])
            psa, rpsa = psum[(2 * h) % 4]
            psb, rpsb = psum[(2 * h + 1) % 4]
            for kc in range(16):
                mm(psa[:], w[:, kc, 0:128], x1T[:, kc, :], kc == 0, kc == 15, [rw, r_x1T], [rpsa])
            for kc in range(16):
                mm(psb[:], w[:, kc, 128:256], x1T[:, kc, :], kc == 0, kc == 15, [rw, r_x1T], [rpsb])
            rope_comb("dve", qs[:, h, :], psa[:], psb[:], cs128, 128, [rpsa, rpsb, r_cs128], [r_qs, r_tmpr], tmpr)
        w, rw = wq_t[wi % 2]; wi += 1
        dma("sp", w[:, :, 0:24], wb["w_q"][:, 2560:2584].rearrange("(kc p) n -> p kc n", p=128), [rWq], [rw])
        ps, rps = psum[4]
        for kc in range(16):
            mm(ps[0:24, :], w[:, kc, 0:24], x1T[:, kc, :], kc == 0, kc == 15, [rw, r_x1T], [rps])
        act(gT, ps[0:24, :], AF.Sigmoid, [rps, r_gateb], [r_gT], bias=gateb[:, 0:1])
        cqn, r_cqn = A.alloc([4, TT], BF16)
        rms_scale(A, cqT, r_cqT, 4, 512.0, qg, r_qg, cqn, r_cqn)
        wuq, r_wuq = A.alloc([4, 2048], BF16)
        dma("sp", wuq, wb["w_uq"].rearrange("(kc p) n -> p kc n", p=128), [dres("b_w_uq")], [r_wuq])
        for h in range(8):
            ps, rps = psum[h % 2]
            for kc in range(4):
                mm(ps[:], wuq[:, kc, h * 256:h * 256 + 128], cqn[:, kc, :], kc == 0, kc == 3, [r_wuq, r_cqn], [rps])
            cast(qn[:, h, :], ps[:], [rps], [r_qn])
            psa, rpsa = psum[2 + (2 * h) % 4]
            psb, rpsb = psum[2 + (2 * h + 1) % 4]
            for kc in range(4):
                mm(psa[0:64, :], wuq[:, kc, h * 256 + 128:h * 256 + 192], cqn[:, kc, :], kc == 0, kc == 3,
                   [r_wuq, r_cqn], [rpsa])
            for kc in range(4):
                mm(psb[0:64, :], wuq[:, kc, h * 256 + 192:h * 256 + 256], cqn[:, kc, :], kc == 0, kc == 3,
                   [r_wuq, r_cqn], [rpsb])
            rope_comb("dve", qr[:, h, :], psa[0:64, :], psb[0:64, :], cs64, 64, [rpsa, rpsb, r_cs64],
                      [r_qr, r_tmpr], tmpr)
        if si == 0:
            dbg("qn", qn, [r_qn]); dbg("qr", qr, [r_qr]); dbg("qs", qs, [r_qs]); dbg("gT", gT, [r_gT])
            dbg("cqT", cqT, [r_cqT])
        cx.barrier()
        if STOP == 4:
            return nc, es, cx

        A.off = mark_q
        e_tiles = [A.alloc([TT], BF16) for _ in range(6)]
        kt_t = [A.alloc([TT], BF16) for _ in range(3)]
        kr_t = [A.alloc([TT], BF16, parts=64) for _ in range(3)]
        vt_t = [A.alloc([4, 128], BF16) for _ in range(3)]
        rsum, r_rsum = A.alloc([TT], F32)
        gbc, r_gbc = A.alloc([TT], F32)
        nkt_all = 8 * (si + 1)

        def causal_mask(kbz, e_, r_e):
            v_stt("dve", e_, iota_pf[:], thr[:, kbz + 4:kbz + 5], e_, ALU.is_le, ALU.mult, [r_iota, r_thr, r_e], [r_e])

        for h in range(8):
            po, rpo = psum[0 + 2 * (h % 2)]
            psm, rpsm = psum[1 + 2 * (h % 2)]
            nkb = nkt_all * 4
            state = {}

            def K_fn(kb, h=h, state=state):
                kt_i, kl = divmod(kb, 4)
                if kl == 0:
                    kt, rkt = kt_t[kt_i % 3]
                    kr_, rkr = kr_t[kt_i % 3]
                    vt, rvt = vt_t[kt_i % 3]
                    k0 = kt_i * TT
                    dma("sp", kt, KN[h, :, k0:k0 + TT], [dres("KN")], [rkt])
                    dma("sp", kr_, KR[:, k0:k0 + TT], [dres("KR")], [rkr])
                    dma("sp", vt, VM[k0:k0 + TT, h * 128:(h + 1) * 128].rearrange("(kb p) n -> p kb n", p=128),
                        [dres("VM")], [rvt])
                    state["cur"] = (kt, rkt, kr_, rkr, vt, rvt)
                kt, rkt, kr_, rkr, vt, rvt = state["cur"]
                return kt[:, kl * 128:(kl + 1) * 128], rkt, kr_[:, kl * 128:(kl + 1) * 128], rkr

            def V_fn(kb, state=state):
                kt, rkt, kr_, rkr, vt, rvt = state["cur"]
                return vt[:, kb % 4, :], rvt

            def M_fn(kb, e_, r_e, si=si):
                if kb >= 32 * si:
                    causal_mask(kb - 32 * si, e_, r_e)

            flash_branch(A, K_fn, nkb, qn[:, h, :], r_qn, qr[:, h, :], V_fn, M_fn, MLA_SCALE, po, rpo, psm, rpsm,
                         sbanks=(4, 5, 6, 7))
            v_ts("dve", rsum, psm[:], 1e-30, None, ALU.max, None, [rpsm], [r_rsum])
            cx.op("dve", lambda e: e.reciprocal(out=rsum, in_=rsum), [r_rsum], [r_rsum])
            v_tt("dve", mixT[:, h, :], po[:], rsum, ALU.mult, [rpo, r_rsum], [r_mixT])

        if si == 0:
            dbg("mix_mla", mixT, [r_mixT])
        if STOP == 5:
            cx.barrier()
            return nc, es, cx
        impT, r_impT = A.alloc([2, TT], F32)
        ocn, r_ocn = A.alloc([4, TT], F32)
        onsa, r_onsa = A.alloc([TT], F32)
        tmpf, r_tmpf = A.alloc([TT], F32)
        selT, r_selT = A.alloc([2, TT], BF16)
        msb_t = [A.alloc([TT], BF16) for _ in range(2)]
        sbias, r_sbias = A.alloc([4, 256], F32)
        dma("sp", sbias, sbias_in[:, si * 4:(si + 1) * 4, :], [], [r_sbias])
        score, r_score = A.alloc([256], F32)
        work, r_work = A.alloc([256], F32)
        m8, r_m8 = A.alloc([16], F32)
        selq, r_selq = A.alloc([256], BF16)
        selq2, r_selq2 = A.alloc([256], F32)

        gm, r_gm = A.alloc([TT], F32, parts=24)

        def gate_bcast(row):
            ps, rps = psum[7]
            v_ts("dve", gm, gT, ident_f[0:24, row:row + 1], None, ALU.mult, None, [r_gT, r_identf], [r_gm])
            mm(ps[:], ones_f[:], gm, True, True, [r_onesf, r_gm], [rps])
            return ps, rps

        def finish_branch(po, rpo, psm, rpsm, row, dst, r_dst, accumulate):
            v_ts("dve", rsum, psm[:], 1e-30, None, ALU.max, None, [rpsm], [r_rsum])
            cx.op("dve", lambda e: e.reciprocal(out=rsum, in_=rsum), [r_rsum], [r_rsum])
            psg, rpsg = gate_bcast(row)
            v_tt("dve", gbc, psg[:], rsum, ALU.mult, [rpsg, r_rsum], [r_gbc])
            if accumulate:
                v_tt("dve", tmpf, po[:], gbc, ALU.mult, [rpo, r_gbc], [r_tmpf])
                v_tt("pool", dst, dst, tmpf, ALU.add, [r_dst, r_tmpf], [r_dst])
            else:
                v_tt("dve", dst, po[:], gbc, ALU.mult, [rpo, r_gbc], [r_dst])

        for kh in range(2):
            ncb = 2 * si + 2
            for g in range(4):
                hq = kh * 4 + g
                po, rpo = psum[0]
                psm, rpsm = psum[1]
                pi0, rpi0 = psum[2]
                pi1, rpi1 = psum[3]
                for jb in range(ncb):
                    ps, rps = psum[4 + jb % 3]
                    mm(ps[:], kcT[:, kh, jb * 128:(jb + 1) * 128], qs[:, hq, :], True, True, [r_kcT, r_qs], [rps])
                    e_, r_e = e_tiles[jb % 3]
                    act(e_, ps[:], AF.Exp, [rps], [r_e], scale=NSA_SCALE)
                    if jb >= 2 * si - 1:
                        col = 36 + 3 * si + (jb - (2 * si - 1))
                        v_stt("dve", e_, iota_16[:], thr[:, col:col + 1], e_, ALU.is_le, ALU.mult,
                              [r_iota16, r_thr, r_e], [r_e])
                    first, last = jb == 0, jb == ncb - 1
                    mm(po[:], vcc[:, kh, jb, :], e_, first, last, [r_vcc, r_e], [rpo])
                    mm(psm[:], ones_b[:], e_, first, last, [r_ones, r_e], [rpsm])
                    mm(pi0[:], ovb[:, jb * 256:jb * 256 + 128], e_, first, last, [r_ov, r_e], [rpi0])
                    mm(pi1[:], ovb[:, jb * 256 + 128:jb * 256 + 256], e_, first, last, [r_ov, r_e], [rpi1])
                v_ts("dve", rsum, psm[:], 1e-30, None, ALU.max, None, [rpsm], [r_rsum])
                cx.op("dve", lambda e: e.reciprocal(out=rsum, in_=rsum), [r_rsum], [r_rsum])
                for jc, (pi, rpi) in enumerate(((pi0, rpi0), (pi1, rpi1))):
                    if g == 0:
                        v_tt("dve", impT[:, jc, :], pi[:], rsum, ALU.mult, [rpi, r_rsum], [r_impT])
                    else:
                        v_tt("dve", tmpf, pi[:], rsum, ALU.mult, [rpi, r_rsum], [r_tmpf])
                        v_tt("pool", impT[:, jc, :], impT[:, jc, :], tmpf, ALU.add, [r_impT, r_tmpf], [r_impT])
                psg, rpsg = gate_bcast(0 * 8 + hq)
                v_tt("dve", gbc, psg[:], rsum, ALU.mult, [rpsg, r_rsum], [r_gbc])
                v_tt("dve", ocn[:, g, :], po[:], gbc, ALU.mult, [rpo, r_gbc], [r_ocn])
            for qb in range(4):
                ps, rps = psum[4 + qb % 3]
                for jc in range(2):
                    tr(ps[:, jc * 128:(jc + 1) * 128], impT[:, jc, qb * 128:(qb + 1) * 128], ident_f[:],
                       [r_impT, r_identf], [rps])
                v_tt("dve", score, ps[:, 0:256], sbias[:, qb, :], ALU.add, [rps, r_sbias], [r_score])
                cx.op("dve", lambda e: e.max(out=m8[:, 0:8], in_=score), [r_score], [r_m8])
                cx.op("dve", lambda e: e.match_replace(out=work, in_to_replace=m8[:, 0:8], in_values=score,
                                                       imm_value=-3.0e38), [r_score, r_m8], [r_work])
                cx.op("dve", lambda e: e.max(out=m8[:, 8:16], in_=work), [r_work], [r_m8])
                v_ts("dve", selq2, score, m8[:, 15:16], None, ALU.is_ge, None, [r_score, r_m8], [r_selq2])
                v_stt("dve", selq, score, -1.0e29, selq2, ALU.is_gt, ALU.mult, [r_score, r_selq2], [r_selq])
                pst, rpst = psum[7]
                pstb = pst[:].bitcast(BF16)
                for jc in range(2):
                    tr(pstb[:, jc * 128:(jc + 1) * 128], selq[:, jc * 128:(jc + 1) * 128], ident_b[:],
                       [r_selq, r_identb], [rpst])
                for jc in range(2):
                    v_copy("dve", selT[:, jc, qb * 128:(qb + 1) * 128], pstb[:, jc * 128:(jc + 1) * 128],
                           [rpst], [r_selT])
            for g in range(4):
                hq = kh * 4 + g
                po, rpo = psum[0]
                psm, rpsm = psum[1]
                nkb = nkt_all * 4
                state = {}

                def K_fn(kb, kh=kh, state=state):
                    kt_i, kl = divmod(kb, 4)
                    if kl == 0:
                        kt, rkt = kt_t[kt_i % 3]
                        vt, rvt = vt_t[kt_i % 3]
                        k0 = kt_i * TT
                        dma("sp", kt, KS[kh, :, k0:k0 + TT], [dres("KS")], [rkt])
                        dma("sp", vt, VSW[k0:k0 + TT, kh * 128:(kh + 1) * 128].rearrange("(kb p) n -> p kb n", p=128),
                            [dres("VSW")], [rvt])
                        state["cur"] = (kt, rkt, vt, rvt)
                    kt, rkt, vt, rvt = state["cur"]
                    return kt[:, kl * 128:(kl + 1) * 128], rkt, None, None

                def V_fn(kb, state=state):
                    kt, rkt, vt, rvt = state["cur"]
                    return vt[:, kb % 4, :], rvt

                def M_fn(kb, e_, r_e, si=si):
                    pm, rpm = psum[6 + kb % 2]
                    mm(pm[:], eexp[:, (kb % 64) * 128:(kb % 64 + 1) * 128], selT[:, kb // 64, :], True, True,
                       [r_eexp, r_selT], [rpm])
                    if kb >= 32 * si:
                        causal_mask(kb - 32 * si, e_, r_e)
                    v_tt("dve", e_, e_, pm[:], ALU.mult, [r_e, rpm], [r_e])

                flash_branch(A, K_fn, nkb, qs[:, hq, :], r_qs, None, V_fn, M_fn, NSA_SCALE, po, rpo, psm, rpsm,
                             sbanks=(2, 3, 4, 5))
                v_copy("pool", onsa, ocn[:, g, :], [r_ocn], [r_onsa])
                finish_branch(po, rpo, psm, rpsm, 1 * 8 + hq, onsa, r_onsa, True)
                po, rpo = psum[0]
                psm, rpsm = psum[1]
                kb_lo = 32 * si - 4 if si > 0 else 0
                nkb = 32 * si + 32 - kb_lo
                state = {}

                def K_fn(kb, kh=kh, state=state, kb_lo=kb_lo):
                    kt_i, kl = divmod(kb, 4)
                    if kl == 0:
                        kt, rkt = kt_t[kt_i % 3]
                        vt, rvt = vt_t[kt_i % 3]
                        k0 = kb_lo * 128 + kt_i * TT
                        dma("sp", kt, KW[kh, :, k0:k0 + TT], [dres("KW")], [rkt])
                        dma("sp", vt, VSW[k0:k0 + TT, 256 + kh * 128:256 + (kh + 1) * 128]
                            .rearrange("(kb p) n -> p kb n", p=128), [dres("VSW")], [rvt])
                        state["cur"] = (kt, rkt, vt, rvt)
                    kt, rkt, vt, rvt = state["cur"]
                    return kt[:, kl * 128:(kl + 1) * 128], rkt, None, None

                def V_fn(kb, state=state):
                    kt, rkt, vt, rvt = state["cur"]
                    return vt[:, kb % 4, :], rvt

                def M_fn(kb, e_, r_e, si=si, kb_lo=kb_lo):
                    kbz = kb + kb_lo - 32 * si
                    if kbz >= 0:
                        causal_mask(kbz, e_, r_e)
                    wcol = 48 + kbz + 4
                    v_stt("dve", e_, iota_pf[:], thr[:, wcol:wcol + 1], e_, ALU.is_gt, ALU.mult,
                          [r_iota, r_thr, r_e], [r_e])

                flash_branch(A, K_fn, nkb, qs[:, hq, :], r_qs, None, V_fn, M_fn, NSA_SCALE, po, rpo, psm, rpsm,
                             sbanks=(2, 3, 4, 5, 6))
                finish_branch(po, rpo, psm, rpsm, 2 * 8 + hq, onsa, r_onsa, True)
                v_copy("dve", mixT[:, 8 + hq, :], onsa, [r_onsa], [r_mixT])
        cx.barrier()

        if si == 0:
            dbg("mix_all", mixT, [r_mixT])
            dbg("kcT", kcT[:], [r_kcT]); dbg("vcc", vcc[:], [r_vcc])
            dbg("KN0", KN[:, :, 0:512].rearrange("h p t -> p h t"), [dres("KN")])
            dbg("KR0", KR[:, 0:512], [dres("KR")])
            dbg("KS0", KS[:, :, 0:512].rearrange("h p t -> p h t"), [dres("KS")])
            dbg("KW0", KW[:, :, 0:512].rearrange("h p t -> p h t"), [dres("KW")])
            dbg("KC0", KC[:, :, 0:512].rearrange("h p t -> p h t"), [dres("KC")])
            dbg("VC0", VC[:, :, 0:512].rearrange("h p t -> p h t"), [dres("VC")])
            dbg("VM0", VM[0:512, :].rearrange("(tb p) n -> p tb n", p=128), [dres("VM")])
            dbg("VSW0", VSW[0:512, :].rearrange("(tb p) n -> p tb n", p=128), [dres("VSW")])
        if STOP == 6:
            return nc, es, cx
        A.off = mark_q
        for tb in range(4):
            act(y_t[:, tb, :], y_t[:, tb, :], AF.Copy, [r_y], [r_y], scale=ALPHA)
        wo_t = [A.alloc([16, 512], BF16) for _ in range(2)]
        for ng in range(4):
            wo, rwo = wo_t[ng % 2]
            dma("sp", wo, wb["w_out"][:, ng * 512:(ng + 1) * 512].rearrange("(kc p) n -> p kc n", p=128),
                [dres("b_w_out")], [rwo])
            for tb in range(4):
                ps, rps = psum[tb + 4 * (ng % 2)]
                for kc in range(16):
                    mm(ps[:], mixT[:, kc, tb * 128:(tb + 1) * 128], wo[:, kc, :], kc == 0, kc == 15, [r_mixT, rwo], [rps])
                v_tt("dve", y_t[:, tb, ng * 512:(ng + 1) * 512], ps[:], y_t[:, tb, ng * 512:(ng + 1) * 512], ALU.add,
                     [rps, r_y], [r_y])
        layer_norm(A, y_t, r_y, ln_in["ln2_g"], ln_in["ln2_b"])
        if si == 0:
            dbg("x2", y_t, [r_y])
        cx.barrier()
        A.off = 4 * D * 2
        x2T, r_x2T = A.alloc([16, TT], BF16)
        transpose_to_bf16(y_t, r_y, x2T, r_x2T)
        ffn_ln(A, x2T, None, "ffn2", ln_in["ln3_g"], ln_in["ln3_b"], y_t, r_y, xT_res=r_x2T)
        dma("sp", out[m0:m0 + TT, :].rearrange("(tb p) d -> p tb d", p=128), y_t, [r_y], [dres("out")])
        cx.barrier()

    cx.barrier()
    return nc, es, cx


def emit_program():
    nc, es, cx = build_program()
    with nc.Block() as block:
        @block.tensor
        def _(e):
            cx.replay("pe", e)

        @block.scalar
        def _(e):
            cx.replay("act", e)

        @block.vector
        def _(e):
            cx.replay("dve", e)

        @block.gpsimd
        def _(e):
            cx.replay("pool", e)

        @block.sync
        def _(e):
            cx.replay("sp", e)
    return nc


def _rope_tables(pos, d):
    half = d // 2
    inv = (np.float32(10000.0) ** (-np.arange(half, dtype=np.float32) * np.float32(2.0 / d))).astype(np.float32)
    ang = pos.astype(np.float32)[None, :] * inv[:, None]
    cos, sin = np.cos(ang).astype(np.float32), np.sin(ang).astype(np.float32)
    t = np.empty((d, 2, pos.shape[0]), np.float32)
    t[:half, 0], t[half:, 0] = cos, cos
    t[:half, 1], t[half:, 1] = -sin, sin
    return t


def _swap_halves(w):
    h = w.shape[1] // 2
    return np.concatenate([w[:, h:], w[:, :h]], axis=1)


def kernel(**inp):
    f = lambda k: np.ascontiguousarray(np.asarray(inp[k], dtype=np.float32)[0])
    x = f("x")
    w_in = f("w_in")
    c_q, c_kv, k_rope = w_in[:, 0:512], w_in[:, 512:768], w_in[:, 768:832]
    q_nsa = w_in[:, 832:1856]
    k_cmp, v_cmp = w_in[:, 1856:2112], w_in[:, 2112:2368]
    k_sel, v_sel = w_in[:, 2368:2624], w_in[:, 2624:2880]
    k_win, v_win = w_in[:, 2880:3136], w_in[:, 3136:3392]
    gl = w_in[:, 3392:3416]
    cols = [c_kv, k_rope, _swap_halves(k_rope)]
    for kk in (k_cmp, k_sel, k_win):
        for kh in range(2):
            blk = kk[:, kh * 128:(kh + 1) * 128]
            cols += [blk, _swap_halves(blk)]
    cols.append(v_cmp)
    w_kvf = np.ascontiguousarray(np.concatenate(cols, axis=1))
    assert w_kvf.shape[1] == NKV_F
    w_kvt = np.ascontiguousarray(np.concatenate([v_sel, v_win], axis=1))
    qcols = [c_q]
    for h in range(8):
        blk = q_nsa[:, h * 128:(h + 1) * 128]
        qcols += [blk, _swap_halves(blk)]
    qcols.append(gl)
    w_q = np.ascontiguousarray(np.concatenate(qcols, axis=1))
    wuq = f("mla_w_uq")
    ucols = []
    for h in range(8):
        blk = wuq[:, h * 192:(h + 1) * 192]
        ucols += [blk[:, :128], blk[:, 128:], _swap_halves(blk[:, 128:])]
    w_uq = np.ascontiguousarray(np.concatenate(ucols, axis=1))
    shared = {
        "x_all": x, "xT_all": np.ascontiguousarray(x.T),
        "ffn1_w_gate": f("ffn1_w_gate"), "ffn1_w_up": f("ffn1_w_up"), "ffn1_w_down": f("ffn1_w_down"),
        "ffn2_w_gate": f("ffn2_w_gate"), "ffn2_w_up": f("ffn2_w_up"), "ffn2_w_down": f("ffn2_w_down"),
        "w_kvf": w_kvf, "w_kvt": w_kvt, "w_q": w_q, "w_uq": w_uq, "w_ukv": f("mla_w_ukv"), "w_out": f("w_out"),
        "w1_k": f("nsa_cmp_w1_k"), "w1_v": f("nsa_cmp_w1_v"), "w2_k": f("nsa_cmp_w2_k"), "w2_v": f("nsa_cmp_w2_v"),
        "qg": np.ascontiguousarray(f("mla_q_norm_g").reshape(4, 128).T),
        "kvg": np.ascontiguousarray(f("mla_kv_norm_g").reshape(2, 128).T),
        "gateb": np.ascontiguousarray(f("nsa_gate_b").reshape(24, 1)),
        "peT_k": np.ascontiguousarray(f("nsa_cmp_pe_k").T), "peT_v": np.ascontiguousarray(f("nsa_cmp_pe_v").T),
        "ident": np.eye(128, dtype=np.float32),
    }
    for k in ("ln1_g", "ln1_b", "ln2_g", "ln2_b", "ln3_g", "ln3_b"):
        shared[k] = np.ascontiguousarray(np.asarray(inp[k], np.float32).reshape(1, D))
    pos_all = np.arange(S)
    shared["cs64_all"] = _rope_tables(pos_all, 64)
    shared["cs128_all"] = _rope_tables(pos_all, 128)
    ee = np.zeros((128, 64, 128), np.float32)
    for kbl in range(64):
        ee[2 * kbl, kbl, :64] = 1.0
        ee[2 * kbl + 1, kbl, 64:] = 1.0
    shared["eexp"] = ee.reshape(128, 64 * 128)
    ci = np.arange(1024)[:, None] * 16
    sj = np.arange(256)[None, :] * 64
    ovm = np.clip(np.minimum(ci + 32, sj + 64) - np.maximum(ci, sj), 0, None).astype(np.float32) / 16.0
    ovm[1023, :] = 0.0
    shared["ov"] = np.ascontiguousarray(ovm.reshape(8, 128, 256).transpose(1, 0, 2).reshape(128, 8 * 256))

    in_maps = []
    tok_idx = []
    for c in range(NCORE):
        idx = np.concatenate([np.arange((8 * i + c) * TT, (8 * i + c + 1) * TT) for i in range(NSLOT)])
        tok_idx.append(idx)
        m = dict(shared)
        m["x_my"] = np.ascontiguousarray(x[idx])
        m["xT_my"] = np.ascontiguousarray(x[idx].T)
        m["cs64_my"] = _rope_tables(idx, 64)
        m["cs128_my"] = _rope_tables(idx, 128)
        tq = idx[:, None]
        bj = np.arange(256)[None, :]
        cur = tq // 64
        valid = bj * 64 <= tq
        forced = (bj == 0) | (bj == cur) | (bj == cur - 1)
        sb = np.where(valid, np.where(forced, np.float32(1e4), np.float32(0.0)), np.float32(-1e30)).astype(np.float32)
        m["sbias"] = np.ascontiguousarray(sb.reshape(16, 128, 256).transpose(1, 0, 2))
        th = np.zeros((96,), np.float32)
        for kbz in range(-4, 32):
            th[kbz + 4] = 512 * c - 128 * kbz
            th[48 + kbz + 4] = 512 * c - 128 * kbz - 512
        for i in range(NSLOT):
            for r in range(3):
                jb = 2 * i - 1 + r
                th[36 + 3 * i + r] = 4096 * i + 512 * c - 31 - 2048 * jb
        m["thr"] = np.ascontiguousarray(np.broadcast_to(th[None, :], (128, 96)))
        in_maps.append(m)

    nc = emit_program()
    if os.environ.get('KTRACE'):
        res = run_bass_kernel_spmd(nc, in_maps, core_ids=list(range(NCORE)), trace=True)
    else:
        res = run_bass_kernel_spmd(nc, in_maps, core_ids=list(range(NCORE)))
    LAST["res"] = res
    out = np.empty((1, S, D), np.float32)
    for c in range(NCORE):
        out[0, tok_idx[c]] = res.results[c]["out"]
    return out
```

```python
import math
import os
STOP = int(os.environ.get('KSTOP', '99'))
NT1 = int(os.environ.get('KNT1', '32'))
KSLOTS = int(os.environ.get('KSLOTS', '4'))
KDBG = int(os.environ.get('KDBG', '0'))
DBG_LAYOUT = {}
LAST = {}
from contextlib import ExitStack
import numpy as np
import concourse.bass as bass
import concourse.mybir as mybir
from concourse.bass_utils import run_bass_kernel_spmd

F32 = mybir.dt.float32
BF16 = mybir.dt.bfloat16
AF = mybir.ActivationFunctionType
ALU = mybir.AluOpType
AX = mybir.AxisListType

S = 16384
D = 2048
DFF = 5632
NCORE = 8
TT = 512
NT_ALL = S // TT
NSLOT = 4
ALPHA = 2.0 ** 0.25
LN_EPS = 1e-5
RMS_EPS = 1e-6
NKV_F = 2176
NKV_T = 512
NQ = 512 + 2048 + 24
MLA_SCALE = 192.0 ** -0.5
NSA_SCALE = 128.0 ** -0.5


class Res:
    __slots__ = ("w", "r")

    def __init__(self):
        self.w = None
        self.r = {}


class Ctx:
    ENG = ("pe", "act", "dve", "pool", "sp")

    def __init__(self, nc, es):
        self.nc = nc
        self.ops = {e: [] for e in self.ENG}
        self.seq = {e: 0 for e in self.ENG}
        self.known = {e: {} for e in self.ENG}
        self.sems = {}
        for e in ("pe", "act", "dve", "pool"):
            self.sems[e] = es.enter_context(nc.semaphore("S_" + e))
        self.dmak = {"sp": 20, "pool": 6, "act": 6}
        self.dman = {q: 0 for q in self.dmak}
        for q, k in self.dmak.items():
            for i in range(k):
                self.sems[(q, i)] = es.enter_context(nc.semaphore("D_%s_%d" % (q, i)))
        self.last = {}

    def _need(self, eng, tok, waits, war=False):
        if tok is None:
            return
        key, val, teng = tok
        if teng == eng and not isinstance(key, tuple):
            if eng == "pe" or war:
                return
        if self.known[eng].get(key, 0) >= val:
            return
        self.known[eng][key] = val
        waits.append((key, val))

    def op(self, eng, fn, reads=(), writes=(), dma=False):
        waits = []
        for r in reads:
            self._need(eng, r.w, waits)
        for w in writes:
            self._need(eng, w.w, waits)
            for key, (val, teng) in w.r.items():
                self._need(eng, (key, val, teng), waits, war=True)
        if dma:
            k = self.dmak[eng]
            n = self.dman[eng]
            self.dman[eng] = n + 1
            key = (eng, n % k)
            val = 16 * (n // k + 1)
            if n >= k:
                self._need(eng, (key, val - 16, eng), waits)
            tok = (key, val, eng)
            inc = (key, 16)
        else:
            self.seq[eng] += 1
            tok = (eng, self.seq[eng], eng)
            inc = (eng, 1)
        self.ops[eng].append((waits, fn, inc))
        self.last[tok[0]] = tok
        for r in reads:
            r.r[tok[0]] = (tok[1], tok[2])
        for w in writes:
            w.w = tok
            w.r = {}
        return tok

    def barrier(self):
        toks = list(self.last.values())
        for e in self.ENG:
            waits = []
            for t in toks:
                self._need(e, (t[0], t[1], "x"), waits)
            if waits:
                self.ops[e].append((waits, None, None))

    def replay(self, eng, e):
        for waits, fn, inc in self.ops[eng]:
            for key, val in waits:
                e.wait_ge(self.sems[key], val)
            if fn is not None:
                ins = fn(e)
                ins.then_inc(self.sems[inc[0]], inc[1])


class Arena:
    def __init__(self, ap, nelem):
        self.ap = ap
        self.n = nelem
        self.off = 0

    def reset(self):
        self.off = 0

    def alloc(self, free_shape, dtype, parts=128):
        n = int(np.prod(free_shape))
        nb = n * (2 if dtype == F32 else 1)
        nb = (nb + 15) // 16 * 16
        assert self.off + nb <= self.n, ("arena overflow", self.off, nb, self.n)
        v = self.ap[0:parts, self.off:self.off + nb]
        self.off += nb
        if dtype == F32:
            v = v.bitcast(F32)
        v = v[:, 0:n]
        if len(free_shape) == 2:
            v = v.rearrange("p (a b) -> p a b", b=free_shape[1])
        elif len(free_shape) == 3:
            v = v.rearrange("p (a b c) -> p a b c", b=free_shape[1], c=free_shape[2])
        return v, Res()


def build_program():
    nc = bass.Bass("TRN2", target_bir_lowering=False)
    es = ExitStack()

    def din(name, shape, dt=F32):
        return nc.dram_tensor(name, list(shape), dt, kind="ExternalInput").ap()

    def dscr(name, shape, dt=BF16):
        return nc.dram_tensor(name, list(shape), dt).ap()

    x_all = din("x_all", [S, D])
    xT_all = din("xT_all", [D, S])
    x_my = din("x_my", [NSLOT * TT, D])
    xT_my = din("xT_my", [D, NSLOT * TT])
    wsrc = {}
    for nm in ("ffn1_w_gate", "ffn1_w_up", "ffn2_w_gate", "ffn2_w_up"):
        wsrc[nm] = din(nm, [D, DFF])
    for nm in ("ffn1_w_down", "ffn2_w_down"):
        wsrc[nm] = din(nm, [DFF, D])
    wsrc["w_kvf"] = din("w_kvf", [D, NKV_F])
    wsrc["w_kvt"] = din("w_kvt", [D, NKV_T])
    wsrc["w_q"] = din("w_q", [D, NQ])
    wsrc["w_uq"] = din("w_uq", [512, 2048])
    wsrc["w_ukv"] = din("w_ukv", [256, 2048])
    wsrc["w_out"] = din("w_out", [D, D])
    wsrc["w1_k"] = din("w1_k", [4096, 256])
    wsrc["w1_v"] = din("w1_v", [4096, 256])
    wsrc["w2_k"] = din("w2_k", [256, 128])
    wsrc["w2_v"] = din("w2_v", [256, 128])
    ln_in = {k: din(k, [1, D]) for k in ("ln1_g", "ln1_b", "ln2_g", "ln2_b", "ln3_g", "ln3_b")}
    lngbT_in = din("ln1_gbT", [128, 32])
    qg_in = din("qg", [128, 4])
    kvg_in = din("kvg", [128, 2])
    gateb_in = din("gateb", [24, 1])
    peT_k_in = din("peT_k", [128, 32])
    peT_v_in = din("peT_v", [128, 32])
    cs64_all = din("cs64_all", [64, 2, S])
    cs128_all = din("cs128_all", [128, 2, S])
    cs64_my = din("cs64_my", [64, 2, NSLOT * TT])
    cs128_my = din("cs128_my", [128, 2, NSLOT * TT])
    sbias_in = din("sbias", [128, 16, 256])
    thr_in = din("thr", [128, 96])
    ident_in = din("ident", [128, 128])
    eexp_in = din("eexp", [128, 64 * 128])
    ov_in = din("ov", [128, 8 * 256])
    out = nc.dram_tensor("out", [NSLOT * TT, D], F32, kind="ExternalOutput").ap()

    wb = {nm: dscr("b_" + nm, ap.shape) for nm, ap in wsrc.items()}
    KN = dscr("KN", [8, 128, S])
    KR = dscr("KR", [64, S])
    VM = dscr("VM", [S, 1024])
    KS = dscr("KS", [2, 128, S])
    KW = dscr("KW", [2, 128, S])
    KC = dscr("KC", [2, 128, S])
    VC = dscr("VC", [2, 128, S])
    VSW = dscr("VSW", [S, 512])

    ARENA_N = 80 * 1024
    arena_t = es.enter_context(nc.sbuf_tensor("arena", [128, ARENA_N], BF16))
    A = Arena(arena_t, ARENA_N)

    def pers(name, shape, dt):
        t = es.enter_context(nc.sbuf_tensor("s_" + name, list(shape), dt))
        return t, Res()

    ident_f, r_identf = pers("ident_f", [128, 128], F32)
    ident_b, r_identb = pers("ident_b", [128, 128], BF16)
    ones_b, r_ones = pers("ones_b", [128, 128], BF16)
    eexp, r_eexp = pers("eexp", [128, 64 * 128], BF16)
    ovb, r_ov = pers("ovb", [128, 8 * 256], BF16)
    ones_f, r_onesf = pers("ones_f", [24, 128], F32)
    thr, r_thr = pers("thr", [128, 96], F32)
    iota_pf, r_iota = pers("iota_pf", [128, 512], F32)
    iota_16, r_iota16 = pers("iota_16", [128, 512], F32)
    lngbT, r_lngbT = pers("lngbT", [128, 32], F32)
    qg, r_qg = pers("qg", [128, 4], F32)
    kvg, r_kvg = pers("kvg", [128, 2], F32)
    gateb, r_gateb = pers("gateb", [24, 1], F32)
    kcT, r_kcT = pers("kcT", [128, 2, 1024], BF16)
    vcc, r_vcc = pers("vcc", [128, 2, 8, 128], BF16)
    psum = []
    for i in range(8):
        t = es.enter_context(nc.psum_tensor("ps%d" % i, [128, 512], F32))
        psum.append((t, Res()))

    cx = Ctx(nc, es)
    rD = {}
    dbg_t = {}
    dbg_off = {"f": 0, "b": 0}
    if KDBG:
        dbg_t["f"] = nc.dram_tensor("dbg_f", [128, 40960], F32, kind="ExternalOutput").ap()
        dbg_t["b"] = nc.dram_tensor("dbg_b", [128, 81920], BF16, kind="ExternalOutput").ap()

    def dbg(name, ap, reads, parts=128):
        if not KDBG:
            return
        k = "f" if ap.dtype == F32 else "b"
        shp = list(ap.shape)
        n = int(np.prod(shp[1:]))
        off = dbg_off[k]
        dbg_off[k] = off + n
        DBG_LAYOUT[name] = (k, off, shp)
        dst = dbg_t[k][0:shp[0], off:off + n]
        if len(shp) == 3:
            dst = dst.rearrange("p (a b) -> p a b", b=shp[2])
        elif len(shp) == 4:
            dst = dst.rearrange("p (a b c) -> p a b c", b=shp[2], c=shp[3])
        cx.op("sp", lambda e: e.dma_start(out=dst, in_=ap), reads, [dres("dbg")], dma=True)

    def dres(name):
        if name not in rD:
            rD[name] = Res()
        return rD[name]

    def dma(q, out_ap, in_ap, reads, writes):
        return cx.op(q, lambda e: e.dma_start(out=out_ap, in_=in_ap), reads, writes, dma=True)

    def mm(ps, lhsT, rhs, start, stop, reads, writes):
        return cx.op("pe", lambda e: e.matmul(ps, lhsT, rhs, start=start, stop=stop), reads, writes)

    def tr(ps, in_, idt, reads, writes):
        return cx.op("pe", lambda e: e.transpose(ps, in_, idt), reads, writes)

    def act(out_ap, in_ap, func, reads, writes, scale=1.0, bias=0.0):
        return cx.op("act", lambda e: e.activation(out=out_ap, in_=in_ap, func=func, scale=scale, bias=bias),
                     reads, writes)

    def v_tt(eng, out_ap, a, b, op, reads, writes):
        return cx.op(eng, lambda e: e.tensor_tensor(out=out_ap, in0=a, in1=b, op=op), reads, writes)

    def v_ts(eng, out_ap, a, s1, s2, op0, op1, reads, writes):
        if s2 is None:
            return cx.op(eng, lambda e: e.tensor_scalar(out=out_ap, in0=a, scalar1=s1, scalar2=None, op0=op0),
                         reads, writes)
        return cx.op(eng, lambda e: e.tensor_scalar(out=out_ap, in0=a, scalar1=s1, scalar2=s2, op0=op0, op1=op1),
                     reads, writes)

    def v_stt(eng, out_ap, a, s, b, op0, op1, reads, writes):
        return cx.op(eng, lambda e: e.scalar_tensor_tensor(out=out_ap, in0=a, scalar=s, in1=b, op0=op0, op1=op1),
                     reads, writes)

    def v_copy(eng, out_ap, in_ap, reads, writes):
        return cx.op(eng, lambda e: e.tensor_copy(out=out_ap, in_=in_ap), reads, writes)

    cast_rr = [0]

    def cast(out_ap, in_ap, reads, writes):
        cast_rr[0] += 1
        if cast_rr[0] % 2:
            return act(out_ap, in_ap, AF.Copy, reads, writes)
        return v_copy("dve", out_ap, in_ap, reads, writes)

    A.reset()
    dma("sp", ident_f[:], ident_in, [], [r_identf])
    v_copy("dve", ident_b[:], ident_f[:], [r_identf], [r_identb])
    cx.op("pool", lambda e: e.memset(ones_b[:], 1.0), [], [r_ones])
    cx.op("pool", lambda e: e.memset(kcT[:], 0.0), [], [r_kcT])
    cx.op("pool", lambda e: e.memset(vcc[:], 0.0), [], [r_vcc])
    cx.op("pool", lambda e: e.memset(ones_f[:], 1.0), [], [r_onesf])
    dma("sp", thr[:], thr_in, [], [r_thr])
    dma("sp", lngbT[:], lngbT_in, [], [r_lngbT])
    dma("sp", qg[:], qg_in, [], [r_qg])
    dma("sp", kvg[:], kvg_in, [], [r_kvg])
    dma("sp", gateb[:], gateb_in, [], [r_gateb])
    cx.op("pool", lambda e: e.iota(iota_pf[:], [[-1, 512]], base=0, channel_multiplier=1,
                                   allow_small_or_imprecise_dtypes=True), [], [r_iota])
    cx.op("pool", lambda e: e.iota(iota_16[:], [[-1, 512]], base=0, channel_multiplier=16,
                                   allow_small_or_imprecise_dtypes=True), [], [r_iota16])
    st0, r_st0 = A.alloc([64 * 128], F32)
    dma("sp", st0, eexp_in, [], [r_st0])
    cast(eexp[:], st0, [r_st0], [r_eexp])
    st1, r_st1 = A.alloc([8 * 256], F32)
    dma("sp", st1, ov_in, [], [r_st1])
    cast(ovb[:], st1, [r_st1], [r_ov])

    CW = 4096
    stg = [A.alloc([CW], F32) for _ in range(3)]
    stb = [A.alloc([CW], BF16) for _ in range(3)]
    ci = 0
    for nm, src in wsrc.items():
        K_, N_ = src.shape
        flat_s = src.rearrange("k n -> (k n)")
        flat_d = wb[nm].rearrange("k n -> (k n)")
        tot = K_ * N_
        per = 128 * CW
        o = 0
        while o < tot:
            n = min(per, tot - o)
            w_ = n // 128
            assert w_ * 128 == n, (nm, n)
            (sf, rsf), (sb_, rsb) = stg[ci % 3], stb[ci % 3]
            ci += 1
            dma("sp", sf[:, 0:w_], flat_s[o:o + n].rearrange("(p w) -> p w", p=128), [], [rsf])
            cast(sb_[:, 0:w_], sf[:, 0:w_], [rsf], [rsb])
            dma("sp", flat_d[o:o + n].rearrange("(p w) -> p w", p=128), sb_[:, 0:w_], [rsb], [dres("b_" + nm)])
            o += n
    cx.barrier()
    if STOP == 0:
        return nc, es, cx

    def ffn_ln(A, xT_src, x_src, pre, lng, lnb, y_t, r_y, xT_res=None, apply_gb=True):
        Wg, Wu, Wd = wb[pre + "_w_gate"], wb[pre + "_w_up"], wb[pre + "_w_down"]
        rWg, rWu, rWd = dres("b_" + pre + "_w_gate"), dres("b_" + pre + "_w_up"), dres("b_" + pre + "_w_down")
        if xT_res is None:
            xTb, r_xTb = A.alloc([16, TT], BF16)
            sx = [A.alloc([2, TT], F32) for _ in range(2)]
            for c4 in range(8):
                sf, rsf = sx[c4 % 2]
                dma("sp", sf, xT_src[c4 * 256:(c4 + 1) * 256, :].rearrange("(kc p) t -> p kc t", p=128), [], [rsf])
                cast(xTb[:, c4 * 2:(c4 + 1) * 2, :], sf, [rsf], [r_xTb])
        else:
            xTb, r_xTb = xT_src, xT_res
        if x_src is not None:
            dma("sp", y_t, x_src.rearrange("(tb p) d -> p tb d", p=128), [], [r_y])
        ln_mark = A.off
        hT, r_hT = A.alloc([44, TT], BF16)
        GW = 256
        wg_t = [A.alloc([16, GW], BF16) for _ in range(2)]
        wu_t = [A.alloc([16, GW], BF16) for _ in range(2)]
        sg_t = [A.alloc([TT], F32) for _ in range(2)]
        for fg in range(DFF // GW):
            (wg, rwg), (wu, rwu) = wg_t[fg % 2], wu_t[fg % 2]
            dma("sp", wg, Wg[:, fg * GW:(fg + 1) * GW].rearrange("(kc p) n -> p kc n", p=128), [rWg], [rwg])
            dma("sp", wu, Wu[:, fg * GW:(fg + 1) * GW].rearrange("(kc p) n -> p kc n", p=128), [rWu], [rwu])
            for j in range(GW // 128):
                m = fg * (GW // 128) + j
                (pg, rpg), (pu, rpu) = psum[(2 * m) % 8], psum[(2 * m + 1) % 8]
                for kc in range(16):
                    mm(pg[:], wg[:, kc, j * 128:(j + 1) * 128], xTb[:, kc, :], kc == 0, kc == 15, [rwg, r_xTb], [rpg])
                for kc in range(16):
                    mm(pu[:], wu[:, kc, j * 128:(j + 1) * 128], xTb[:, kc, :], kc == 0, kc == 15, [rwu, r_xTb], [rpu])
                sg, rsg = sg_t[m % 2]
                act(sg, pg[:], AF.Silu, [rpg], [rsg])
                v_tt("dve", hT[:, m, :], sg, pu[:], ALU.mult, [rsg, rpu], [r_hT])
        for tb in range(4):
            act(y_t[:, tb, :], y_t[:, tb, :], AF.Copy, [r_y], [r_y], scale=ALPHA)
        KG = 4
        wd_t = [A.alloc([KG, 512], BF16) for _ in range(2)]
        wi = 0
        for ng in range(4):
            for kg in range(11):
                wd, rwd = wd_t[wi % 2]
                wi += 1
                dma("sp", wd, Wd[kg * KG * 128:(kg + 1) * KG * 128, ng * 512:(ng + 1) * 512]
                    .rearrange("(kc p) n -> p kc n", p=128), [rWd], [rwd])
                for tb in range(4):
                    ps, rps = psum[tb + 4 * (ng % 2)]
                    for kl in range(KG):
                        kc = kg * KG + kl
                        mm(ps[:], hT[:, kc, tb * 128:(tb + 1) * 128], wd[:, kl, :], kc == 0, kc == 43,
                           [r_hT, rwd], [rps])
            for tb in range(4):
                ps, rps = psum[tb + 4 * (ng % 2)]
                v_stt("dve", y_t[:, tb, ng * 512:(ng + 1) * 512], ps[:], 0.5, y_t[:, tb, ng * 512:(ng + 1) * 512],
                      ALU.mult, ALU.add, [rps, r_y], [r_y])
        cx.barrier()
        A.off = ln_mark
        layer_norm(A, y_t, r_y, lng, lnb, apply_gb)

    def layer_norm(A, y_t, r_y, lng, lnb, apply_gb=True):
        if apply_gb:
            gb, r_gb = A.alloc([2, D], F32)
            dma("sp", gb[:, 0, :], lng.partition_broadcast(128), [], [r_gb])
            dma("sp", gb[:, 1, :], lnb.partition_broadcast(128), [], [r_gb])
        st, r_st = A.alloc([4, 4, 6], F32)
        mv, r_mv = A.alloc([4, 2], F32)
        rs, r_rs = A.alloc([4, 1], F32)
        for tb in range(4):
            for c in range(4):
                cx.op("dve", lambda e, tb=tb, c=c: e.bn_stats(out=st[:, tb, c, :], in_=y_t[:, tb, c * 512:(c + 1) * 512]),
                      [r_y], [r_st])
            cx.op("dve", lambda e, tb=tb: e.bn_aggr(out=mv[:, tb, :], in_=st[:, tb, :, :]), [r_st], [r_mv])
            act(rs[:, tb, :], mv[:, tb, 1:2], AF.Sqrt, [r_mv], [r_rs], scale=1.0, bias=LN_EPS)
            cx.op("dve", lambda e, tb=tb: e.reciprocal(out=rs[:, tb, :], in_=rs[:, tb, :]), [r_rs], [r_rs])
            v_ts("dve", y_t[:, tb, :], y_t[:, tb, :], mv[:, tb, 0:1], rs[:, tb, 0:1], ALU.subtract, ALU.mult,
                 [r_y, r_mv, r_rs], [r_y])
            if apply_gb:
                v_tt("pool", y_t[:, tb, :], y_t[:, tb, :], gb[:, 0, :], ALU.mult, [r_y, r_gb], [r_y])
                v_tt("dve", y_t[:, tb, :], y_t[:, tb, :], gb[:, 1, :], ALU.add, [r_y, r_gb], [r_y])

    def transpose_to_bf16(y_t, r_y, xT, r_xT, gbT=None):
        for fc in range(16):
            ps, rps = psum[fc % 8]
            for tb in range(4):
                tr(ps[:, tb * 128:(tb + 1) * 128], y_t[:, tb, fc * 128:(fc + 1) * 128], ident_f[:],
                   [r_y, r_identf], [rps])
            if gbT is None:
                cast(xT[:, fc, :], ps[:], [rps], [r_xT])
            else:
                act(xT[:, fc, :], ps[:], AF.Identity, [rps, r_lngbT], [r_xT], scale=gbT[:, fc:fc + 1],
                    bias=gbT[:, 16 + fc:17 + fc])

    def rms_scale(A, srcT, r_src, nchunk, width, gvec, r_g, dstT, r_dst):
        sq, r_sq = A.alloc([nchunk, TT], BF16)
        for c in range(nchunk):
            act(sq[:, c, :], srcT[:, c, :], AF.Square, [r_src], [r_sq])
        ps, rps = psum[7]
        for c in range(nchunk):
            mm(ps[:], ones_b[:], sq[:, c, :], c == 0, c == nchunk - 1, [r_ones, r_sq], [rps])
        rr, r_rr = A.alloc([TT], F32)
        act(rr, ps[:], AF.Sqrt, [rps], [r_rr], scale=1.0 / width, bias=RMS_EPS)
        cx.op("dve", lambda e: e.reciprocal(out=rr, in_=rr), [r_rr], [r_rr])
        for c in range(nchunk):
            v_stt("dve", dstT[:, c, :], srcT[:, c, :], gvec[:, c:c + 1], rr, ALU.mult, ALU.mult,
                  [r_src, r_g, r_rr], [r_dst])

    def rope_comb(eng, dst, xa, xb, cs, npart, reads, writes, tmp):
        v_tt("dve", tmp[0:npart, :], xb, cs[0:npart, 1, :], ALU.mult, reads, [writes[1]])
        v_tt("dve", dst, xa, cs[0:npart, 0, :], ALU.mult, reads, [writes[0]])
        v_tt(eng, dst, dst, tmp[0:npart, :], ALU.add, [writes[0], writes[1]], [writes[0]])

    for t in range(min(NT_ALL, NT1)):
        A.reset()
        t0 = t * TT
        y_t, r_y = A.alloc([4, D], F32)
        ffn_ln(A, xT_all[:, t0:t0 + TT], x_all[t0:t0 + TT, :], "ffn1", ln_in["ln1_g"], ln_in["ln1_b"], y_t, r_y,
               apply_gb=False)
        cx.barrier()
        A.off = 4 * D * 2
        x1T, r_x1T = A.alloc([16, TT], BF16)
        transpose_to_bf16(y_t, r_y, x1T, r_x1T, gbT=lngbT)
        cs64, r_cs64 = A.alloc([2, TT], F32, parts=64)
        cs128, r_cs128 = A.alloc([2, TT], F32)
        dma("sp", cs64, cs64_all[:, :, t0:t0 + TT], [], [r_cs64])
        dma("sp", cs128, cs128_all[:, :, t0:t0 + TT], [], [r_cs128])
        wkv_t = [A.alloc([16, 256], BF16) for _ in range(2)]
        tmpr, r_tmpr = A.alloc([TT], F32)
        ob_t = [A.alloc([TT], BF16) for _ in range(2)]
        ckvT, r_ckvT = A.alloc([2, TT], F32)
        rW = dres("b_w_kvf")
        oi = 0
        def load_w(col0, ncol, idx):
            w, rw = wkv_t[idx % 2]
            dma("sp", w[:, :, 0:ncol], wb["w_kvf"][:, col0:col0 + ncol].rearrange("(kc p) n -> p kc n", p=128),
                [rW], [rw])
            return w, rw
        wi = 0
        w, rw = load_w(0, 256, wi); wi += 1
        for c in range(2):
            ps, rps = psum[c]
            for kc in range(16):
                mm(ps[:], w[:, kc, c * 128:(c + 1) * 128], x1T[:, kc, :], kc == 0, kc == 15, [rw, r_x1T], [rps])
            act(ckvT[:, c, :], ps[:], AF.Copy, [rps], [r_ckvT])
        w, rw = load_w(256, 128, wi); wi += 1
        psa, rpsa = psum[2]
        psb, rpsb = psum[3]
        for kc in range(16):
            mm(psa[0:64, :], w[:, kc, 0:64], x1T[:, kc, :], kc == 0, kc == 15, [rw, r_x1T], [rpsa])
        for kc in range(16):
            mm(psb[0:64, :], w[:, kc, 64:128], x1T[:, kc, :], kc == 0, kc == 15, [rw, r_x1T], [rpsb])
        ob, rob = ob_t[oi % 2]; oi += 1
        rope_comb("dve", ob[0:64, :], psa[0:64, :], psb[0:64, :], cs64, 64, [rpsa, rpsb, r_cs64], [rob, r_tmpr], tmpr)
        dma("pool", KR[:, t0:t0 + TT], ob[0:64, :], [rob], [dres("KR")])
        for gi, (dst, nm) in enumerate(((KC, "KC"), (KS, "KS"), (KW, "KW"))):
            for kh in range(2):
                w, rw = load_w(384 + 256 * (gi * 2 + kh), 256, wi); wi += 1
                psa, rpsa = psum[(2 * wi) % 8]
                psb, rpsb = psum[(2 * wi + 1) % 8]
                for kc in range(16):
                    mm(psa[:], w[:, kc, 0:128], x1T[:, kc, :], kc == 0, kc == 15, [rw, r_x1T], [rpsa])
                for kc in range(16):
                    mm(psb[:], w[:, kc, 128:256], x1T[:, kc, :], kc == 0, kc == 15, [rw, r_x1T], [rpsb])
                ob, rob = ob_t[oi % 2]; oi += 1
                rope_comb("dve", ob, psa[:], psb[:], cs128, 128, [rpsa, rpsb, r_cs128], [rob, r_tmpr], tmpr)
                dma("pool", dst[kh, :, t0:t0 + TT], ob, [rob], [dres(nm)])
        w, rw = load_w(1920, 256, wi); wi += 1
        for kh in range(2):
            ps, rps = psum[4 + kh]
            for kc in range(16):
                mm(ps[:], w[:, kc, kh * 128:(kh + 1) * 128], x1T[:, kc, :], kc == 0, kc == 15, [rw, r_x1T], [rps])
            ob, rob = ob_t[oi % 2]; oi += 1
            cast(ob, ps[:], [rps], [rob])
            dma("pool", VC[kh, :, t0:t0 + TT], ob, [rob], [dres("VC")])
        wt_t = [A.alloc([16, 256], BF16) for _ in range(2)]
        vo, r_vo = A.alloc([4, 512], BF16)
        for hf in range(2):
            w, rw = wt_t[hf]
            dma("sp", w, wb["w_kvt"][:, hf * 256:(hf + 1) * 256].rearrange("(kc p) n -> p kc n", p=128),
                [dres("b_w_kvt")], [rw])
            for tb in range(4):
                ps, rps = psum[tb + 4 * hf]
                for kc in range(16):
                    mm(ps[:, 0:256], x1T[:, kc, tb * 128:(tb + 1) * 128], w[:, kc, :], kc == 0, kc == 15,
                       [rw, r_x1T], [rps])
                cast(vo[:, tb, hf * 256:(hf + 1) * 256], ps[:, 0:256], [rps], [r_vo])
        dma("pool", VSW[t0:t0 + TT, :].rearrange("(tb p) n -> p tb n", p=128), vo, [r_vo], [dres("VSW")])
        ckvn, r_ckvn = A.alloc([2, TT], BF16)
        rms_scale(A, ckvT, r_ckvT, 2, 256.0, kvg, r_kvg, ckvn, r_ckvn)
        wukv, r_wukv = A.alloc([2, 2048], BF16)
        dma("sp", wukv, wb["w_ukv"].rearrange("(kc p) n -> p kc n", p=128), [dres("b_w_ukv")], [r_wukv])
        for h in range(8):
            ps, rps = psum[h % 4]
            for kc in range(2):
                mm(ps[:], wukv[:, kc, h * 256:h * 256 + 128], ckvn[:, kc, :], kc == 0, kc == 1, [r_wukv, r_ckvn], [rps])
            ob, rob = ob_t[oi % 2]; oi += 1
            cast(ob, ps[:], [rps], [rob])
            dma("pool", KN[h, :, t0:t0 + TT], ob, [rob], [dres("KN")])
        vm, r_vm = A.alloc([4, 1024], BF16)
        for tb in range(4):
            for hg in range(2):
                ps, rps = psum[4 + (tb * 2 + hg) % 4]
                for kc in range(2):
                    rhs = wukv[:, kc, hg * 1024:(hg + 1) * 1024].rearrange("p (h c) -> p h c", c=256)[:, :, 128:256]
                    mm(ps[:].rearrange("p (h c) -> p h c", c=128), ckvn[:, kc, tb * 128:(tb + 1) * 128], rhs,
                       kc == 0, kc == 1, [r_wukv, r_ckvn], [rps])
                cast(vm[:, tb, hg * 512:(hg + 1) * 512], ps[:], [rps], [r_vm])
        dma("pool", VM[t0:t0 + TT, :].rearrange("(tb p) n -> p tb n", p=128), vm, [r_vm], [dres("VM")])
        cx.barrier()
    cx.barrier()

    if STOP == 1:
        return nc, es, cx
    A.reset()
    w1, r_w1 = A.alloc([32, 256], BF16)
    w2, r_w2 = A.alloc([2, 128], BF16)
    pef, r_pef = A.alloc([32], F32)
    peb, r_peb = A.alloc([32], BF16)
    bias, r_bias = A.alloc([2], F32)
    tT, r_tT = A.alloc([S], BF16)
    hid, r_hid = A.alloc([2, 1024], BF16)
    u1, r_u1 = A.alloc([512], F32)
    u2, r_u2 = A.alloc([512], F32)
    for ti, (srcD, nm, w1n, w2n, peT_in) in enumerate(((KC, "KC", "w1_k", "w2_k", peT_k_in),
                                                         (VC, "VC", "w1_v", "w2_v", peT_v_in))):
        dma("sp", w1, wb[w1n].rearrange("(l p) n -> p l n", p=128), [dres("b_" + w1n)], [r_w1])
        dma("sp", w2, wb[w2n].rearrange("(c p) n -> p c n", p=128), [dres("b_" + w2n)], [r_w2])
        dma("sp", pef, peT_in, [], [r_pef])
        v_copy("dve", peb, pef, [r_pef], [r_peb])
        for hc in range(2):
            ps, rps = psum[hc]
            for l in range(32):
                mm(ps[:, 0:1], w1[:, l, hc * 128:(hc + 1) * 128], peb[:, l:l + 1], l == 0, l == 31, [r_w1, r_peb], [rps])
            v_copy("dve", bias[:, hc:hc + 1], ps[:, 0:1], [rps], [r_bias])
        for kh in range(2):
            dma("sp", tT, srcD[kh], [dres(nm)], [r_tT])
            cx.op("pool", lambda e: e.memset(hid, 0.0), [], [r_hid])
            for cg in range(2):
                ncs = 512 if cg == 0 else 511
                for hc in range(2):
                    ps, rps = psum[2 + (cg * 2 + hc) % 4]
                    for l in range(32):
                        st_ = l + 16 * 512 * cg
                        mm(ps[:, 0:ncs], w1[:, l, hc * 128:(hc + 1) * 128], tT[:, st_:st_ + 16 * (ncs - 1) + 1:16],
                           l == 0, l == 31, [r_w1, r_tT], [rps])
                    act(u1[:, 0:ncs], ps[:, 0:ncs], AF.Identity, [rps, r_bias], [r_u1], bias=bias[:, hc:hc + 1])
                    v_tt("dve", u2[:, 0:ncs], u1[:, 0:ncs], u1[:, 0:ncs], ALU.mult, [r_u1], [r_u2])
                    v_ts("dve", u2[:, 0:ncs], u2[:, 0:ncs], 0.044715, 1.0, ALU.mult, ALU.add, [r_u2], [r_u2])
                    v_tt("dve", u2[:, 0:ncs], u2[:, 0:ncs], u1[:, 0:ncs], ALU.mult, [r_u2, r_u1], [r_u2])
                    act(u2[:, 0:ncs], u2[:, 0:ncs], AF.Sigmoid, [r_u2], [r_u2], scale=1.5957691216057308)
                    v_tt("dve", hid[:, hc, cg * 512:cg * 512 + ncs], u2[:, 0:ncs], u1[:, 0:ncs], ALU.mult,
                         [r_u2, r_u1], [r_hid])
            if ti == 0:
                for cg in range(2):
                    ps, rps = psum[6 + cg]
                    for hc in range(2):
                        mm(ps[:], w2[:, hc, :], hid[:, hc, cg * 512:(cg + 1) * 512], hc == 0, hc == 1, [r_w2, r_hid], [rps])
                    cast(kcT[:, kh, cg * 512:(cg + 1) * 512], ps[:], [rps], [r_kcT])
            else:
                for jb in range(8):
                    ps, rps = psum[6 + jb % 2]
                    for hc in range(2):
                        mm(ps[:, 0:128], hid[:, hc, jb * 128:(jb + 1) * 128], w2[:, hc, :], hc == 0, hc == 1,
                           [r_w2, r_hid], [rps])
                    cast(vcc[:, kh, jb, :], ps[:, 0:128], [rps], [r_vcc])
    cx.barrier()

    if STOP == 3:
        return nc, es, cx
    def flash_branch(A, K_tile_fn, nkb, q_ap, r_q, extra_q, v_fn, mask_fn, scale, po, rpo, psm, rpsm,
                     sbanks=(4, 5, 6), depth=2):
        pend = {}
        for step in range(nkb + depth):
            kb = step
            if kb < nkb:
                kT, r_kT, kT2, r_kT2 = K_tile_fn(kb)
                vv, r_vv = v_fn(kb)
                ps, rps = psum[sbanks[kb % len(sbanks)]]
                mm(ps[:], kT, q_ap, True, kT2 is None, [r_kT, r_q], [rps])
                if kT2 is not None:
                    mm(ps[:], kT2, extra_q, False, True, [r_kT2, r_q], [rps])
                e_, r_e = e_tiles[kb % len(e_tiles)]
                act(e_, ps[:], AF.Exp, [rps], [r_e], scale=scale)
                mask_fn(kb, e_, r_e)
                pend[kb] = (e_, r_e, vv, r_vv)
            kb = step - depth
            if kb >= 0:
                e_, r_e, vv, r_vv = pend.pop(kb)
                mm(po[:], vv, e_, kb == 0, kb == nkb - 1, [r_vv, r_e], [rpo])
                mm(psm[:], ones_b[:], e_, kb == 0, kb == nkb - 1, [r_ones, r_e], [rpsm])

    e_tiles = None

    for si in range(min(NSLOT, KSLOTS)):
        A.reset()
        m0 = si * TT
        y_t, r_y = A.alloc([4, D], F32)
        ffn_ln(A, xT_my[:, m0:m0 + TT], x_my[m0:m0 + TT, :], "ffn1", ln_in["ln1_g"], ln_in["ln1_b"], y_t, r_y)
        cx.barrier()
        A.off = 4 * D * 2
        x1T, r_x1T = A.alloc([16, TT], BF16)
        transpose_to_bf16(y_t, r_y, x1T, r_x1T)
        if si == 0:
            dbg("x1", y_t, [r_y])
            dbg("x1T", x1T, [r_x1T])
        cs64, r_cs64 = A.alloc([2, TT], F32, parts=64)
        cs128, r_cs128 = A.alloc([2, TT], F32)
        dma("sp", cs64, cs64_my[:, :, m0:m0 + TT], [], [r_cs64])
        dma("sp", cs128, cs128_my[:, :, m0:m0 + TT], [], [r_cs128])
        tmpr, r_tmpr = A.alloc([TT], F32)
        qn, r_qn = A.alloc([8, TT], BF16)
        qr, r_qr = A.alloc([8, TT], BF16, parts=64)
        qs, r_qs = A.alloc([8, TT], BF16)
        gT, r_gT = A.alloc([TT], F32, parts=24)
        mixT, r_mixT = A.alloc([16, TT], BF16)
        mark_q = A.off
        wq_t = [A.alloc([16, 256], BF16) for _ in range(2)]
        cqT, r_cqT = A.alloc([4, TT], F32)
        rWq = dres("b_w_q")
        wi = 0
        for cg in range(2):
            w, rw = wq_t[wi % 2]; wi += 1
            dma("sp", w, wb["w_q"][:, cg * 256:(cg + 1) * 256].rearrange("(kc p) n -> p kc n", p=128), [rWq], [rw])
            for c in range(2):
                ps, rps = psum[(cg * 2 + c) % 4]
                for kc in range(16):
                    mm(ps[:], w[:, kc, c * 128:(c + 1) * 128], x1T[:, kc, :], kc == 0, kc == 15, [rw, r_x1T], [rps])
                act(cqT[:, cg * 2 + c, :], ps[:], AF.Copy, [rps], [r_cqT])
        for h in range(8):
            w, rw = wq_t[wi % 2]; wi += 1
            dma("sp", w, wb["w_q"][:, 512 + h * 256:512 + (h + 1) * 256].rearrange("(kc p) n -> p kc n", p=128),
                [rWq], [rw])
            psa, rpsa = psum[(2 * h) % 4]
            psb, rpsb = psum[(2 * h + 1) % 4]
            for kc in range(16):
                mm(psa[:], w[:, kc, 0:128], x1T[:, kc, :], kc == 0, kc == 15, [rw, r_x1T], [rpsa])
            for kc in range(16):
                mm(psb[:], w[:, kc, 128:256], x1T[:, kc, :], kc == 0, kc == 15, [rw, r_x1T], [rpsb])
            rope_comb("dve", qs[:, h, :], psa[:], psb[:], cs128, 128, [rpsa, rpsb, r_cs128], [r_qs, r_tmpr], tmpr)
        w, rw = wq_t[wi % 2]; wi += 1
        dma("sp", w[:, :, 0:24], wb["w_q"][:, 2560:2584].rearrange("(kc p) n -> p kc n", p=128), [rWq], [rw])
        ps, rps = psum[4]
        for kc in range(16):
            mm(ps[0:24, :], w[:, kc, 0:24], x1T[:, kc, :], kc == 0, kc == 15, [rw, r_x1T], [rps])
        act(gT, ps[0:24, :], AF.Sigmoid, [rps, r_gateb], [r_gT], bias=gateb[:, 0:1])
        cqn, r_cqn = A.alloc([4, TT], BF16)
        rms_scale(A, cqT, r_cqT, 4, 512.0, qg, r_qg, cqn, r_cqn)
        wuq, r_wuq = A.alloc([4, 2048], BF16)
        dma("sp", wuq, wb["w_uq"].rearrange("(kc p) n -> p kc n", p=128), [dres("b_w_uq")], [r_wuq])
        for h in range(8):
            ps, rps = psum[h % 2]
            for kc in range(4):
                mm(ps[:], wuq[:, kc, h * 256:h * 256 + 128], cqn[:, kc, :], kc == 0, kc == 3, [r_wuq, r_cqn], [rps])
            cast(qn[:, h, :], ps[:], [rps], [r_qn])
            psa, rpsa = psum[2 + (2 * h) % 4]
            psb, rpsb = psum[2 + (2 * h + 1) % 4]
            for kc in range(4):
                mm(psa[0:64, :], wuq[:, kc, h * 256 + 128:h * 256 + 192], cqn[:, kc, :], kc == 0, kc == 3,
                   [r_wuq, r_cqn], [rpsa])
            for kc in range(4):
                mm(psb[0:64, :], wuq[:, kc, h * 256 + 192:h * 256 + 256], cqn[:, kc, :], kc == 0, kc == 3,
                   [r_wuq, r_cqn], [rpsb])
            rope_comb("dve", qr[:, h, :], psa[0:64, :], psb[0:64, :], cs64, 64, [rpsa, rpsb, r_cs64],
                      [r_qr, r_tmpr], tmpr)
        if si == 0:
            dbg("qn", qn, [r_qn]); dbg("qr", qr, [r_qr]); dbg("qs", qs, [r_qs]); dbg("gT", gT, [r_gT])
            dbg("cqT", cqT, [r_cqT])
        cx.barrier()
        if STOP == 4:
            return nc, es, cx

        A.off = mark_q
        e_tiles = [A.alloc([TT], BF16) for _ in range(6)]
        kt_t = [A.alloc([TT], BF16) for _ in range(3)]
        kr_t = [A.alloc([TT], BF16, parts=64) for _ in range(3)]
        vt_t = [A.alloc([4, 128], BF16) for _ in range(3)]
        rsum, r_rsum = A.alloc([TT], F32)
        gbc, r_gbc = A.alloc([TT], F32)
        nkt_all = 8 * (si + 1)

        def causal_mask(kbz, e_, r_e):
            v_stt("dve", e_, iota_pf[:], thr[:, kbz + 4:kbz + 5], e_, ALU.is_le, ALU.mult, [r_iota, r_thr, r_e], [r_e])

        for h in range(8):
            po, rpo = psum[0 + 2 * (h % 2)]
            psm, rpsm = psum[1 + 2 * (h % 2)]
            nkb = nkt_all * 4
            state = {}

            def K_fn(kb, h=h, state=state):
                kt_i, kl = divmod(kb, 4)
                if kl == 0:
                    kt, rkt = kt_t[kt_i % 3]
                    kr_, rkr = kr_t[kt_i % 3]
                    vt, rvt = vt_t[kt_i % 3]
                    k0 = kt_i * TT
                    dma("sp", kt, KN[h, :, k0:k0 + TT], [dres("KN")], [rkt])
                    dma("sp", kr_, KR[:, k0:k0 + TT], [dres("KR")], [rkr])
                    dma("sp", vt, VM[k0:k0 + TT, h * 128:(h + 1) * 128].rearrange("(kb p) n -> p kb n", p=128),
                        [dres("VM")], [rvt])
                    state["cur"] = (kt, rkt, kr_, rkr, vt, rvt)
                kt, rkt, kr_, rkr, vt, rvt = state["cur"]
                return kt[:, kl * 128:(kl + 1) * 128], rkt, kr_[:, kl * 128:(kl + 1) * 128], rkr

            def V_fn(kb, state=state):
                kt, rkt, kr_, rkr, vt, rvt = state["cur"]
                return vt[:, kb % 4, :], rvt

            def M_fn(kb, e_, r_e, si=si):
                if kb >= 32 * si:
                    causal_mask(kb - 32 * si, e_, r_e)

            flash_branch(A, K_fn, nkb, qn[:, h, :], r_qn, qr[:, h, :], V_fn, M_fn, MLA_SCALE, po, rpo, psm, rpsm,
                         sbanks=(4, 5, 6, 7))
            v_ts("dve", rsum, psm[:], 1e-30, None, ALU.max, None, [rpsm], [r_rsum])
            cx.op("dve", lambda e: e.reciprocal(out=rsum, in_=rsum), [r_rsum], [r_rsum])
            v_tt("dve", mixT[:, h, :], po[:], rsum, ALU.mult, [rpo, r_rsum], [r_mixT])

        if si == 0:
            dbg("mix_mla", mixT, [r_mixT])
        if STOP == 5:
            cx.barrier()
            return nc, es, cx
        impT, r_impT = A.alloc([2, TT], F32)
        ocn, r_ocn = A.alloc([4, TT], F32)
        onsa, r_onsa = A.alloc([TT], F32)
        tmpf, r_tmpf = A.alloc([TT], F32)
        selT, r_selT = A.alloc([2, TT], BF16)
        msb_t = [A.alloc([TT], BF16) for _ in range(2)]
        sbias, r_sbias = A.alloc([4, 256], F32)
        dma("sp", sbias, sbias_in[:, si * 4:(si + 1) * 4, :], [], [r_sbias])
        score, r_score = A.alloc([256], F32)
        work, r_work = A.alloc([256], F32)
        m8, r_m8 = A.alloc([16], F32)
        selq, r_selq = A.alloc([256], BF16)
        selq2, r_selq2 = A.alloc([256], F32)

        gm, r_gm = A.alloc([TT], F32, parts=24)

        def gate_bcast(row):
            ps, rps = psum[7]
            v_ts("dve", gm, gT, ident_f[0:24, row:row + 1], None, ALU.mult, None, [r_gT, r_identf], [r_gm])
            mm(ps[:], ones_f[:], gm, True, True, [r_onesf, r_gm], [rps])
            return ps, rps

        def finish_branch(po, rpo, psm, rpsm, row, dst, r_dst, accumulate):
            v_ts("dve", rsum, psm[:], 1e-30, None, ALU.max, None, [rpsm], [r_rsum])
            cx.op("dve", lambda e: e.reciprocal(out=rsum, in_=rsum), [r_rsum], [r_rsum])
            psg, rpsg = gate_bcast(row)
            v_tt("dve", gbc, psg[:], rsum, ALU.mult, [rpsg, r_rsum], [r_gbc])
            if accumulate:
                v_tt("dve", tmpf, po[:], gbc, ALU.mult, [rpo, r_gbc], [r_tmpf])
                v_tt("pool", dst, dst, tmpf, ALU.add, [r_dst, r_tmpf], [r_dst])
            else:
                v_tt("dve", dst, po[:], gbc, ALU.mult, [rpo, r_gbc], [r_dst])

        for kh in range(2):
            ncb = 2 * si + 2
            for g in range(4):
                hq = kh * 4 + g
                po, rpo = psum[0]
                psm, rpsm = psum[1]
                pi0, rpi0 = psum[2]
                pi1, rpi1 = psum[3]
                for jb in range(ncb):
                    ps, rps = psum[4 + jb % 3]
                    mm(ps[:], kcT[:, kh, jb * 128:(jb + 1) * 128], qs[:, hq, :], True, True, [r_kcT, r_qs], [rps])
                    e_, r_e = e_tiles[jb % 3]
                    act(e_, ps[:], AF.Exp, [rps], [r_e], scale=NSA_SCALE)
                    if jb >= 2 * si - 1:
                        col = 36 + 3 * si + (jb - (2 * si - 1))
                        v_stt("dve", e_, iota_16[:], thr[:, col:col + 1], e_, ALU.is_le, ALU.mult,
                              [r_iota16, r_thr, r_e], [r_e])
                    first, last = jb == 0, jb == ncb - 1
                    mm(po[:], vcc[:, kh, jb, :], e_, first, last, [r_vcc, r_e], [rpo])
                    mm(psm[:], ones_b[:], e_, first, last, [r_ones, r_e], [rpsm])
                    mm(pi0[:], ovb[:, jb * 256:jb * 256 + 128], e_, first, last, [r_ov, r_e], [rpi0])
                    mm(pi1[:], ovb[:, jb * 256 + 128:jb * 256 + 256], e_, first, last, [r_ov, r_e], [rpi1])
                v_ts("dve", rsum, psm[:], 1e-30, None, ALU.max, None, [rpsm], [r_rsum])
                cx.op("dve", lambda e: e.reciprocal(out=rsum, in_=rsum), [r_rsum], [r_rsum])
                for jc, (pi, rpi) in enumerate(((pi0, rpi0), (pi1, rpi1))):
                    if g == 0:
                        v_tt("dve", impT[:, jc, :], pi[:], rsum, ALU.mult, [rpi, r_rsum], [r_impT])
                    else:
                        v_tt("dve", tmpf, pi[:], rsum, ALU.mult, [rpi, r_rsum], [r_tmpf])
                        v_tt("pool", impT[:, jc, :], impT[:, jc, :], tmpf, ALU.add, [r_impT, r_tmpf], [r_impT])
                psg, rpsg = gate_bcast(0 * 8 + hq)
                v_tt("dve", gbc, psg[:], rsum, ALU.mult, [rpsg, r_rsum], [r_gbc])
                v_tt("dve", ocn[:, g, :], po[:], gbc, ALU.mult, [rpo, r_gbc], [r_ocn])
            for qb in range(4):
                ps, rps = psum[4 + qb % 3]
                for jc in range(2):
                    tr(ps[:, jc * 128:(jc + 1) * 128], impT[:, jc, qb * 128:(qb + 1) * 128], ident_f[:],
                       [r_impT, r_identf], [rps])
                v_tt("dve", score, ps[:, 0:256], sbias[:, qb, :], ALU.add, [rps, r_sbias], [r_score])
                cx.op("dve", lambda e: e.max(out=m8[:, 0:8], in_=score), [r_score], [r_m8])
                cx.op("dve", lambda e: e.match_replace(out=work, in_to_replace=m8[:, 0:8], in_values=score,
                                                       imm_value=-3.0e38), [r_score, r_m8], [r_work])
                cx.op("dve", lambda e: e.max(out=m8[:, 8:16], in_=work), [r_work], [r_m8])
                v_ts("dve", selq2, score, m8[:, 15:16], None, ALU.is_ge, None, [r_score, r_m8], [r_selq2])
                v_stt("dve", selq, score, -1.0e29, selq2, ALU.is_gt, ALU.mult, [r_score, r_selq2], [r_selq])
                pst, rpst = psum[7]
                pstb = pst[:].bitcast(BF16)
                for jc in range(2):
                    tr(pstb[:, jc * 128:(jc + 1) * 128], selq[:, jc * 128:(jc + 1) * 128], ident_b[:],
                       [r_selq, r_identb], [rpst])
                for jc in range(2):
                    v_copy("dve", selT[:, jc, qb * 128:(qb + 1) * 128], pstb[:, jc * 128:(jc + 1) * 128],
                           [rpst], [r_selT])
            for g in range(4):
                hq = kh * 4 + g
                po, rpo = psum[0]
                psm, rpsm = psum[1]
                nkb = nkt_all * 4
                state = {}

                def K_fn(kb, kh=kh, state=state):
                    kt_i, kl = divmod(kb, 4)
                    if kl == 0:
                        kt, rkt = kt_t[kt_i % 3]
                        vt, rvt = vt_t[kt_i % 3]
                        k0 = kt_i * TT
                        dma("sp", kt, KS[kh, :, k0:k0 + TT], [dres("KS")], [rkt])
                        dma("sp", vt, VSW[k0:k0 + TT, kh * 128:(kh + 1) * 128].rearrange("(kb p) n -> p kb n", p=128),
                            [dres("VSW")], [rvt])
                        state["cur"] = (kt, rkt, vt, rvt)
                    kt, rkt, vt, rvt = state["cur"]
                    return kt[:, kl * 128:(kl + 1) * 128], rkt, None, None

                def V_fn(kb, state=state):
                    kt, rkt, vt, rvt = state["cur"]
                    return vt[:, kb % 4, :], rvt

                def M_fn(kb, e_, r_e, si=si):
                    pm, rpm = psum[6 + kb % 2]
                    mm(pm[:], eexp[:, (kb % 64) * 128:(kb % 64 + 1) * 128], selT[:, kb // 64, :], True, True,
                       [r_eexp, r_selT], [rpm])
                    if kb >= 32 * si:
                        causal_mask(kb - 32 * si, e_, r_e)
                    v_tt("dve", e_, e_, pm[:], ALU.mult, [r_e, rpm], [r_e])

                flash_branch(A, K_fn, nkb, qs[:, hq, :], r_qs, None, V_fn, M_fn, NSA_SCALE, po, rpo, psm, rpsm,
                             sbanks=(2, 3, 4, 5))
                v_copy("pool", onsa, ocn[:, g, :], [r_ocn], [r_onsa])
                finish_branch(po, rpo, psm, rpsm, 1 * 8 + hq, onsa, r_onsa, True)
                po, rpo = psum[0]
                psm, rpsm = psum[1]
                kb_lo = 32 * si - 4 if si > 0 else 0
                nkb = 32 * si + 32 - kb_lo
                state = {}

                def K_fn(kb, kh=kh, state=state, kb_lo=kb_lo):
                    kt_i, kl = divmod(kb, 4)
                    if kl == 0:
                        kt, rkt = kt_t[kt_i % 3]
                        vt, rvt = vt_t[kt_i % 3]
                        k0 = kb_lo * 128 + kt_i * TT
                        dma("sp", kt, KW[kh, :, k0:k0 + TT], [dres("KW")], [rkt])
                        dma("sp", vt, VSW[k0:k0 + TT, 256 + kh * 128:256 + (kh + 1) * 128]
                            .rearrange("(kb p) n -> p kb n", p=128), [dres("VSW")], [rvt])
                        state["cur"] = (kt, rkt, vt, rvt)
                    kt, rkt, vt, rvt = state["cur"]
                    return kt[:, kl * 128:(kl + 1) * 128], rkt, None, None

                def V_fn(kb, state=state):
                    kt, rkt, vt, rvt = state["cur"]
                    return vt[:, kb % 4, :], rvt

                def M_fn(kb, e_, r_e, si=si, kb_lo=kb_lo):
                    kbz = kb + kb_lo - 32 * si
                    if kbz >= 0:
                        causal_mask(kbz, e_, r_e)
                    wcol = 48 + kbz + 4
                    v_stt("dve", e_, iota_pf[:], thr[:, wcol:wcol + 1], e_, ALU.is_gt, ALU.mult,
                          [r_iota, r_thr, r_e], [r_e])

                flash_branch(A, K_fn, nkb, qs[:, hq, :], r_qs, None, V_fn, M_fn, NSA_SCALE, po, rpo, psm, rpsm,
                             sbanks=(2, 3, 4, 5, 6))
                finish_branch(po, rpo, psm, rpsm, 2 * 8 + hq, onsa, r_onsa, True)
                v_copy("dve", mixT[:, 8 + hq, :], onsa, [r_onsa], [r_mixT])
        cx.barrier()

        if si == 0:
            dbg("mix_all", mixT, [r_mixT])
            dbg("kcT", kcT[:], [r_kcT]); dbg("vcc", vcc[:], [r_vcc])
            dbg("KN0", KN[:, :, 0:512].rearrange("h p t -> p h t"), [dres("KN")])
            dbg("KR0", KR[:, 0:512], [dres("KR")])
            dbg("KS0", KS[:, :, 0:512].rearrange("h p t -> p h t"), [dres("KS")])
            dbg("KW0", KW[:, :, 0:512].rearrange("h p t -> p h t"), [dres("KW")])
            dbg("KC0", KC[:, :, 0:512].rearrange("h p t -> p h t"), [dres("KC")])
            dbg("VC0", VC[:, :, 0:512].rearrange("h p t -> p h t"), [dres("VC")])
            dbg("VM0", VM[0:512, :].rearrange("(tb p) n -> p tb n", p=128), [dres("VM")])
            dbg("VSW0", VSW[0:512, :].rearrange("(tb p) n -> p tb n", p=128), [dres("VSW")])
        if STOP == 6:
            return nc, es, cx
        A.off = mark_q
        for tb in range(4):
            act(y_t[:, tb, :], y_t[:, tb, :], AF.Copy, [r_y], [r_y], scale=ALPHA)
        wo_t = [A.alloc([16, 512], BF16) for _ in range(2)]
        for ng in range(4):
            wo, rwo = wo_t[ng % 2]
            dma("sp", wo, wb["w_out"][:, ng * 512:(ng + 1) * 512].rearrange("(kc p) n -> p kc n", p=128),
                [dres("b_w_out")], [rwo])
            for tb in range(4):
                ps, rps = psum[tb + 4 * (ng % 2)]
                for kc in range(16):
                    mm(ps[:], mixT[:, kc, tb * 128:(tb + 1) * 128], wo[:, kc, :], kc == 0, kc == 15, [r_mixT, rwo], [rps])
                v_tt("dve", y_t[:, tb, ng * 512:(ng + 1) * 512], ps[:], y_t[:, tb, ng * 512:(ng + 1) * 512], ALU.add,
                     [rps, r_y], [r_y])
        layer_norm(A, y_t, r_y, ln_in["ln2_g"], ln_in["ln2_b"])
        if si == 0:
            dbg("x2", y_t, [r_y])
        cx.barrier()
        A.off = 4 * D * 2
        x2T, r_x2T = A.alloc([16, TT], BF16)
        transpose_to_bf16(y_t, r_y, x2T, r_x2T)
        ffn_ln(A, x2T, None, "ffn2", ln_in["ln3_g"], ln_in["ln3_b"], y_t, r_y, xT_res=r_x2T)
        dma("sp", out[m0:m0 + TT, :].rearrange("(tb p) d -> p tb d", p=128), y_t, [r_y], [dres("out")])
        cx.barrier()

    cx.barrier()
    return nc, es, cx


def emit_program():
    nc, es, cx = build_program()
    with nc.Block() as block:
        @block.tensor
        def _(e):
            cx.replay("pe", e)

        @block.scalar
        def _(e):
            cx.replay("act", e)

        @block.vector
        def _(e):
            cx.replay("dve", e)

        @block.gpsimd
        def _(e):
            cx.replay("pool", e)

        @block.sync
        def _(e):
            cx.replay("sp", e)
    return nc


def _rope_tables(pos, d):
    half = d // 2
    inv = (np.float32(10000.0) ** (-np.arange(half, dtype=np.float32) * np.float32(2.0 / d))).astype(np.float32)
    ang = pos.astype(np.float32)[None, :] * inv[:, None]
    cos, sin = np.cos(ang).astype(np.float32), np.sin(ang).astype(np.float32)
    t = np.empty((d, 2, pos.shape[0]), np.float32)
    t[:half, 0], t[half:, 0] = cos, cos
    t[:half, 1], t[half:, 1] = -sin, sin
    return t


def _swap_halves(w):
    h = w.shape[1] // 2
    return np.concatenate([w[:, h:], w[:, :h]], axis=1)


def kernel(**inp):
    f = lambda k: np.ascontiguousarray(np.asarray(inp[k], dtype=np.float32)[0])
    x = f("x")
    w_in = f("w_in")
    c_q, c_kv, k_rope = w_in[:, 0:512], w_in[:, 512:768], w_in[:, 768:832]
    q_nsa = w_in[:, 832:1856]
    k_cmp, v_cmp = w_in[:, 1856:2112], w_in[:, 2112:2368]
    k_sel, v_sel = w_in[:, 2368:2624], w_in[:, 2624:2880]
    k_win, v_win = w_in[:, 2880:3136], w_in[:, 3136:3392]
    gl = w_in[:, 3392:3416]
    cols = [c_kv, k_rope, _swap_halves(k_rope)]
    for kk in (k_cmp, k_sel, k_win):
        for kh in range(2):
            blk = kk[:, kh * 128:(kh + 1) * 128]
            cols += [blk, _swap_halves(blk)]
    cols.append(v_cmp)
    w_kvf = np.ascontiguousarray(np.concatenate(cols, axis=1))
    assert w_kvf.shape[1] == NKV_F
    w_kvt = np.ascontiguousarray(np.concatenate([v_sel, v_win], axis=1))
    qcols = [c_q]
    for h in range(8):
        blk = q_nsa[:, h * 128:(h + 1) * 128]
        qcols += [blk, _swap_halves(blk)]
    qcols.append(gl)
    w_q = np.ascontiguousarray(np.concatenate(qcols, axis=1))
    wuq = f("mla_w_uq")
    ucols = []
    for h in range(8):
        blk = wuq[:, h * 192:(h + 1) * 192]
        ucols += [blk[:, :128], blk[:, 128:], _swap_halves(blk[:, 128:])]
    w_uq = np.ascontiguousarray(np.concatenate(ucols, axis=1))
    shared = {
        "x_all": x, "xT_all": np.ascontiguousarray(x.T),
        "ffn1_w_gate": f("ffn1_w_gate"), "ffn1_w_up": f("ffn1_w_up"), "ffn1_w_down": f("ffn1_w_down"),
        "ffn2_w_gate": f("ffn2_w_gate"), "ffn2_w_up": f("ffn2_w_up"), "ffn2_w_down": f("ffn2_w_down"),
        "w_kvf": w_kvf, "w_kvt": w_kvt, "w_q": w_q, "w_uq": w_uq, "w_ukv": f("mla_w_ukv"), "w_out": f("w_out"),
        "w1_k": f("nsa_cmp_w1_k"), "w1_v": f("nsa_cmp_w1_v"), "w2_k": f("nsa_cmp_w2_k"), "w2_v": f("nsa_cmp_w2_v"),
        "qg": np.ascontiguousarray(f("mla_q_norm_g").reshape(4, 128).T),
        "kvg": np.ascontiguousarray(f("mla_kv_norm_g").reshape(2, 128).T),
        "gateb": np.ascontiguousarray(f("nsa_gate_b").reshape(24, 1)),
        "peT_k": np.ascontiguousarray(f("nsa_cmp_pe_k").T), "peT_v": np.ascontiguousarray(f("nsa_cmp_pe_v").T),
        "ident": np.eye(128, dtype=np.float32),
        "ln1_gbT": np.ascontiguousarray(np.concatenate([f("ln1_g").reshape(16, 128).T, f("ln1_b").reshape(16, 128).T],
                                                       axis=1)),
    }
    for k in ("ln1_g", "ln1_b", "ln2_g", "ln2_b", "ln3_g", "ln3_b"):
        shared[k] = np.ascontiguousarray(np.asarray(inp[k], np.float32).reshape(1, D))
    pos_all = np.arange(S)
    shared["cs64_all"] = _rope_tables(pos_all, 64)
    shared["cs128_all"] = _rope_tables(pos_all, 128)
    ee = np.zeros((128, 64, 128), np.float32)
    for kbl in range(64):
        ee[2 * kbl, kbl, :64] = 1.0
        ee[2 * kbl + 1, kbl, 64:] = 1.0
    shared["eexp"] = ee.reshape(128, 64 * 128)
    ci = np.arange(1024)[:, None] * 16
    sj = np.arange(256)[None, :] * 64
    ovm = np.clip(np.minimum(ci + 32, sj + 64) - np.maximum(ci, sj), 0, None).astype(np.float32) / 16.0
    ovm[1023, :] = 0.0
    shared["ov"] = np.ascontiguousarray(ovm.reshape(8, 128, 256).transpose(1, 0, 2).reshape(128, 8 * 256))

    in_maps = []
    tok_idx = []
    for c in range(NCORE):
        idx = np.concatenate([np.arange((8 * i + c) * TT, (8 * i + c + 1) * TT) for i in range(NSLOT)])
        tok_idx.append(idx)
        m = dict(shared)
        m["x_my"] = np.ascontiguousarray(x[idx])
        m["xT_my"] = np.ascontiguousarray(x[idx].T)
        m["cs64_my"] = _rope_tables(idx, 64)
        m["cs128_my"] = _rope_tables(idx, 128)
        tq = idx[:, None]
        bj = np.arange(256)[None, :]
        cur = tq // 64
        valid = bj * 64 <= tq
        forced = (bj == 0) | (bj == cur) | (bj == cur - 1)
        sb = np.where(valid, np.where(forced, np.float32(1e4), np.float32(0.0)), np.float32(-1e30)).astype(np.float32)
        m["sbias"] = np.ascontiguousarray(sb.reshape(16, 128, 256).transpose(1, 0, 2))
        th = np.zeros((96,), np.float32)
        for kbz in range(-4, 32):
            th[kbz + 4] = 512 * c - 128 * kbz
            th[48 + kbz + 4] = 512 * c - 128 * kbz - 512
        for i in range(NSLOT):
            for r in range(3):
                jb = 2 * i - 1 + r
                th[36 + 3 * i + r] = 4096 * i + 512 * c - 31 - 2048 * jb
        m["thr"] = np.ascontiguousarray(np.broadcast_to(th[None, :], (128, 96)))
        in_maps.append(m)

    nc = emit_program()
    if os.environ.get('KTRACE'):
        res = run_bass_kernel_spmd(nc, in_maps, core_ids=list(range(NCORE)), trace=True)
    else:
        res = run_bass_kernel_spmd(nc, in_maps, core_ids=list(range(NCORE)))
    LAST["res"] = res
    out = np.empty((1, S, D), np.float32)
    for c in range(NCORE):
        out[0, tok_idx[c]] = res.results[c]["out"]
    return out
```
